# Optimizing a Trainium2 kernel written in Bass

```python
import jax, jax.numpy as jnp
from jax import lax
import numpy as np

D_MODEL = 1024
BATCH = 16
SEQ = 2048
DEPTH = 1
DEC_BATCH = 32
DEC_SEQ = 32
PAST_LEN = 2048

CHUNK = 64
D_LRU = D_MODEL
N_LRU_HEADS = 16
LRU_BLOCK = D_LRU // N_LRU_HEADS
LRU_C = 8.0
CONV_W = 4
POOL_WINDOWS = (2, 4, 8, 16)
N_POOL_GROUPS = len(POOL_WINDOWS)
D_POOL = D_MODEL // 2
POOL_GROUP = D_POOL // N_POOL_GROUPS
POOL_OUT_GROUP = D_MODEL // N_POOL_GROUPS
POOL_STATE = max(POOL_WINDOWS) - 1
D_PLE = 256
EPS = 1e-6
SPLITS = (D_LRU, 2 * D_LRU, 2 * D_LRU + D_POOL, 2 * D_LRU + D_POOL + D_MODEL,
          2 * D_LRU + D_POOL + 2 * D_MODEL)
IN_COLS = 2 * D_LRU + D_POOL + 3 * D_MODEL

kernel_name = "hybrid_rglru_pool_streaming_step"


def rmsnorm(x, g):
    xf = x.astype(jnp.float32)
    y = xf * lax.rsqrt(jnp.mean(xf * xf, axis=-1, keepdims=True) + EPS)
    return (y * g.astype(jnp.float32)).astype(x.dtype)


def causal_conv(x, state, w, b):
    T = x.shape[1]
    z = jnp.concatenate([state.astype(x.dtype), x], axis=1)
    y = sum(z[:, k:k + T] * w[k] for k in range(CONV_W)) + b
    return y, z[:, -(CONV_W - 1):]


def rg_lru(x, h0, pos0, w_a, b_a, w_i, b_i, lam):
    B, T, _ = x.shape
    xb = x.reshape(B, T, N_LRU_HEADS, LRU_BLOCK)
    r = jax.nn.sigmoid(jnp.einsum('bthi,hij->bthj', xb, w_a).reshape(B, T, D_LRU) + b_a)
    i = jax.nn.sigmoid(jnp.einsum('bthi,hij->bthj', xb, w_i).reshape(B, T, D_LRU) + b_i)
    log_a = -LRU_C * r.astype(jnp.float32) * jax.nn.softplus(-lam.astype(jnp.float32))
    a = jnp.exp(log_a)
    mult = jnp.sqrt(-jnp.expm1(2.0 * log_a))
    pos = (pos0 + jnp.arange(T))[None, :, None]
    mult = jnp.where(pos == 0, 1.0, mult)
    bterm = mult * (i.astype(jnp.float32) * x.astype(jnp.float32))
    bterm = bterm.at[:, 0].add(a[:, 0] * h0.astype(jnp.float32))

    def combine(left, right):
        a1, b1 = left
        a2, b2 = right
        return a1 * a2, a2 * b1 + b2

    _, h = lax.associative_scan(combine, (a, bterm), axis=1)
    return h.astype(x.dtype), h[:, -1].astype(h0.dtype)


def multi_pool(x, state, pos0):
    B, T, _ = x.shape
    z_raw = jnp.concatenate([state.astype(x.dtype), x], axis=1)
    z = z_raw.astype(jnp.float32)
    c = jnp.concatenate([jnp.zeros((B, 1, D_POOL), jnp.float32), jnp.cumsum(z, axis=1)], axis=1)
    pos = pos0 + jnp.arange(T)
    outs = []
    for g, w in enumerate(POOL_WINDOWS):
        sl = slice(g * POOL_GROUP, (g + 1) * POOL_GROUP)
        end = c[:, POOL_STATE + 1:POOL_STATE + 1 + T, sl]
        start = c[:, POOL_STATE + 1 - w:POOL_STATE + 1 - w + T, sl]
        cnt = jnp.minimum(w, pos + 1).astype(jnp.float32)[None, :, None]
        outs.append((end - start) / cnt - z[:, POOL_STATE:, sl])
    pooled = jnp.concatenate(outs, axis=-1).astype(x.dtype)
    return pooled, z_raw[:, -POOL_STATE:].astype(state.dtype)


def mixer_layer(x, p, conv_s, lru_s, pool_s, pos0, norm_mix, w_in, conv_w, conv_b,
                w_rg_a, b_rg_a, w_rg_i, b_rg_i, lru_lambda, w_pool, b_pool, pool_scale,
                w_out, norm_ple, w_ple_gate, w_ple):
    B, T, _ = x.shape
    u = rmsnorm(x, norm_mix)
    proj = u @ w_in
    xl, gl, xp, gp, ml, mp = jnp.split(proj, SPLITS, axis=-1)
    xc, new_conv = causal_conv(xl, conv_s, conv_w, conv_b)
    h, new_lru = rg_lru(xc, lru_s, pos0, w_rg_a, b_rg_a, w_rg_i, b_rg_i, lru_lambda)
    lru_out = h * jax.nn.silu(gl)
    pooled, new_pool = multi_pool(xp, pool_s, pos0)
    pg = jnp.einsum('btgi,gij->btgj', pooled.reshape(B, T, N_POOL_GROUPS, POOL_GROUP), w_pool) + b_pool
    pool_out = pg.reshape(B, T, D_MODEL) * pool_scale * jax.nn.silu(gp)
    merged = jax.nn.sigmoid(ml) * lru_out + jax.nn.sigmoid(mp) * pool_out
    x = x + merged @ w_out
    gate = jax.nn.sigmoid(rmsnorm(x, norm_ple) @ w_ple_gate)
    x = x + gate * (p @ w_ple)
    return x, new_conv, new_lru, new_pool


def setup_inputs(seed: int = 0) -> dict:
    key = jax.random.key(seed)
    ks = jax.random.split(key, 32)
    f32 = jnp.float32
    nrm = lambda k, shape, s: jax.random.normal(k, shape, f32) * s
    a0 = jax.random.uniform(ks[13], (DEPTH, D_LRU), f32, 0.9, 0.999)
    s = a0 ** (1.0 / LRU_C)
    lru_lambda = jnp.log(s) - jnp.log1p(-s)
    return {
        "x_prompt": nrm(ks[0], (BATCH, SEQ, D_MODEL), 1.0),
        "x_sample": nrm(ks[1], (DEC_BATCH, DEC_SEQ, D_MODEL), 1.0),
        "p_prompt": nrm(ks[2], (DEPTH, BATCH, SEQ, D_PLE), 1.0),
        "p_sample": nrm(ks[3], (DEPTH, DEC_BATCH, DEC_SEQ, D_PLE), 1.0),
        "state_conv": nrm(ks[4], (DEPTH, DEC_BATCH, CONV_W - 1, D_LRU), 1.0),
        "state_lru": nrm(ks[5], (DEPTH, DEC_BATCH, D_LRU), 0.5),
        "state_pool": nrm(ks[6], (DEPTH, DEC_BATCH, POOL_STATE, D_POOL), 1.0),
        "norm_mix": 1.0 + nrm(ks[7], (DEPTH, D_MODEL), 0.05),
        "w_in": nrm(ks[8], (DEPTH, D_MODEL, IN_COLS), D_MODEL ** -0.5),
        "conv_w": nrm(ks[9], (DEPTH, CONV_W, D_LRU), CONV_W ** -0.5),
        "conv_b": nrm(ks[10], (DEPTH, D_LRU), 0.02),
        "w_rg_a": nrm(ks[11], (DEPTH, N_LRU_HEADS, LRU_BLOCK, LRU_BLOCK), LRU_BLOCK ** -0.5),
        "b_rg_a": nrm(ks[12], (DEPTH, D_LRU), 0.02),
        "w_rg_i": nrm(ks[14], (DEPTH, N_LRU_HEADS, LRU_BLOCK, LRU_BLOCK), LRU_BLOCK ** -0.5),
        "b_rg_i": nrm(ks[15], (DEPTH, D_LRU), 0.02),
        "lru_lambda": lru_lambda,
        "w_pool": nrm(ks[16], (DEPTH, N_POOL_GROUPS, POOL_GROUP, POOL_OUT_GROUP), POOL_GROUP ** -0.5),
        "b_pool": nrm(ks[17], (DEPTH, N_POOL_GROUPS, POOL_OUT_GROUP), 0.02),
        "pool_scale": 1.0 + nrm(ks[18], (DEPTH, D_MODEL), 0.1),
        "w_out": nrm(ks[19], (DEPTH, D_MODEL, D_MODEL), D_MODEL ** -0.5),
        "norm_ple": 1.0 + nrm(ks[20], (DEPTH, D_MODEL), 0.05),
        "w_ple_gate": nrm(ks[21], (DEPTH, D_MODEL, D_MODEL), D_MODEL ** -0.5),
        "w_ple": nrm(ks[22], (DEPTH, D_PLE, D_MODEL), D_PLE ** -0.5),
        "final_norm": 1.0 + nrm(ks[23], (D_MODEL,), 0.05),
    }


def reference(x_prompt, x_sample, p_prompt, p_sample, state_conv, state_lru, state_pool,
              norm_mix, w_in, conv_w, conv_b, w_rg_a, b_rg_a, w_rg_i, b_rg_i, lru_lambda,
              w_pool, b_pool, pool_scale, w_out, norm_ple, w_ple_gate, w_ple, final_norm):
    dt = x_prompt.dtype
    bp = x_prompt.shape[0]
    hp, hs = x_prompt, x_sample
    conv_p_l, lru_p_l, pool_p_l = [], [], []
    conv_s_l, lru_s_l, pool_s_l = [], [], []
    for i in range(DEPTH):
        wts = (norm_mix[i], w_in[i], conv_w[i], conv_b[i], w_rg_a[i], b_rg_a[i], w_rg_i[i],
               b_rg_i[i], lru_lambda[i], w_pool[i], b_pool[i], pool_scale[i], w_out[i],
               norm_ple[i], w_ple_gate[i], w_ple[i])
        hp, cp, lp, pp = mixer_layer(
            hp, p_prompt[i],
            jnp.zeros((bp, CONV_W - 1, D_LRU), dt),
            jnp.zeros((bp, D_LRU), state_lru.dtype),
            jnp.zeros((bp, POOL_STATE, D_POOL), dt),
            0, *wts)
        hs, cs, ls, ps = mixer_layer(
            hs, p_sample[i], state_conv[i], state_lru[i], state_pool[i],
            PAST_LEN, *wts)
        conv_p_l.append(cp); lru_p_l.append(lp); pool_p_l.append(pp)
        conv_s_l.append(cs); lru_s_l.append(ls); pool_s_l.append(ps)
    y_prompt = rmsnorm(hp, final_norm)
    y_sample = rmsnorm(hs, final_norm)
    return (y_prompt, y_sample,
            jnp.stack(conv_p_l, 0), jnp.stack(lru_p_l, 0), jnp.stack(pool_p_l, 0),
            jnp.stack(conv_s_l, 0), jnp.stack(lru_s_l, 0), jnp.stack(pool_s_l, 0))
```

```python
import numpy as np
from contextlib import ExitStack

import concourse.bass as bass
import concourse.mybir as mybir
from concourse.bass_utils import run_bass_kernel_spmd

F32 = mybir.dt.float32
BF16 = mybir.dt.bfloat16
AF = mybir.ActivationFunctionType
ALU = mybir.AluOpType

NCORES = 8
EPS = 1e-6
D = 1024
IN_COLS = 5632
NM = IN_COLS // 128


class Buf:
    __slots__ = ("name", "last_w", "readers", "sem", "cnt")

    def __init__(self, name):
        self.name = name
        self.last_w = None
        self.readers = []
        self.sem = None
        self.cnt = 0


class Op:
    __slots__ = ("eng", "fn", "deps", "needed", "count", "sem", "waits", "knows", "is_dma", "lane", "dbg", "wdbg")

    def __init__(self, eng, fn, is_dma=False, lane=None):
        self.eng = eng
        self.fn = fn
        self.deps = []
        self.needed = is_dma
        self.count = 0
        self.sem = None
        self.waits = []
        self.knows = None
        self.is_dma = is_dma
        self.lane = lane


class Prog:
    ENGS = ("pe", "act", "dve", "pool", "sp")

    def __init__(self, nc, es):
        self.nc = nc
        self.es = es
        self.ops = []
        self.lanes = []

    def buf(self, name):
        return Buf(name)

    def _add(self, o, reads, writes):
        o.dbg = ([b.name for b in reads], [b.name for b in writes])
        o.wdbg = []
        deps = {}
        for b in reads:
            if b.last_w is not None:
                deps[id(b.last_w)] = (b.last_w, "raw")
        for b in writes:
            if b.last_w is not None and id(b.last_w) not in deps:
                deps[id(b.last_w)] = (b.last_w, "waw")
            for r in b.readers:
                if id(r) not in deps:
                    deps[id(r)] = (r, "war")
        for p, kind in deps.values():
            if p is o:
                continue
            if (not p.is_dma) and (not o.is_dma) and p.eng == o.eng:
                if o.eng == "pe":
                    continue
            if p.is_dma and o.is_dma and p.lane is o.lane and kind == "waw":
                continue
            o.deps.append(p)
            p.needed = True
        for b in reads:
            if b in writes:
                continue
            if o.is_dma:
                b.readers = [r for r in b.readers if not (r.is_dma and r.lane is o.lane)]
            else:
                b.readers = [r for r in b.readers if r.is_dma or r.eng != o.eng]
            b.readers.append(o)
        for b in writes:
            b.last_w = o
            b.readers = []
        self.ops.append(o)
        return o

    def op(self, eng, fn, reads=(), writes=()):
        return self._add(Op(eng, fn), list(reads), list(writes))

    def dma(self, fn, lane, reads=(), writes=(), queue="sp"):
        if not any(l is lane for l in self.lanes):
            self.lanes.append(lane)
        o = Op(queue, fn, is_dma=True, lane=lane)
        return self._add(o, list(reads), list(writes))

    def finalize(self):
        nc, es = self.nc, self.es
        engsem = {e: es.enter_context(nc.semaphore("sem_" + e)) for e in ("pe", "act", "dve", "pool")}
        for i, ln in enumerate(self.lanes):
            ln.sem = es.enter_context(nc.semaphore("lane%d" % i))
        engcnt = {e: 0 for e in engsem}
        known = {e: {} for e in self.ENGS}
        per = {e: [] for e in self.ENGS}
        nwaits = 0
        for o in self.ops:
            K = known[o.eng]
            waits = {}
            for p in o.deps:
                sid = id(p.sem)
                if K.get(sid, 0) >= p.count:
                    continue
                if sid not in waits or waits[sid][1] < p.count:
                    waits[sid] = (p.sem, p.count, p)
            for sid, (sem, val, p) in waits.items():
                if p.knows:
                    for k2, v2 in p.knows.items():
                        if K.get(k2, 0) < v2:
                            K[k2] = v2
                if K.get(sid, 0) < val:
                    K[sid] = val
            o.waits = [(sem, val) for sid, (sem, val, p) in waits.items()]
            o.wdbg = [(p.eng, p.dbg) for sid, (sem, val, p) in waits.items()]
            nwaits += len(o.waits)
            if o.is_dma:
                o.lane.cnt += 16
                o.count = o.lane.cnt
                o.sem = o.lane.sem
                o.knows = dict(K)
            elif o.needed:
                engcnt[o.eng] += 1
                o.count = engcnt[o.eng]
                o.sem = engsem[o.eng]
                o.knows = dict(K)
            per[o.eng].append(o)
        self.per = per
        self.stats = {e: len(per[e]) for e in per}
        self.stats["waits"] = nwaits

        def emit(eng_handle, ops, final=False):
            for o in ops:
                for sem, val in o.waits:
                    eng_handle.wait_ge(sem, val)
                ins = o.fn(eng_handle)
                if o.is_dma:
                    ins.then_inc(o.sem, 16)
                elif o.needed:
                    ins.then_inc(o.sem, 1)
            if final:
                for ln in self.lanes:
                    if ln.cnt:
                        eng_handle.wait_ge(ln.sem, ln.cnt)
                for e, s in engsem.items():
                    if engcnt[e]:
                        eng_handle.wait_ge(s, engcnt[e])

        block = es.enter_context(nc.Block())

        @block.sync
        def _(e):
            emit(e, per["sp"], final=True)

        @block.tensor
        def _(e):
            emit(e, per["pe"])

        @block.scalar
        def _(e):
            emit(e, per["act"])

        @block.vector
        def _(e):
            emit(e, per["dve"])

        @block.gpsimd
        def _(e):
            emit(e, per["pool"])


class T:
    def __init__(self, ap, buf):
        self.ap = ap
        self.buf = buf


G1, G2, CW, CB, BA, BI, LAM, BP, PSC = 0, 8, 16, 48, 56, 64, 72, 80, 88
HBA, HBI, CL, HCL, BPS = 0, 8, 16, 24, 32

B_TASKS = [
    ("L1", 0), ("L1", 1), ("L2a", 0), ("Q1", 0), ("L1", 2), ("L2a", 1), ("L2b", 0), ("L2b", 1), ("Q1", 1),
    ("L1", 3), ("L2a", 2), ("Q2", 0), ("L1", 4), ("L2a", 3), ("L2b", 2), ("L2b", 3), ("Q1", 2),
    ("L1", 5), ("L2a", 4), ("Q2", 1), ("L1", 6), ("L2a", 5), ("L2b", 4), ("L2b", 5), ("Q1", 3),
    ("L1", 7), ("L2a", 6), ("Q2", 2), ("L2a", 7), ("L2b", 6), ("L2b", 7), ("Q2", 3),
]
W_ORDER = []
for _t, _a in B_TASKS:
    if _t == "L1":
        W_ORDER += [_a, 8 + _a, 28 + _a]
    elif _t == "Q1":
        W_ORDER += [16 + _a]
    elif _t == "Q2":
        W_ORDER += [20 + 2 * _a, 36 + 2 * _a, 21 + 2 * _a, 37 + 2 * _a]
assert sorted(W_ORDER) == list(range(NM))
PAIR_ORDER = []
for _m in W_ORDER:
    if _m // 2 not in PAIR_ORDER:
        PAIR_ORDER.append(_m // 2)


def build_program(stop=None):
    nc = bass.Bass("TRN2", target_bir_lowering=False)
    es = ExitStack()
    P = Prog(nc, es)

    def din(name, shape, dt=F32):
        return nc.dram_tensor(name, shape, dt, kind="ExternalInput").ap()

    def dout(name, shape):
        return nc.dram_tensor(name, shape, F32, kind="ExternalOutput").ap()

    d_xp = din("xp", [4096, 1024])
    d_pp = din("pp", [4096, 256])
    d_xs = din("xs", [128, 1024])
    d_ps = din("ps", [128, 256])
    d_sconv = din("sconv", [12, 1024])
    d_slru = din("slru", [4, 1024])
    d_spool = din("spool", [60, 512])
    d_norm_mix = din("norm_mix", [1, 1024])
    d_w_in = din("w_in", [1, 1024, IN_COLS])
    d_conv_w = din("conv_w", [1, 4, 1024])
    d_conv_b = din("conv_b", [1, 1024])
    d_w_rg_a = din("w_rg_a", [1, 16, 64, 64])
    d_b_rg_a = din("b_rg_a", [1, 1024])
    d_w_rg_i = din("w_rg_i", [1, 16, 64, 64])
    d_b_rg_i = din("b_rg_i", [1, 1024])
    d_lam = din("lru_lambda", [1, 1024])
    d_w_pool = din("w_pool", [1, 4, 128, 256])
    d_b_pool = din("b_pool", [1, 4, 256])
    d_pool_scale = din("pool_scale", [1, 1024])
    d_w_out = din("w_out", [1, 1024, 1024])
    d_norm_ple = din("norm_ple", [1, 1024])
    d_w_pg = din("w_ple_gate", [1, 1024, 1024])
    d_w_ple = din("w_ple", [1, 256, 1024])
    d_final_norm = din("final_norm", [1024])

    o_yp = dout("yp", [4096, 1024])
    o_ys = dout("ys", [128, 1024])
    o_ncp = dout("ncp", [2, 3, 1024])
    o_nlp = dout("nlp", [2, 1024])
    o_npp = dout("npp", [2, 15, 512])
    o_ncs = dout("ncs", [4, 3, 1024])
    o_nls = dout("nls", [4, 1024])
    o_nps = dout("nps", [4, 15, 512])

    d_wsc = nc.dram_tensor("win_bf16", [NM, 128, 1024], BF16, kind="Internal").ap()
    wsc_buf = [P.buf("wsc%d" % m) for m in range(NM)]

    def sb(name, shape, dt=F32):
        t = es.enter_context(nc.sbuf_tensor(name, shape, dt))
        return T(t[:], P.buf(name))

    WOUT = sb("wout", [128, 8 * 1024], BF16)
    WPG = sb("wpg", [128, 8 * 1024], BF16)
    WPE = sb("wpe", [128, 2 * 1024], BF16)
    WA = sb("wa", [128, 8 * 128], BF16)
    WI = sb("wi", [128, 8 * 128], BF16)
    WPOOL = sb("wpool", [128, 4 * 256], BF16)
    G3 = sb("g3bc", [128, 1024], F32)
    IDF = sb("identf", [128, 128], F32)
    IDB = sb("identb", [128, 128], BF16)
    CV = sb("cvec", [128, 96], F32)
    CV2 = sb("cvec2", [128, 40], F32)
    INVC = sb("invc", [128, 4 * 16], F32)
    NEGH = sb("negh", [128, 1], F32)
    SC16 = sb("sc16", [128, 8 * 16], F32)
    SP60 = sb("sp60", [128, 4 * 60], F32)
    CONVH = [sb("convh%d" % c, [128, 3], F32) for c in range(8)]
    HST = [sb("hst%d" % c, [128, 1], F32) for c in range(8)]
    POOLH = [sb("poolh%d" % g, [128, 15], F32) for g in range(4)]
    ST = sb("stg", [128, 4 * 92], F32)
    STO = sb("sto", [128, 4 * 128], F32)

    wout_v = WOUT.ap.rearrange("p (k n) -> p k n", k=8)
    wpg_v = WPG.ap.rearrange("p (k n) -> p k n", k=8)
    wpe_v = WPE.ap.rearrange("p (k n) -> p k n", k=2)
    wa_v = WA.ap.rearrange("p (k n) -> p k n", k=8)
    wi_v = WI.ap.rearrange("p (k n) -> p k n", k=8)
    wpool_v = WPOOL.ap.rearrange("p (g n) -> p g n", g=4)
    invc_v = INVC.ap.rearrange("p (g n) -> p g n", g=4)
    sc16_v = SC16.ap.rearrange("p (c n) -> p c n", c=8)
    sp60_v = SP60.ap.rearrange("p (g n) -> p g n", g=4)
    st_v = ST.ap.rearrange("p (s n) -> p s n", s=4)

    XS = [sb("xs%d" % i, [128, 1024], F32) for i in range(3)]
    XRT = sb("xrt", [128, 4 * 1024], F32)
    XR = [T(XRT.ap[:, i * 1024:(i + 1) * 1024], P.buf("xr%d" % i)) for i in range(3)]
    UB = [sb("ub%d" % i, [128, 1024], BF16) for i in range(2)]
    U2 = [sb("u2_%d" % i, [128, 1024], BF16) for i in range(2)]
    JUNK = sb("junk", [128, 1024], BF16)
    UT = [sb("ut%d" % i, [128, 8 * 512], BF16) for i in range(2)]
    MG = [sb("mg%d" % i, [128, 8 * 512], BF16) for i in range(2)]
    RING_N = 16
    RING = [sb("wr%d" % i, [128, 1024], BF16) for i in range(8)]
    EXTRA = sb("extra", [128, 8 * 1024], BF16)
    RING += [T(EXTRA.ap[:, i * 1024:(i + 1) * 1024], P.buf("wrx%d" % i)) for i in range(8)]
    NWORK = 14
    WORK = [sb("wk%d" % i, [128, 512], F32) for i in range(NWORK)]
    XLW = [sb("xlw%d" % i, [128, 515], F32) for i in range(2)]
    XPW = [sb("xpw%d" % i, [128, 527], F32) for i in range(2)]
    XLWH = [P.buf("xlwh%d" % i) for i in range(2)]
    XPWH = [P.buf("xpwh%d" % i) for i in range(2)]
    SA = sb("sa", [128, 527], F32)
    SBB = sb("sbb", [128, 527], F32)
    XCB = [sb("xcb%d" % i, [128, 512], BF16) for i in range(2)]
    PL = [sb("pl%d" % i, [128, 512], BF16) for i in range(2)]
    T16 = sb("t16", [128, 16], F32)
    STAT = [sb("stat%d" % i, [128, 4], F32) for i in range(8)]
    U2T = [sb("u2t%d" % i, [128, 8 * 128], BF16) for i in range(2)]
    P32 = [sb("p32_%d" % i, [128, 256], F32) for i in range(2)]
    PB = [sb("pb%d" % i, [128, 256], BF16) for i in range(2)]
    PT = [sb("pt%d" % i, [128, 256], BF16) for i in range(2)]
    TGC = [T(XRT.ap[:, 3072:4096], P.buf("tgc0"))]

    ps_t = es.enter_context(nc.psum_tensor("psum", [128, 8 * 512], F32))
    NFB = 6
    NTB = 8 - NFB
    FB = [T(ps_t[:, i * 512:(i + 1) * 512], P.buf("fb%d" % i)) for i in range(NFB)]
    TB = [T(ps_t[:, i * 512:(i + 1) * 512], P.buf("tb%d" % i)) for i in range(NFB, 8)]
    ctr = {"fb": 0, "tb": 0, "wk": 0, "stat": 0}

    fb_free = list(range(NFB))

    def newbank():
        assert fb_free, "out of PSUM banks"
        return FB[fb_free.pop(0)]

    def relbank(b):
        i = [j for j in range(NFB) if FB[j] is b][0]
        assert i not in fb_free
        fb_free.append(i)

    def newtbank():
        b = TB[ctr["tb"] % len(TB)]
        ctr["tb"] += 1
        return b

    wk_free = list(range(NWORK))

    def newwork():
        assert wk_free, "out of work tiles"
        return WORK[wk_free.pop(0)]

    def relwork(*ts):
        for t_ in ts:
            i = [j for j in range(NWORK) if WORK[j] is t_][0]
            assert i not in wk_free
            wk_free.append(i)

    def newstat():
        b = STAT[ctr["stat"] % len(STAT)]
        ctr["stat"] += 1
        return b

    rr = {"i": 0}

    def anyeng():
        e = "dve"
        rr["i"] += 1
        return e

    P.op("pool", lambda e: e.memset(NEGH.ap, -0.5), writes=[NEGH.buf])
    P.op("pool", lambda e: e.memset(IDF.ap, 1.0), writes=[IDF.buf])
    P.op("pool", lambda e: e.affine_select(out=IDF.ap, in_=IDF.ap, pattern=[[-1, 128]],
                                           compare_op=ALU.is_equal, fill=0.0, base=0,
                                           channel_multiplier=1), reads=[IDF.buf], writes=[IDF.buf])
    P.op("dve", lambda e: e.tensor_copy(out=IDB.ap, in_=IDF.ap), reads=[IDF.buf], writes=[IDB.buf])

    vs = XS[0]
    vec_rows = [
        (d_norm_mix[0], G1, 8), (d_norm_ple[0], G2, 8), (d_conv_b[0], CB, 8),
        (d_b_rg_a[0], BA, 8), (d_b_rg_i[0], BI, 8), (d_lam[0], LAM, 8), (d_pool_scale[0], PSC, 8),
    ]
    for src, r0, n in vec_rows:
        P.dma(lambda e, src=src, r0=r0, n=n: e.dma_start(out=vs.ap[r0:r0 + n, 0:128],
                                                         in_=src.rearrange("(a b) -> a b", b=128)),
              lane=vs.buf, writes=[vs.buf])
    P.dma(lambda e: e.dma_start(out=vs.ap[CW:CW + 32, 0:128],
                                in_=d_conv_w[0].rearrange("k (a b) -> (k a) b", b=128)),
          lane=vs.buf, writes=[vs.buf])
    P.dma(lambda e: e.dma_start(out=vs.ap[BP:BP + 8, 0:128],
                                in_=d_b_pool[0].rearrange("g (a b) -> (g a) b", b=128)),
          lane=vs.buf, writes=[vs.buf])
    bk = newbank()
    P.op("pe", lambda e: e.transpose(out=bk.ap[:, 0:96], in_=vs.ap[0:96, 0:128], identity=IDF.ap[0:96, 0:96]),
         reads=[vs.buf, IDF.buf], writes=[bk.buf])
    P.op("dve", lambda e: e.tensor_copy(out=CV.ap, in_=bk.ap[:, 0:96]), reads=[bk.buf], writes=[CV.buf])
    relbank(bk)
    P.op("dve", lambda e: e.tensor_scalar(out=CV2.ap[:, HBA:HBA + 16], in0=CV.ap[:, BA:BA + 16], scalar1=0.5,
                                          scalar2=None, op0=ALU.mult), reads=[CV.buf], writes=[CV2.buf])
    P.op("dve", lambda e: e.tensor_tensor(out=CV2.ap[:, BPS:BPS + 8], in0=CV.ap[:, BP:BP + 8],
                                          in1=CV.ap[:, PSC:PSC + 8], op=ALU.mult), reads=[CV.buf], writes=[CV2.buf])
    P.op("act", lambda e: e.activation(out=CV2.ap[:, CL:CL + 8], in_=CV.ap[:, LAM:LAM + 8], func=AF.Exp, scale=-1.0),
         reads=[CV.buf], writes=[CV2.buf])
    P.op("act", lambda e: e.activation(out=CV2.ap[:, CL:CL + 8], in_=CV2.ap[:, CL:CL + 8], func=AF.Ln, bias=1.0),
         reads=[CV2.buf], writes=[CV2.buf])
    P.op("dve", lambda e: e.tensor_scalar(out=CV2.ap[:, HCL:HCL + 8], in0=CV2.ap[:, CL:CL + 8], scalar1=-4.0,
                                          scalar2=None, op0=ALU.mult), reads=[CV2.buf], writes=[CV2.buf])
    P.op("dve", lambda e: e.tensor_scalar(out=CV2.ap[:, CL:CL + 8], in0=CV2.ap[:, CL:CL + 8], scalar1=-8.0,
                                          scalar2=None, op0=ALU.mult), reads=[CV2.buf], writes=[CV2.buf])
    P.op("pool", lambda e: e.iota(INVC.ap, pattern=[[0, 4], [1, 16]], base=1, channel_multiplier=0,
                                  allow_small_or_imprecise_dtypes=True), writes=[INVC.buf])
    for g in range(4):
        P.op("dve", lambda e, g=g: e.tensor_scalar(out=invc_v[:, g, :], in0=invc_v[:, g, :], scalar1=float(2 ** (g + 1)),
                                                   scalar2=None, op0=ALU.min), reads=[INVC.buf], writes=[INVC.buf])
    P.op("dve", lambda e: e.reciprocal(out=INVC.ap, in_=INVC.ap), reads=[INVC.buf], writes=[INVC.buf])

    P.op("dve", lambda e: e.tensor_scalar(out=CV.ap[:, G1:G1 + 8], in0=CV.ap[:, G1:G1 + 8], scalar1=32.0, scalar2=None,
                                          op0=ALU.mult), reads=[CV.buf], writes=[CV.buf])
    P.dma(lambda e: e.dma_start(out=G3.ap, in_=d_final_norm.partition_broadcast(128)), lane=G3.buf, writes=[G3.buf])
    P.op("dve", lambda e: e.tensor_scalar(out=G3.ap, in0=G3.ap, scalar1=32.0, scalar2=None, op0=ALU.mult),
         reads=[G3.buf], writes=[G3.buf])

    stg = [XS[1], XS[2], XR[0], XR[1], XR[2], XS[0]]
    sc = {"i": 0}

    def nextstg():
        s = stg[sc["i"] % len(stg)]
        sc["i"] += 1
        return s

    s1 = nextstg()
    P.dma(lambda e: e.dma_start(out=s1.ap[0:12, :], in_=d_sconv), lane=s1.buf, writes=[s1.buf])
    P.dma(lambda e: e.dma_start(out=s1.ap[12:16, :], in_=d_slru), lane=s1.buf, writes=[s1.buf])
    bk1 = newbank()
    for c in range(8):
        P.op("pe", lambda e, c=c: e.transpose(out=bk1.ap[:, c * 16:(c + 1) * 16], in_=s1.ap[0:16, c * 128:(c + 1) * 128],
                                              identity=IDF.ap[0:16, 0:16]), reads=[s1.buf, IDF.buf], writes=[bk1.buf])
    P.op("dve", lambda e: e.tensor_copy(out=SC16.ap, in_=bk1.ap[:, 0:128]), reads=[bk1.buf], writes=[SC16.buf])
    relbank(bk1)
    s2 = nextstg()
    P.dma(lambda e: e.dma_start(out=s2.ap[0:60, 0:512], in_=d_spool), lane=s2.buf, writes=[s2.buf])
    bk2 = newbank()
    for g in range(4):
        P.op("pe", lambda e, g=g: e.transpose(out=bk2.ap[:, g * 60:(g + 1) * 60], in_=s2.ap[0:60, g * 128:(g + 1) * 128],
                                              identity=IDF.ap[0:60, 0:60]), reads=[s2.buf, IDF.buf], writes=[bk2.buf])
    P.op("dve", lambda e: e.tensor_copy(out=SP60.ap, in_=bk2.ap[:, 0:240]), reads=[bk2.buf], writes=[SP60.buf])
    relbank(bk2)

    for dsrc, dst in ((d_w_rg_a, WA), (d_w_rg_i, WI)):
        s = nextstg()
        sv = s.ap.rearrange("p (c n) -> p c n", c=8)
        q = dsrc[0].rearrange("(c two) i j -> two i c j", two=2)
        P.op("pool", lambda e, s=s: e.memset(s.ap, 0.0), writes=[s.buf])
        P.dma(lambda e, sv=sv, q=q: e.dma_start(out=sv[0:64, :, 0:64], in_=q[0]), lane=s.buf, writes=[s.buf])
        P.dma(lambda e, sv=sv, q=q: e.dma_start(out=sv[64:128, :, 64:128], in_=q[1]), lane=s.buf, writes=[s.buf])
        P.op("dve", lambda e, s=s, dst=dst: e.tensor_copy(out=dst.ap, in_=s.ap), reads=[s.buf], writes=[dst.buf])
    sa_ = nextstg()
    sb_ = nextstg()
    P.dma(lambda e: e.dma_start(out=sa_.ap.rearrange("p (g j) -> p g j", g=4),
                                in_=d_w_pool[0].rearrange("g p j -> p g j")), lane=sa_.buf, writes=[sa_.buf])
    P.dma(lambda e: e.dma_start(out=sb_.ap, in_=d_pool_scale[0].partition_broadcast(128)), lane=sb_.buf, writes=[sb_.buf])
    P.op("dve", lambda e: e.tensor_tensor(out=WPOOL.ap, in0=sa_.ap, in1=sb_.ap, op=ALU.mult),
         reads=[sa_.buf, sb_.buf], writes=[WPOOL.buf])
    deferred_w = []
    xr_i = {"i": 0}

    def w_unit(dsrc, kc, dst_v, dstbuf, scal, post=1.0):
        def emit():
            s_ = XR[xr_i["i"] % 3]
            xr_i["i"] += 1
            P.dma(lambda e: e.dma_start(out=s_.ap, in_=dsrc[0][kc * 128:(kc + 1) * 128, :]), lane=s_.buf,
                  writes=[s_.buf])
            rd = [s_.buf] + ([CV.buf] if not isinstance(scal, float) else [])
            P.op(anyeng(), lambda e: e.tensor_scalar(out=dst_v[:, kc, :], in0=s_.ap, scalar1=scal, scalar2=post,
                                                     op0=ALU.mult, op1=ALU.mult), reads=rd, writes=[dstbuf])
        return emit

    for kc in range(8):
        deferred_w.append(w_unit(d_w_out, kc, wout_v, WOUT.buf, 0.25))
    for kc in range(8):
        deferred_w.append(w_unit(d_w_pg, kc, wpg_v, WPG.buf, CV.ap[:, G2 + kc:G2 + kc + 1], 16.0))
    for kc in range(2):
        deferred_w.append(w_unit(d_w_ple, kc, wpe_v, WPE.buf, 0.5))

    g1b = CV.ap[:, G1:G1 + 8].unsqueeze(2).to_broadcast([128, 8, 128])
    STGC = [(T(XR[i].ap.rearrange("p (k n) -> p k n", k=8), XR[i].buf), []) for i in range(3)]
    STGC.append((T(TGC[0].ap.rearrange("p (k n) -> p k n", k=8), TGC[0].buf), []))
    mg1f = MG[1].ap.bitcast(F32)
    for h in range(2):
        STGC.append((T(mg1f[:, h * 1024:(h + 1) * 1024].rearrange("p (k n) -> p k n", k=8), P.buf("mg1st%d" % h)),
                     [MG[1].buf]))
    cu = {"loaded": 0, "done": 0}
    chunks_done = set()

    def chunk_load_next():
        i = cu["loaded"]
        m = W_ORDER[i]
        hb, own = STGC[i % len(STGC)]
        P.dma(lambda e: e.dma_start(out=hb.ap, in_=d_w_in[0][:, m * 128:(m + 1) * 128].rearrange(
            "(k p) n -> p k n", p=128)), lane=hb.buf, writes=[hb.buf])
        cu["loaded"] += 1

    def chunk_compute_next():
        i = cu["done"]
        m = W_ORDER[i]
        while cu["loaded"] <= i:
            chunk_load_next()
        hb, own = STGC[i % len(STGC)]
        slot = RING[i % RING_N]
        P.op("dve", lambda e: e.tensor_tensor(out=slot.ap.rearrange("p (k n) -> p k n", k=8), in0=hb.ap, in1=g1b,
                                              op=ALU.mult), reads=[hb.buf, CV.buf] + own, writes=[slot.buf])
        P.dma(lambda e: e.dma_start(out=d_wsc[m], in_=slot.ap), lane=slot.buf, reads=[slot.buf], writes=[wsc_buf[m]])
        cu["done"] += 1
        chunks_done.add(m)
        if cu["loaded"] < NM:
            chunk_load_next()

    def ensure_chunk(m):
        while m not in chunks_done:
            chunk_compute_next()

    tiles = []
    for seq in range(2):
        for st in range(4):
            tiles.append(dict(kind="p", seq=seq, st=st, nseg=1, L=512, N=512, nt=4, tok0=seq * 2048 + st * 512,
                              first=(st == 0), last=(st == 3), idx=len(tiles)))
    tiles.append(dict(kind="s", seq=0, st=0, nseg=4, L=32, N=128, nt=1, tok0=0, first=False, last=True,
                      idx=len(tiles)))
    NST = len(tiles)

    stream = {"q_loaded": 0, "q_used": 0}
    total_q = NM * NST

    def ring_load(q):
        m = W_ORDER[q % NM]
        if q < NM:
            assert cu["done"] == q
            chunk_compute_next()
            return
        slot = RING[q % RING_N]
        P.dma(lambda e, slot=slot, m=m: e.dma_start(out=slot.ap, in_=d_wsc[m]), lane=slot.buf,
              reads=[wsc_buf[m]], writes=[slot.buf])

    def win_get(m):
        q = stream["q_used"]
        assert W_ORDER[q % NM] == m, (q, m)
        while stream["q_loaded"] < min(total_q, q + RING_N):
            ring_load(stream["q_loaded"])
            stream["q_loaded"] += 1
        stream["q_used"] += 1
        return RING[q % RING_N]

    a_tiles = []
    for S in tiles:
        for j in range(S["nt"]):
            a_tiles.append((S, j))
    NAT = len(a_tiles)

    def x_src(S, j):
        if S["kind"] == "p":
            r0 = S["tok0"] + j * 128
            return d_xp[r0:r0 + 128, :]
        return d_xs[:, :]

    def p_src(S, j):
        if S["kind"] == "p":
            r0 = S["tok0"] + j * 128
            return d_pp[r0:r0 + 128, :]
        return d_ps[:, :]

    def y_dst(S, j):
        if S["kind"] == "p":
            r0 = S["tok0"] + j * 128
            return o_yp[r0:r0 + 128, :]
        return o_ys[:, :]

    aload = {"n": 0}

    def a_ensure(i):
        while aload["n"] <= min(i, NAT - 1):
            k = aload["n"]
            S, j = a_tiles[k]
            slot = XS[k % 3]
            P.dma(lambda e, slot=slot, src=x_src(S, j): e.dma_start(out=slot.ap, in_=src), lane=slot.buf,
                  writes=[slot.buf])
            aload["n"] += 1

    cload = {"n": 0}

    def c_ensure(i):
        while cload["n"] <= min(i, NAT - 1):
            k = cload["n"]
            S, j = a_tiles[k]
            slot = XR[k % 3]
            P.dma(lambda e, slot=slot, src=x_src(S, j): e.dma_start(out=slot.ap, in_=src), lane=slot.buf,
                  writes=[slot.buf])
            ps_ = P32[k % 2]
            P.dma(lambda e, ps_=ps_, src=p_src(S, j): e.dma_start(out=ps_.ap, in_=src), lane=ps_.buf,
                  writes=[ps_.buf])
            cload["n"] += 1

    def rstd_ops(st):
        P.op("dve", lambda e: e.tensor_scalar(out=st.ap[:, 2:3], in0=st.ap[:, 0:1], scalar1=float(D) * EPS,
                                              scalar2=None, op0=ALU.add), reads=[st.buf], writes=[st.buf])
        P.op("pool", lambda e: e.tensor_tensor(out=st.ap[:, 1:2], in0=st.ap[:, 2:3], in1=NEGH.ap, op=ALU.pow),
             reads=[st.buf, NEGH.buf], writes=[st.buf])

    a_stat = {}

    def phase_a1(k):
        S, j = a_tiles[k]
        a_ensure(k + 2)
        x = XS[k % 3]
        st = newstat()
        u = UB[k % 2]
        P.op("act", lambda e: e.activation(out=JUNK.ap, in_=x.ap, func=AF.Square, accum_out=st.ap[:, 0:1]),
             reads=[x.buf], writes=[st.buf, JUNK.buf])
        a_stat[k] = st

    def phase_a1r(k):
        rstd_ops(a_stat[k])

    def phase_a1b(k):
        x = XS[k % 3]
        u = UB[k % 2]
        st = a_stat.pop(k)
        P.op("act", lambda e: e.activation(out=u.ap, in_=x.ap, func=AF.Identity, scale=st.ap[:, 1:2]),
             reads=[x.buf, st.buf], writes=[u.buf])

    def phase_a2(k):
        S, j = a_tiles[k]
        u = UB[k % 2]
        utb = UT[S["idx"] % 2]
        N = S["N"]
        ut_v = utb.ap[:, 0:8 * N].rearrange("p (k n) -> p k n", k=8)
        tb = newtbank()
        tbv = tb.ap.bitcast(BF16)
        for c in range(8):
            P.op("pe", lambda e, c=c: e.transpose(out=tbv[:, c * 128:(c + 1) * 128], in_=u.ap[:, c * 128:(c + 1) * 128],
                                                  identity=IDB.ap), reads=[u.buf, IDB.buf], writes=[tb.buf])
        P.op("act", lambda e: e.activation(out=ut_v[:, :, j * 128:(j + 1) * 128],
                                           in_=tbv.rearrange("p (k n) -> p k n", k=8), func=AF.Copy),
             reads=[tb.buf], writes=[utb.buf])

    stepc = {"n": 0}

    def seg3(ap2, nseg):
        return ap2.rearrange("p (s l) -> p s l", s=nseg)

    def mm_group(bank, N, wslot, utb, ut_v):
        wv = wslot.ap.rearrange("p (k n) -> p k n", k=8)
        for kc in range(8):
            P.op("pe", lambda e, kc=kc: e.matmul(bank.ap[:, 0:N], lhsT=wv[:, kc, :], rhs=ut_v[:, kc, :],
                                                 start=(kc == 0), stop=(kc == 7)),
                 reads=[wslot.buf, utb.buf], writes=[bank.buf])

    carry = {}
    deferred_ops = []
    deferred_ops2 = []

    def flush_deferred():
        while deferred_ops:
            deferred_ops.pop(0)()
        while deferred_ops2:
            deferred_ops.append(deferred_ops2.pop(0))

    def views(S):
        N = S["N"]
        utb = UT[S["idx"] % 2]
        ut_v = utb.ap[:, 0:8 * N].rearrange("p (k n) -> p k n", k=8)
        mgb = MG[S["idx"] % 2]
        mg_v = mgb.ap[:, 0:8 * N].rearrange("p (k n) -> p k n", k=8)
        return utb, ut_v, mgb, mg_v

    def lru_p1(S, c):
        nseg, L, N = S["nseg"], S["L"], S["N"]
        par = c % 2
        utb, ut_v, mgb, mg_v = views(S)
        sample = S["kind"] == "s"
        b_xl, b_gl, b_ml = newbank(), newbank(), newbank()
        for bank, m in ((b_xl, c), (b_gl, 8 + c), (b_ml, 28 + c)):
            mm_group(bank, N, win_get(m), utb, ut_v)
        xlw = XLW[par]
        xlw_v = xlw.ap[:, 0:nseg * (3 + L)].rearrange("p (s w) -> p s w", s=nseg)
        if S["first"]:
            P.op("pool", lambda e: e.memset(xlw_v[:, :, 0:3], 0.0), writes=[XLWH[par]])
        elif sample:
            P.op("pool", lambda e: e.tensor_copy(out=xlw_v[:, :, 0:3],
                                                 in_=sc16_v[:, c, 0:12].rearrange("p (s r) -> p s r", r=3)),
                 reads=[SC16.buf], writes=[XLWH[par]])
        else:
            P.op("pool", lambda e: e.tensor_copy(out=xlw_v[:, :, 0:3], in_=CONVH[c].ap.unsqueeze(1)),
                 reads=[CONVH[c].buf], writes=[XLWH[par]])
        P.op("act", lambda e: e.activation(out=xlw_v[:, :, 3:3 + L], in_=seg3(b_xl.ap[:, 0:N], nseg), func=AF.Copy),
             reads=[b_xl.buf], writes=[xlw.buf])
        acc = newwork()
        acc2 = acc.ap[:, 0:N]
        acc_v = seg3(acc2, nseg)
        P.op("act", lambda e: e.activation(out=acc2, in_=b_xl.ap[:, 0:N], func=AF.Identity,
                                           scale=CV.ap[:, CW + 24 + c:CW + 24 + c + 1], bias=CV.ap[:, CB + c:CB + c + 1]),
             reads=[b_xl.buf, CV.buf], writes=[acc.buf])
        relbank(b_xl)
        tg, tm = newwork(), newwork()
        P.op("act", lambda e: e.activation(out=tg.ap[:, 0:N], in_=b_gl.ap[:, 0:N], func=AF.Tanh, scale=0.5),
             reads=[b_gl.buf], writes=[tg.buf])
        P.op("act", lambda e: e.activation(out=tm.ap[:, 0:N], in_=b_ml.ap[:, 0:N], func=AF.Tanh, scale=0.5),
             reads=[b_ml.buf], writes=[tm.buf])
        relbank(b_ml)
        if not S["last"]:
            P.op("pool", lambda e: e.tensor_copy(out=CONVH[c].ap.unsqueeze(1), in_=xlw_v[:, :, L:L + 3]),
                 reads=[xlw.buf], writes=[CONVH[c].buf])
        else:
            P.op("pool", lambda e: e.tensor_copy(
                out=st_v[:, 0:nseg, 0:24].rearrange("p s (r c) -> p s r c", c=8)[:, :, :, c],
                in_=xlw_v[:, :, L:L + 3]), reads=[xlw.buf], writes=[ST.buf])
        P.op("dve", lambda e: e.scalar_tensor_tensor(out=tg.ap[:, 0:N], in0=tg.ap[:, 0:N], scalar=1.0,
                                                     in1=b_gl.ap[:, 0:N], op0=ALU.add, op1=ALU.mult),
             reads=[tg.buf, b_gl.buf], writes=[tg.buf])
        relbank(b_gl)
        xcb = XCB[par]

        def later():
            for j in (1, 2, 3):
                k = 3 - j
                P.op("dve", lambda e, j=j, k=k: e.scalar_tensor_tensor(
                    out=acc_v, in0=xlw_v[:, :, 3 - j:3 - j + L], scalar=CV.ap[:, CW + k * 8 + c:CW + k * 8 + c + 1],
                    in1=acc_v, op0=ALU.mult, op1=ALU.add), reads=[xlw.buf, XLWH[par], acc.buf, CV.buf],
                    writes=[acc.buf])
            P.op("dve", lambda e: e.tensor_copy(out=xcb.ap[:, 0:N], in_=acc2), reads=[acc.buf], writes=[xcb.buf])
            P.op("dve", lambda e: e.scalar_tensor_tensor(out=tm.ap[:, 0:N], in0=tm.ap[:, 0:N], scalar=1.0,
                                                         in1=tg.ap[:, 0:N], op0=ALU.add, op1=ALU.mult),
                 reads=[tm.buf, tg.buf], writes=[tm.buf])
            relwork(tg)
        deferred_ops.append(later)
        carry[("L", c)] = (acc, tm, xcb)

    def lru_p2(S, c):
        nseg, L, N = S["nseg"], S["L"], S["N"]
        utb, ut_v, mgb, mg_v = views(S)
        sample = S["kind"] == "s"
        acc, tm, xcb = carry.pop(("L", c))
        acc2 = acc.ap[:, 0:N]
        b_r, b_i = newbank(), newbank()
        P.op("pe", lambda e: e.matmul(b_r.ap[:, 0:N], lhsT=wa_v[:, c, :], rhs=xcb.ap[:, 0:N], start=True, stop=True),
             reads=[xcb.buf, WA.buf], writes=[b_r.buf])
        P.op("pe", lambda e: e.matmul(b_i.ap[:, 0:N], lhsT=wi_v[:, c, :], rhs=xcb.ap[:, 0:N], start=True, stop=True),
             reads=[xcb.buf, WI.buf], writes=[b_i.buf])
        tr, ti, aa, a2 = (newwork() for _ in range(4))
        P.op("act", lambda e: e.activation(out=tr.ap[:, 0:N], in_=b_r.ap[:, 0:N], func=AF.Tanh, scale=0.5,
                                           bias=CV2.ap[:, HBA + c:HBA + c + 1]), reads=[b_r.buf, CV2.buf], writes=[tr.buf])
        P.op("act", lambda e: e.activation(out=ti.ap[:, 0:N], in_=b_i.ap[:, 0:N], func=AF.Tanh, scale=0.5,
                                           bias=CV2.ap[:, HBI + c:HBI + c + 1]), reads=[b_i.buf, CV2.buf], writes=[ti.buf])
        relbank(b_r)
        relbank(b_i)
        P.op("act", lambda e: e.activation(out=aa.ap[:, 0:N], in_=tr.ap[:, 0:N], func=AF.Exp,
                                           scale=CV2.ap[:, HCL + c:HCL + c + 1], bias=CV2.ap[:, HCL + c:HCL + c + 1]),
             reads=[tr.buf, CV2.buf], writes=[aa.buf])
        P.op("act", lambda e: e.activation(out=a2.ap[:, 0:N], in_=tr.ap[:, 0:N], func=AF.Exp,
                                           scale=CV2.ap[:, CL + c:CL + c + 1], bias=CV2.ap[:, CL + c:CL + c + 1]),
             reads=[tr.buf, CV2.buf], writes=[a2.buf])
        relwork(tr)

        def later():
            P.op("dve", lambda e: e.tensor_scalar(out=a2.ap[:, 0:N], in0=a2.ap[:, 0:N], scalar1=1.0 - 2.0 ** -24,
                                                  scalar2=None, op0=ALU.min), reads=[a2.buf], writes=[a2.buf])
            P.op("dve", lambda e: e.scalar_tensor_tensor(out=ti.ap[:, 0:N], in0=ti.ap[:, 0:N], scalar=1.0, in1=acc2,
                                                         op0=ALU.add, op1=ALU.mult), reads=[ti.buf, acc.buf],
                 writes=[ti.buf])
            relwork(acc)
        deferred_ops.append(later)
        carry[("M", c)] = (ti, aa, a2, tm)

    def lru_p2b(S, c):
        nseg, L, N = S["nseg"], S["L"], S["N"]
        utb, ut_v, mgb, mg_v = views(S)
        sample = S["kind"] == "s"
        ti, aa, a2, tm = carry.pop(("M", c))
        hh = newwork()
        P.op("act", lambda e: e.activation(out=a2.ap[:, 0:N], in_=a2.ap[:, 0:N], func=AF.Sqrt, scale=-0.25, bias=0.25),
             reads=[a2.buf], writes=[a2.buf])
        deferred_ops.append(lambda: lru_p2c(S, c, ti, aa, a2, tm, hh))

    def lru_p2c(S, c, ti, aa, a2, tm, hh):
        nseg, L, N = S["nseg"], S["L"], S["N"]
        utb, ut_v, mgb, mg_v = views(S)
        sample = S["kind"] == "s"
        if S["first"]:
            P.op("dve", lambda e: e.memset(a2.ap[:, 0:1], 0.5), reads=[a2.buf], writes=[a2.buf])
        P.op("dve", lambda e: e.tensor_tensor(out=ti.ap[:, 0:N], in0=ti.ap[:, 0:N], in1=a2.ap[:, 0:N], op=ALU.mult),
             reads=[ti.buf, a2.buf], writes=[ti.buf])
        relwork(a2)
        for sg in range(nseg):
            if S["first"]:
                init, rb = 0.0, []
            elif sample:
                init, rb = sc16_v[:, c, 12 + sg:13 + sg], [SC16.buf]
            else:
                init, rb = HST[c].ap, [HST[c].buf]
            P.op("dve", lambda e, sg=sg, init=init: e.tensor_tensor_scan(
                out=hh.ap[:, sg * L:(sg + 1) * L], data0=aa.ap[:, sg * L:(sg + 1) * L],
                data1=ti.ap[:, sg * L:(sg + 1) * L], initial=init, op0=ALU.mult, op1=ALU.add),
                reads=[aa.buf, ti.buf] + rb, writes=[hh.buf])
        relwork(ti, aa)
        hh_v = seg3(hh.ap[:, 0:N], nseg)
        if not S["last"]:
            P.op("pool", lambda e: e.tensor_copy(out=HST[c].ap, in_=hh.ap[:, N - 1:N]), reads=[hh.buf],
                 writes=[HST[c].buf])
        else:
            P.op("pool", lambda e: e.tensor_copy(out=st_v[:, 0:nseg, 84 + c:85 + c], in_=hh_v[:, :, L - 1:L]),
                 reads=[hh.buf], writes=[ST.buf])
        P.op("pool", lambda e: e.tensor_tensor(out=mg_v[:, c, :], in0=hh.ap[:, 0:N], in1=tm.ap[:, 0:N], op=ALU.mult),
             reads=[hh.buf, tm.buf], writes=[mgb.buf])
        relwork(hh, tm)

    def pool_p1(S, g):
        nseg, L, N = S["nseg"], S["L"], S["N"]
        par = g % 2
        W = 15 + L
        utb, ut_v, mgb, mg_v = views(S)
        sample = S["kind"] == "s"
        wnd = 2 ** (g + 1)
        b_xp = newbank()
        mm_group(b_xp, N, win_get(16 + g), utb, ut_v)
        xpw = XPW[par]
        xpw_v = xpw.ap[:, 0:nseg * W].rearrange("p (s w) -> p s w", s=nseg)
        if S["first"]:
            P.op("pool", lambda e: e.memset(xpw_v[:, :, 0:15], 0.0), writes=[XPWH[par]])
        elif sample:
            P.op("pool", lambda e: e.tensor_copy(out=xpw_v[:, :, 0:15],
                                                 in_=sp60_v[:, g, :].rearrange("p (s r) -> p s r", r=15)),
                 reads=[SP60.buf], writes=[XPWH[par]])
        else:
            P.op("pool", lambda e: e.tensor_copy(out=xpw_v[:, :, 0:15], in_=POOLH[g].ap.unsqueeze(1)),
                 reads=[POOLH[g].buf], writes=[XPWH[par]])
        P.op("act", lambda e: e.activation(out=xpw_v[:, :, 15:W], in_=seg3(b_xp.ap[:, 0:N], nseg), func=AF.Copy),
             reads=[b_xp.buf], writes=[xpw.buf])
        relbank(b_xp)
        src, src_v = xpw, xpw_v
        for i in range(g + 1):
            sh = 2 ** i
            lo = 2 ** (i + 1) - 1
            dst = (SA, SBB)[i % 2]
            dst_v = dst.ap[:, 0:nseg * W].rearrange("p (s w) -> p s w", s=nseg)
            P.op("pool", lambda e, dst_v=dst_v, src_v=src_v, lo=lo, sh=sh: e.tensor_tensor(
                out=dst_v[:, :, lo:W], in0=src_v[:, :, lo:W], in1=src_v[:, :, lo - sh:W - sh], op=ALU.add),
                reads=[src.buf, XPWH[par]], writes=[dst.buf])
            src, src_v = dst, dst_v
        if not S["last"]:
            P.op("pool", lambda e: e.tensor_copy(out=POOLH[g].ap.unsqueeze(1), in_=xpw_v[:, :, L:L + 15]),
                 reads=[xpw.buf], writes=[POOLH[g].buf])
        else:
            P.op("pool", lambda e: e.tensor_copy(
                out=st_v[:, 0:nseg, 24:84].rearrange("p s (r g) -> p s r g", g=4)[:, :, :, g],
                in_=xpw_v[:, :, L:L + 15]), reads=[xpw.buf], writes=[ST.buf])
        pl = PL[par]
        pl_v = seg3(pl.ap[:, 0:N], nseg)
        wsum_v = src_v[:, :, 15:W]

        def later():
            P.op("dve", lambda e: e.scalar_tensor_tensor(out=pl_v, in0=wsum_v, scalar=1.0 / wnd, in1=xpw_v[:, :, 15:W],
                                                         op0=ALU.mult, op1=ALU.subtract),
                 reads=[src.buf, xpw.buf], writes=[pl.buf])
            if S["first"]:
                P.op("dve", lambda e: e.tensor_tensor(out=T16.ap, in0=src_v[:, 0, 15:31], in1=invc_v[:, g, :],
                                                      op=ALU.mult), reads=[src.buf, INVC.buf], writes=[T16.buf])
                P.op("dve", lambda e: e.tensor_tensor(out=pl.ap[:, 0:16], in0=T16.ap, in1=xpw_v[:, 0, 15:31],
                                                      op=ALU.subtract), reads=[T16.buf, xpw.buf, pl.buf], writes=[pl.buf])
        (deferred_ops2 if g < 3 else deferred_ops).append(later)

    def pool_p2(S, g):
        nseg, L, N = S["nseg"], S["L"], S["N"]
        utb, ut_v, mgb, mg_v = views(S)
        pl = PL[g % 2]
        for hf in range(2):
            cc = 2 * g + hf
            b_gp, b_mp = newbank(), newbank()
            mm_group(b_gp, N, win_get(20 + cc), utb, ut_v)
            mm_group(b_mp, N, win_get(36 + cc), utb, ut_v)
            b_pg = newbank()
            P.op("pe", lambda e, hf=hf, b_pg=b_pg: e.matmul(b_pg.ap[:, 0:N], lhsT=wpool_v[:, g, hf * 128:(hf + 1) * 128],
                                                            rhs=pl.ap[:, 0:N], start=True, stop=True),
                 reads=[pl.buf, WPOOL.buf], writes=[b_pg.buf])
            tg, tm = newwork(), newwork()
            P.op("act", lambda e, tg=tg, b_gp=b_gp: e.activation(out=tg.ap[:, 0:N], in_=b_gp.ap[:, 0:N], func=AF.Tanh,
                                                                 scale=0.5), reads=[b_gp.buf], writes=[tg.buf])
            P.op("act", lambda e, tm=tm, b_mp=b_mp: e.activation(out=tm.ap[:, 0:N], in_=b_mp.ap[:, 0:N], func=AF.Tanh,
                                                                 scale=0.5), reads=[b_mp.buf], writes=[tm.buf])
            relbank(b_mp)
            P.op("dve", lambda e, tg=tg, b_gp=b_gp: e.scalar_tensor_tensor(
                out=tg.ap[:, 0:N], in0=tg.ap[:, 0:N], scalar=1.0, in1=b_gp.ap[:, 0:N], op0=ALU.add, op1=ALU.mult),
                reads=[tg.buf, b_gp.buf], writes=[tg.buf])
            relbank(b_gp)
            P.op("dve", lambda e, tg=tg, tm=tm: e.scalar_tensor_tensor(
                out=tm.ap[:, 0:N], in0=tm.ap[:, 0:N], scalar=1.0, in1=tg.ap[:, 0:N], op0=ALU.add, op1=ALU.mult),
                reads=[tm.buf, tg.buf], writes=[tm.buf])
            P.op("dve", lambda e, tg=tg, tm=tm, b_pg=b_pg, cc=cc: e.scalar_tensor_tensor(
                out=tg.ap[:, 0:N], in0=b_pg.ap[:, 0:N], scalar=CV2.ap[:, BPS + cc:BPS + cc + 1],
                in1=tm.ap[:, 0:N], op0=ALU.add, op1=ALU.mult), reads=[b_pg.buf, tm.buf, CV2.buf], writes=[tg.buf])
            relbank(b_pg)
            P.op("pool", lambda e, tg=tg, cc=cc: e.tensor_tensor(out=mg_v[:, cc, :], in0=mg_v[:, cc, :],
                                                                 in1=tg.ap[:, 0:N], op=ALU.add),
                 reads=[mgb.buf, tg.buf], writes=[mgb.buf])
            relwork(tg, tm)

    def state_out(S):
        nseg = S["nseg"]
        bk_ = newbank()
        for sg in range(nseg):
            P.op("pe", lambda e, sg=sg: e.transpose(out=bk_.ap[0:92, sg * 128:(sg + 1) * 128], in_=st_v[:, sg, :],
                                                    identity=IDF.ap), reads=[ST.buf, IDF.buf], writes=[bk_.buf])
        P.op("act", lambda e: e.activation(out=STO.ap[0:92, 0:nseg * 128], in_=bk_.ap[0:92, 0:nseg * 128], func=AF.Copy),
             reads=[bk_.buf], writes=[STO.buf])
        relbank(bk_)
        for sg in range(nseg):
            if S["kind"] == "p":
                oc, ol, op_ = o_ncp[S["seq"]], o_nlp[S["seq"]], o_npp[S["seq"]]
            else:
                oc, ol, op_ = o_ncs[sg], o_nls[sg], o_nps[sg]
            P.dma(lambda e, sg=sg, oc=oc: e.dma_start(out=oc.rearrange("r (c p) -> (r c) p", p=128),
                                                      in_=STO.ap[0:24, sg * 128:(sg + 1) * 128]),
                  lane=STO.buf, reads=[STO.buf], queue="act")
            P.dma(lambda e, sg=sg, op_=op_: e.dma_start(out=op_.rearrange("r (g p) -> (r g) p", p=128),
                                                        in_=STO.ap[24:84, sg * 128:(sg + 1) * 128]),
                  lane=STO.buf, reads=[STO.buf], queue="act")
            P.dma(lambda e, sg=sg, ol=ol: e.dma_start(out=ol.rearrange("(c p) -> c p", p=128),
                                                      in_=STO.ap[84:92, sg * 128:(sg + 1) * 128]),
                  lane=STO.buf, reads=[STO.buf], queue="act")

    def b_task(S, t):
        kind, a = B_TASKS[t]
        flush_deferred()
        {"L1": lru_p1, "L2a": lru_p2, "L2b": lru_p2b, "Q1": pool_p1, "Q2": pool_p2}[kind](S, a)
        if t == len(B_TASKS) - 1:
            flush_deferred()
            flush_deferred()
            if S["last"]:
                state_out(S)

    pending_st = []

    def flush_stores():
        while pending_st:
            pending_st.pop(0)()

    NCS = 8
    cst_state = {}

    def c_stage(k, stage):
        S, j = a_tiles[k]
        N = S["N"]
        mgb = MG[S["idx"] % 2]
        mg_v = mgb.ap[:, 0:8 * N].rearrange("p (k n) -> p k n", k=8)
        xr = XR[k % 3]
        u2 = U2[k % 2]
        u2t = U2T[k % 2]
        p32, pb, pt = P32[k % 2], PB[k % 2], PT[k % 2]
        tg = TGC[0]
        if stage == 0:
            c_ensure(k + 1)
            P.op("pool", lambda e: e.tensor_copy(out=pb.ap, in_=p32.ap), reads=[p32.buf], writes=[pb.buf])
            d = [newbank(), newbank()]
            for hf in range(2):
                for kc in range(8):
                    P.op("pe", lambda e, hf=hf, kc=kc: e.matmul(
                        d[hf].ap, lhsT=mg_v[:, kc, j * 128:(j + 1) * 128], rhs=wout_v[:, kc, hf * 512:(hf + 1) * 512],
                        start=(kc == 0), stop=(kc == 7)), reads=[mgb.buf, WOUT.buf], writes=[d[hf].buf])
            for hf in range(2):
                P.op("dve", lambda e, hf=hf: e.tensor_tensor(out=xr.ap[:, hf * 512:(hf + 1) * 512],
                                                             in0=xr.ap[:, hf * 512:(hf + 1) * 512], in1=d[hf].ap,
                                                             op=ALU.add), reads=[xr.buf, d[hf].buf], writes=[xr.buf])
            relbank(d[0])
            relbank(d[1])
            P.op("dve", lambda e: e.tensor_copy(out=u2.ap, in_=xr.ap), reads=[xr.buf], writes=[u2.buf])
        elif stage == 1:
            st = newstat()
            P.op("act", lambda e: e.activation(out=JUNK.ap, in_=xr.ap, func=AF.Square, accum_out=st.ap[:, 0:1]),
                 reads=[xr.buf], writes=[st.buf, JUNK.buf])
            cst_state[k] = st
        elif stage == 2:
            rstd_ops(cst_state[k])
            tb = newtbank()
            tbv = tb.ap.bitcast(BF16)
            for c in range(8):
                P.op("pe", lambda e, c=c: e.transpose(out=tbv[:, c * 128:(c + 1) * 128],
                                                      in_=u2.ap[:, c * 128:(c + 1) * 128], identity=IDB.ap),
                     reads=[u2.buf, IDB.buf], writes=[tb.buf])
            P.op("act", lambda e: e.activation(out=u2t.ap, in_=tbv, func=AF.Copy), reads=[tb.buf], writes=[u2t.buf])
            tb2 = newtbank()
            tb2v = tb2.ap.bitcast(BF16)
            for c in range(2):
                P.op("pe", lambda e, c=c: e.transpose(out=tb2v[:, c * 128:(c + 1) * 128],
                                                      in_=pb.ap[:, c * 128:(c + 1) * 128], identity=IDB.ap),
                     reads=[pb.buf, IDB.buf], writes=[tb2.buf])
            P.op("act", lambda e: e.activation(out=pt.ap, in_=tb2v[:, 0:256], func=AF.Copy), reads=[tb2.buf],
                 writes=[pt.buf])
        elif stage == 3:
            u2t_v = u2t.ap.rearrange("p (k n) -> p k n", k=8)
            eb = [newbank(), newbank()]
            for hf in range(2):
                for kc in range(8):
                    P.op("pe", lambda e, hf=hf, kc=kc: e.matmul(
                        eb[hf].ap, lhsT=u2t_v[:, kc, :], rhs=wpg_v[:, kc, hf * 512:(hf + 1) * 512],
                        start=(kc == 0), stop=(kc == 7)), reads=[u2t.buf, WPG.buf], writes=[eb[hf].buf])
            st = cst_state.pop(k)
            for hf in range(2):
                P.op("act", lambda e, hf=hf: e.activation(out=tg.ap[:, hf * 512:(hf + 1) * 512], in_=eb[hf].ap,
                                                          func=AF.Tanh, scale=st.ap[:, 1:2]),
                     reads=[eb[hf].buf, st.buf], writes=[tg.buf])
            relbank(eb[0])
            relbank(eb[1])
        elif stage == 4:
            pt_v = pt.ap.rearrange("p (k n) -> p k n", k=2)
            fbk = [newbank(), newbank()]
            for hf in range(2):
                for kc in range(2):
                    P.op("pe", lambda e, hf=hf, kc=kc: e.matmul(
                        fbk[hf].ap, lhsT=pt_v[:, kc, :], rhs=wpe_v[:, kc, hf * 512:(hf + 1) * 512],
                        start=(kc == 0), stop=(kc == 1)), reads=[pt.buf, WPE.buf], writes=[fbk[hf].buf])
            for hf in range(2):
                P.op("dve", lambda e, hf=hf: e.scalar_tensor_tensor(
                    out=tg.ap[:, hf * 512:(hf + 1) * 512], in0=tg.ap[:, hf * 512:(hf + 1) * 512], scalar=1.0,
                    in1=fbk[hf].ap, op0=ALU.add, op1=ALU.mult), reads=[tg.buf, fbk[hf].buf], writes=[tg.buf])
            relbank(fbk[0])
            relbank(fbk[1])
            P.op("pool", lambda e: e.tensor_tensor(out=xr.ap, in0=xr.ap, in1=tg.ap, op=ALU.add),
                 reads=[xr.buf, tg.buf], writes=[xr.buf])
        elif stage == 5:
            st = newstat()
            P.op("act", lambda e: e.activation(out=JUNK.ap, in_=xr.ap, func=AF.Square, accum_out=st.ap[:, 0:1]),
                 reads=[xr.buf], writes=[st.buf, JUNK.buf])
            cst_state[("y", k)] = st
        elif stage == 6:
            rstd_ops(cst_state[("y", k)])
        elif stage == 7:
            st = cst_state.pop(("y", k))
            P.op("dve", lambda e: e.scalar_tensor_tensor(out=xr.ap, in0=xr.ap, scalar=st.ap[:, 1:2], in1=G3.ap,
                                                         op0=ALU.mult, op1=ALU.mult),
                 reads=[xr.buf, st.buf, G3.buf], writes=[xr.buf])
            pending_st.append(lambda dst=y_dst(S, j), xr=xr: P.dma(
                lambda e: e.dma_start(out=dst, in_=xr.ap), lane=xr.buf, reads=[xr.buf], queue="act"))

    first_tile = {}
    k0 = 0
    for S in tiles:
        first_tile[S["idx"]] = k0
        k0 += S["nt"]
    NT = len(B_TASKS)
    a_ensure(1)
    for j in range(tiles[0]["nt"]):
        phase_a1(first_tile[0] + j)
        phase_a1r(first_tile[0] + j)
        phase_a1b(first_tile[0] + j)
        if j >= 1:
            phase_a2(first_tile[0] + j - 1)
    phase_a2(first_tile[0] + tiles[0]["nt"] - 1)
    for _ in range(len(STGC)):
        chunk_load_next()
    for r in range(NST + 1):
        Sb = tiles[r] if r < NST else None
        Sc = tiles[r - 1] if r >= 1 else None
        Sa = tiles[r + 1] if r + 1 < NST else None
        cslot = {}
        if Sc is not None:
            ntc = Sc["nt"]
            for j in range(ntc):
                base = (j * NT) // ntc
                offs = (0, 1, 3, 5, 6, 8, 9, 11) if ntc > 1 else (0, 4, 8, 12, 16, 20, 24, 28)
                for stg_ in range(NCS):
                    cslot.setdefault(min(NT - 1, base + offs[stg_]), []).append((first_tile[Sc["idx"]] + j, stg_))
        ast = []
        if Sa is not None:
            ast = [first_tile[Sa["idx"]] + j for j in range(Sa["nt"])]
        for t in range(NT):
            if r == 0:
                if cu["done"] >= NM and deferred_w:
                    deferred_w.pop(0)()
                    if deferred_w:
                        deferred_w.pop(0)()
            if Sb is not None:
                b_task(Sb, t)
            flush_stores()
            for (kk, stg_) in cslot.get(t, []):
                c_stage(kk, stg_)
            nA = len(ast)
            for i_, kk in enumerate(ast):
                t2 = ((i_ + 1) * NT) // nA - 1
                if t == max(0, t2 - 7):
                    phase_a1(kk)
                if t == max(1, t2 - 5):
                    phase_a1r(kk)
                if t == max(2, t2 - 3):
                    phase_a1b(kk)
                if t == t2:
                    phase_a2(kk)
        if r == 0:
            assert cu["done"] == NM
            while deferred_w:
                deferred_w.pop(0)()
            c_ensure(0)

    flush_stores()
    P.finalize()
    es.close()
    return nc, P


_CACHE = {}


def _get_program():
    if "nc" not in _CACHE:
        nc, P = build_program()
        _CACHE["nc"] = nc
        _CACHE["stats"] = P.stats
    return _CACHE["nc"]


def kernel(x_prompt, x_sample, p_prompt, p_sample, state_conv, state_lru, state_pool,
           norm_mix, w_in, conv_w, conv_b, w_rg_a, b_rg_a, w_rg_i, b_rg_i, lru_lambda,
           w_pool, b_pool, pool_scale, w_out, norm_ple, w_ple_gate, w_ple, final_norm):
    f = lambda a: np.ascontiguousarray(np.asarray(a, dtype=np.float32))
    x_prompt, x_sample, p_prompt, p_sample = f(x_prompt), f(x_sample), f(p_prompt), f(p_sample)
    state_conv, state_lru, state_pool = f(state_conv), f(state_lru), f(state_pool)
    shared = {
        "norm_mix": f(norm_mix), "w_in": f(w_in), "conv_w": f(conv_w), "conv_b": f(conv_b),
        "w_rg_a": f(w_rg_a), "b_rg_a": f(b_rg_a), "w_rg_i": f(w_rg_i), "b_rg_i": f(b_rg_i),
        "lru_lambda": f(lru_lambda), "w_pool": f(w_pool), "b_pool": f(b_pool), "pool_scale": f(pool_scale),
        "w_out": f(w_out), "norm_ple": f(norm_ple), "w_ple_gate": f(w_ple_gate), "w_ple": f(w_ple),
        "final_norm": f(final_norm),
    }
    in_maps = []
    for i in range(NCORES):
        m = dict(shared)
        m["xp"] = np.ascontiguousarray(x_prompt[2 * i:2 * i + 2].reshape(4096, 1024))
        m["pp"] = np.ascontiguousarray(p_prompt[0, 2 * i:2 * i + 2].reshape(4096, 256))
        m["xs"] = np.ascontiguousarray(x_sample[4 * i:4 * i + 4].reshape(128, 1024))
        m["ps"] = np.ascontiguousarray(p_sample[0, 4 * i:4 * i + 4].reshape(128, 256))
        m["sconv"] = np.ascontiguousarray(state_conv[0, 4 * i:4 * i + 4].reshape(12, 1024))
        m["slru"] = np.ascontiguousarray(state_lru[0, 4 * i:4 * i + 4].reshape(4, 1024))
        m["spool"] = np.ascontiguousarray(state_pool[0, 4 * i:4 * i + 4].reshape(60, 512))
        in_maps.append(m)
    nc = _get_program()
    res = run_bass_kernel_spmd(nc, in_maps, core_ids=list(range(NCORES)))
    R = res.results
    y_prompt = np.concatenate([R[i]["yp"].reshape(2, 2048, 1024) for i in range(NCORES)], axis=0)
    y_sample = np.concatenate([R[i]["ys"].reshape(4, 32, 1024) for i in range(NCORES)], axis=0)
    ncp = np.concatenate([R[i]["ncp"] for i in range(NCORES)], axis=0)[None]
    nlp = np.concatenate([R[i]["nlp"] for i in range(NCORES)], axis=0)[None]
    npp = np.concatenate([R[i]["npp"] for i in range(NCORES)], axis=0)[None]
    ncs = np.concatenate([R[i]["ncs"] for i in range(NCORES)], axis=0)[None]
    nls = np.concatenate([R[i]["nls"] for i in range(NCORES)], axis=0)[None]
    nps = np.concatenate([R[i]["nps"] for i in range(NCORES)], axis=0)[None]
    return tuple(np.ascontiguousarray(a.astype(np.float32)) for a in (y_prompt, y_sample, ncp, nlp, npp, ncs, nls, nps))
```

```python
import numpy as np
from contextlib import ExitStack

import concourse.bass as bass
import concourse.mybir as mybir
from concourse.bass_utils import run_bass_kernel_spmd

F32 = mybir.dt.float32
BF16 = mybir.dt.bfloat16
AF = mybir.ActivationFunctionType
ALU = mybir.AluOpType

NCORES = 8
EPS = 1e-6
D = 1024
IN_COLS = 5632
NM = IN_COLS // 128


class Buf:
    __slots__ = ("name", "last_w", "readers", "sem", "cnt")

    def __init__(self, name):
        self.name = name
        self.last_w = None
        self.readers = []
        self.sem = None
        self.cnt = 0


class Op:
    __slots__ = ("eng", "fn", "deps", "needed", "count", "sem", "waits", "knows", "is_dma", "lane", "dbg", "wdbg")

    def __init__(self, eng, fn, is_dma=False, lane=None):
        self.eng = eng
        self.fn = fn
        self.deps = []
        self.needed = is_dma
        self.count = 0
        self.sem = None
        self.waits = []
        self.knows = None
        self.is_dma = is_dma
        self.lane = lane


class Prog:
    ENGS = ("pe", "act", "dve", "pool", "sp")

    def __init__(self, nc, es):
        self.nc = nc
        self.es = es
        self.ops = []
        self.lanes = []

    def buf(self, name):
        return Buf(name)

    def _add(self, o, reads, writes):
        o.dbg = ([b.name for b in reads], [b.name for b in writes])
        o.wdbg = []
        deps = {}
        for b in reads:
            if b.last_w is not None:
                deps[id(b.last_w)] = (b.last_w, "raw")
        for b in writes:
            if b.last_w is not None and id(b.last_w) not in deps:
                deps[id(b.last_w)] = (b.last_w, "waw")
            for r in b.readers:
                if id(r) not in deps:
                    deps[id(r)] = (r, "war")
        for p, kind in deps.values():
            if p is o:
                continue
            if (not p.is_dma) and (not o.is_dma) and p.eng == o.eng:
                if o.eng == "pe":
                    continue
            if p.is_dma and o.is_dma and p.lane is o.lane and kind == "waw":
                continue
            o.deps.append(p)
            p.needed = True
        for b in reads:
            if b in writes:
                continue
            if o.is_dma:
                b.readers = [r for r in b.readers if not (r.is_dma and r.lane is o.lane)]
            else:
                b.readers = [r for r in b.readers if r.is_dma or r.eng != o.eng]
            b.readers.append(o)
        for b in writes:
            b.last_w = o
            b.readers = []
        self.ops.append(o)
        return o

    def op(self, eng, fn, reads=(), writes=()):
        return self._add(Op(eng, fn), list(reads), list(writes))

    def dma(self, fn, lane, reads=(), writes=(), queue="sp"):
        if not any(l is lane for l in self.lanes):
            self.lanes.append(lane)
        o = Op(queue, fn, is_dma=True, lane=lane)
        return self._add(o, list(reads), list(writes))

    def finalize(self):
        nc, es = self.nc, self.es
        engsem = {e: es.enter_context(nc.semaphore("sem_" + e)) for e in ("pe", "act", "dve", "pool")}
        for i, ln in enumerate(self.lanes):
            ln.sem = es.enter_context(nc.semaphore("lane%d" % i))
        engcnt = {e: 0 for e in engsem}
        known = {e: {} for e in self.ENGS}
        per = {e: [] for e in self.ENGS}
        nwaits = 0
        for o in self.ops:
            K = known[o.eng]
            waits = {}
            for p in o.deps:
                sid = id(p.sem)
                if K.get(sid, 0) >= p.count:
                    continue
                if sid not in waits or waits[sid][1] < p.count:
                    waits[sid] = (p.sem, p.count, p)
            for sid, (sem, val, p) in waits.items():
                if p.knows:
                    for k2, v2 in p.knows.items():
                        if K.get(k2, 0) < v2:
                            K[k2] = v2
                if K.get(sid, 0) < val:
                    K[sid] = val
            o.waits = [(sem, val) for sid, (sem, val, p) in waits.items()]
            o.wdbg = [(p.eng, p.dbg) for sid, (sem, val, p) in waits.items()]
            nwaits += len(o.waits)
            if o.is_dma:
                o.lane.cnt += 16
                o.count = o.lane.cnt
                o.sem = o.lane.sem
                o.knows = dict(K)
            elif o.needed:
                engcnt[o.eng] += 1
                o.count = engcnt[o.eng]
                o.sem = engsem[o.eng]
                o.knows = dict(K)
            per[o.eng].append(o)
        self.per = per
        self.stats = {e: len(per[e]) for e in per}
        self.stats["waits"] = nwaits

        def emit(eng_handle, ops, final=False):
            for o in ops:
                for sem, val in o.waits:
                    eng_handle.wait_ge(sem, val)
                ins = o.fn(eng_handle)
                if o.is_dma:
                    ins.then_inc(o.sem, 16)
                elif o.needed:
                    ins.then_inc(o.sem, 1)
            if final:
                for ln in self.lanes:
                    if ln.cnt:
                        eng_handle.wait_ge(ln.sem, ln.cnt)
                for e, s in engsem.items():
                    if engcnt[e]:
                        eng_handle.wait_ge(s, engcnt[e])

        block = es.enter_context(nc.Block())

        @block.sync
        def _(e):
            emit(e, per["sp"], final=True)

        @block.tensor
        def _(e):
            emit(e, per["pe"])

        @block.scalar
        def _(e):
            emit(e, per["act"])

        @block.vector
        def _(e):
            emit(e, per["dve"])

        @block.gpsimd
        def _(e):
            emit(e, per["pool"])


class T:
    def __init__(self, ap, buf):
        self.ap = ap
        self.buf = buf


G1, G2, CW, CB, BA, BI, LAM, BP, PSC = 0, 8, 16, 48, 56, 64, 72, 80, 88
HBA, HBI, CL, HCL, BPS = 0, 8, 16, 24, 32

B_TASKS = [("L1", 0)]
for _c in range(1, 8):
    B_TASKS.append(("L1", _c))
    B_TASKS.append(("L2a", _c - 1))
    if _c % 2 == 0:
        B_TASKS += [("L2b", _c - 2), ("L2b", _c - 1)]
B_TASKS += [("Q1", 0), ("L2a", 7), ("L2b", 6), ("L2b", 7), ("Q1", 1), ("Q2", 0), ("Q1", 2), ("Q2", 1), ("Q1", 3),
            ("Q2", 2), ("Q2", 3)]
W_ORDER = []
for _t, _a in B_TASKS:
    if _t == "L1":
        W_ORDER += [_a, 8 + _a, 28 + _a]
    elif _t == "Q1":
        W_ORDER += [16 + _a]
    elif _t == "Q2":
        W_ORDER += [20 + 2 * _a, 36 + 2 * _a, 21 + 2 * _a, 37 + 2 * _a]
assert sorted(W_ORDER) == list(range(NM))
PAIR_ORDER = []
for _m in W_ORDER:
    if _m // 2 not in PAIR_ORDER:
        PAIR_ORDER.append(_m // 2)


def build_program(stop=None):
    nc = bass.Bass("TRN2", target_bir_lowering=False)
    es = ExitStack()
    P = Prog(nc, es)

    def din(name, shape, dt=F32):
        return nc.dram_tensor(name, shape, dt, kind="ExternalInput").ap()

    def dout(name, shape):
        return nc.dram_tensor(name, shape, F32, kind="ExternalOutput").ap()

    d_xp = din("xp", [4096, 1024])
    d_pp = din("pp", [4096, 256])
    d_xs = din("xs", [128, 1024])
    d_ps = din("ps", [128, 256])
    d_sconv = din("sconv", [12, 1024])
    d_slru = din("slru", [4, 1024])
    d_spool = din("spool", [60, 512])
    d_norm_mix = din("norm_mix", [1, 1024])
    d_w_in = din("w_in", [1, 1024, IN_COLS])
    d_conv_w = din("conv_w", [1, 4, 1024])
    d_conv_b = din("conv_b", [1, 1024])
    d_w_rg_a = din("w_rg_a", [1, 16, 64, 64])
    d_b_rg_a = din("b_rg_a", [1, 1024])
    d_w_rg_i = din("w_rg_i", [1, 16, 64, 64])
    d_b_rg_i = din("b_rg_i", [1, 1024])
    d_lam = din("lru_lambda", [1, 1024])
    d_w_pool = din("w_pool", [1, 4, 128, 256])
    d_b_pool = din("b_pool", [1, 4, 256])
    d_pool_scale = din("pool_scale", [1, 1024])
    d_w_out = din("w_out", [1, 1024, 1024])
    d_norm_ple = din("norm_ple", [1, 1024])
    d_w_pg = din("w_ple_gate", [1, 1024, 1024])
    d_w_ple = din("w_ple", [1, 256, 1024])
    d_final_norm = din("final_norm", [1024])

    o_yp = dout("yp", [4096, 1024])
    o_ys = dout("ys", [128, 1024])
    o_ncp = dout("ncp", [2, 3, 1024])
    o_nlp = dout("nlp", [2, 1024])
    o_npp = dout("npp", [2, 15, 512])
    o_ncs = dout("ncs", [4, 3, 1024])
    o_nls = dout("nls", [4, 1024])
    o_nps = dout("nps", [4, 15, 512])

    d_wsc = nc.dram_tensor("win_bf16", [NM, 128, 1024], BF16, kind="Internal").ap()
    wsc_buf = [P.buf("wsc%d" % m) for m in range(NM)]

    def sb(name, shape, dt=F32):
        t = es.enter_context(nc.sbuf_tensor(name, shape, dt))
        return T(t[:], P.buf(name))

    WOUT = sb("wout", [128, 8 * 1024], BF16)
    WPG = sb("wpg", [128, 8 * 1024], BF16)
    WPE = sb("wpe", [128, 2 * 1024], BF16)
    WA = sb("wa", [128, 8 * 128], BF16)
    WI = sb("wi", [128, 8 * 128], BF16)
    WPOOL = sb("wpool", [128, 4 * 256], BF16)
    G3 = sb("g3bc", [128, 1024], F32)
    IDF = sb("identf", [128, 128], F32)
    IDB = sb("identb", [128, 128], BF16)
    CV = sb("cvec", [128, 96], F32)
    CV2 = sb("cvec2", [128, 40], F32)
    INVC = sb("invc", [128, 4 * 16], F32)
    NEGH = sb("negh", [128, 1], F32)
    SC16 = sb("sc16", [128, 8 * 16], F32)
    SP60 = sb("sp60", [128, 4 * 60], F32)
    CONVH = [sb("convh%d" % c, [128, 3], F32) for c in range(8)]
    HST = [sb("hst%d" % c, [128, 1], F32) for c in range(8)]
    POOLH = [sb("poolh%d" % g, [128, 15], F32) for g in range(4)]
    ST = sb("stg", [128, 4 * 92], F32)
    STO = sb("sto", [128, 4 * 128], F32)

    wout_v = WOUT.ap.rearrange("p (k n) -> p k n", k=8)
    wpg_v = WPG.ap.rearrange("p (k n) -> p k n", k=8)
    wpe_v = WPE.ap.rearrange("p (k n) -> p k n", k=2)
    wa_v = WA.ap.rearrange("p (k n) -> p k n", k=8)
    wi_v = WI.ap.rearrange("p (k n) -> p k n", k=8)
    wpool_v = WPOOL.ap.rearrange("p (g n) -> p g n", g=4)
    invc_v = INVC.ap.rearrange("p (g n) -> p g n", g=4)
    sc16_v = SC16.ap.rearrange("p (c n) -> p c n", c=8)
    sp60_v = SP60.ap.rearrange("p (g n) -> p g n", g=4)
    st_v = ST.ap.rearrange("p (s n) -> p s n", s=4)

    XS = [sb("xs%d" % i, [128, 1024], F32) for i in range(3)]
    XRT = sb("xrt", [128, 4 * 1024], F32)
    XR = [T(XRT.ap[:, i * 1024:(i + 1) * 1024], P.buf("xr%d" % i)) for i in range(3)]
    UB = [sb("ub%d" % i, [128, 1024], BF16) for i in range(2)]
    U2 = [sb("u2_%d" % i, [128, 1024], BF16) for i in range(2)]
    JUNK = sb("junk", [128, 1024], BF16)
    UT = [sb("ut%d" % i, [128, 8 * 512], BF16) for i in range(2)]
    MG = [sb("mg%d" % i, [128, 8 * 512], BF16) for i in range(2)]
    RING_N = 12
    RING = [sb("wr%d" % i, [128, 1024], BF16) for i in range(8)]
    EXTRA = sb("extra", [128, 4 * 1024], BF16)
    RING += [T(EXTRA.ap[:, i * 1024:(i + 1) * 1024], P.buf("wrx%d" % i)) for i in range(4)]
    NWORK = 18
    WORK = [sb("wk%d" % i, [128, 512], F32) for i in range(NWORK)]
    XLW = [sb("xlw%d" % i, [128, 515], F32) for i in range(2)]
    XPW = [sb("xpw%d" % i, [128, 527], F32) for i in range(2)]
    XLWH = [P.buf("xlwh%d" % i) for i in range(2)]
    XPWH = [P.buf("xpwh%d" % i) for i in range(2)]
    SA = sb("sa", [128, 527], F32)
    SBB = sb("sbb", [128, 527], F32)
    XCB = [sb("xcb%d" % i, [128, 512], BF16) for i in range(2)]
    PL = [sb("pl%d" % i, [128, 512], BF16) for i in range(2)]
    T16 = sb("t16", [128, 16], F32)
    STAT = [sb("stat%d" % i, [128, 4], F32) for i in range(8)]
    U2T = [sb("u2t%d" % i, [128, 8 * 128], BF16) for i in range(2)]
    P32 = [sb("p32_%d" % i, [128, 256], F32) for i in range(2)]
    PB = [sb("pb%d" % i, [128, 256], BF16) for i in range(2)]
    PT = [sb("pt%d" % i, [128, 256], BF16) for i in range(2)]
    TGC = [T(XRT.ap[:, 3072:4096], P.buf("tgc0"))]

    ps_t = es.enter_context(nc.psum_tensor("psum", [128, 8 * 512], F32))
    NFB = 6
    NTB = 8 - NFB
    FB = [T(ps_t[:, i * 512:(i + 1) * 512], P.buf("fb%d" % i)) for i in range(NFB)]
    TB = [T(ps_t[:, i * 512:(i + 1) * 512], P.buf("tb%d" % i)) for i in range(NFB, 8)]
    ctr = {"fb": 0, "tb": 0, "wk": 0, "stat": 0}

    fb_free = list(range(NFB))

    def newbank():
        assert fb_free, "out of PSUM banks"
        return FB[fb_free.pop(0)]

    def relbank(b):
        i = [j for j in range(NFB) if FB[j] is b][0]
        assert i not in fb_free
        fb_free.append(i)

    def newtbank():
        b = TB[ctr["tb"] % len(TB)]
        ctr["tb"] += 1
        return b

    wk_free = list(range(NWORK))

    def newwork():
        assert wk_free, "out of work tiles"
        return WORK[wk_free.pop(0)]

    def relwork(*ts):
        for t_ in ts:
            i = [j for j in range(NWORK) if WORK[j] is t_][0]
            assert i not in wk_free
            wk_free.append(i)

    def newstat():
        b = STAT[ctr["stat"] % len(STAT)]
        ctr["stat"] += 1
        return b

    rr = {"i": 0}

    def anyeng():
        e = "dve"
        rr["i"] += 1
        return e

    P.op("pool", lambda e: e.memset(NEGH.ap, -0.5), writes=[NEGH.buf])
    P.op("pool", lambda e: e.memset(IDF.ap, 1.0), writes=[IDF.buf])
    P.op("pool", lambda e: e.affine_select(out=IDF.ap, in_=IDF.ap, pattern=[[-1, 128]],
                                           compare_op=ALU.is_equal, fill=0.0, base=0,
                                           channel_multiplier=1), reads=[IDF.buf], writes=[IDF.buf])
    P.op("dve", lambda e: e.tensor_copy(out=IDB.ap, in_=IDF.ap), reads=[IDF.buf], writes=[IDB.buf])

    vs = XS[0]
    vec_rows = [
        (d_norm_mix[0], G1, 8), (d_norm_ple[0], G2, 8), (d_conv_b[0], CB, 8),
        (d_b_rg_a[0], BA, 8), (d_b_rg_i[0], BI, 8), (d_lam[0], LAM, 8), (d_pool_scale[0], PSC, 8),
    ]
    for src, r0, n in vec_rows:
        P.dma(lambda e, src=src, r0=r0, n=n: e.dma_start(out=vs.ap[r0:r0 + n, 0:128],
                                                         in_=src.rearrange("(a b) -> a b", b=128)),
              lane=vs.buf, writes=[vs.buf])
    P.dma(lambda e: e.dma_start(out=vs.ap[CW:CW + 32, 0:128],
                                in_=d_conv_w[0].rearrange("k (a b) -> (k a) b", b=128)),
          lane=vs.buf, writes=[vs.buf])
    P.dma(lambda e: e.dma_start(out=vs.ap[BP:BP + 8, 0:128],
                                in_=d_b_pool[0].rearrange("g (a b) -> (g a) b", b=128)),
          lane=vs.buf, writes=[vs.buf])
    bk = newbank()
    P.op("pe", lambda e: e.transpose(out=bk.ap[:, 0:96], in_=vs.ap[0:96, 0:128], identity=IDF.ap[0:96, 0:96]),
         reads=[vs.buf, IDF.buf], writes=[bk.buf])
    P.op("dve", lambda e: e.tensor_copy(out=CV.ap, in_=bk.ap[:, 0:96]), reads=[bk.buf], writes=[CV.buf])
    relbank(bk)
    P.op("dve", lambda e: e.tensor_scalar(out=CV2.ap[:, HBA:HBA + 16], in0=CV.ap[:, BA:BA + 16], scalar1=0.5,
                                          scalar2=None, op0=ALU.mult), reads=[CV.buf], writes=[CV2.buf])
    P.op("dve", lambda e: e.tensor_tensor(out=CV2.ap[:, BPS:BPS + 8], in0=CV.ap[:, BP:BP + 8],
                                          in1=CV.ap[:, PSC:PSC + 8], op=ALU.mult), reads=[CV.buf], writes=[CV2.buf])
    P.op("act", lambda e: e.activation(out=CV2.ap[:, CL:CL + 8], in_=CV.ap[:, LAM:LAM + 8], func=AF.Exp, scale=-1.0),
         reads=[CV.buf], writes=[CV2.buf])
    P.op("act", lambda e: e.activation(out=CV2.ap[:, CL:CL + 8], in_=CV2.ap[:, CL:CL + 8], func=AF.Ln, bias=1.0),
         reads=[CV2.buf], writes=[CV2.buf])
    P.op("dve", lambda e: e.tensor_scalar(out=CV2.ap[:, HCL:HCL + 8], in0=CV2.ap[:, CL:CL + 8], scalar1=-4.0,
                                          scalar2=None, op0=ALU.mult), reads=[CV2.buf], writes=[CV2.buf])
    P.op("dve", lambda e: e.tensor_scalar(out=CV2.ap[:, CL:CL + 8], in0=CV2.ap[:, CL:CL + 8], scalar1=-8.0,
                                          scalar2=None, op0=ALU.mult), reads=[CV2.buf], writes=[CV2.buf])
    P.op("pool", lambda e: e.iota(INVC.ap, pattern=[[0, 4], [1, 16]], base=1, channel_multiplier=0,
                                  allow_small_or_imprecise_dtypes=True), writes=[INVC.buf])
    for g in range(4):
        P.op("dve", lambda e, g=g: e.tensor_scalar(out=invc_v[:, g, :], in0=invc_v[:, g, :], scalar1=float(2 ** (g + 1)),
                                                   scalar2=None, op0=ALU.min), reads=[INVC.buf], writes=[INVC.buf])
    P.op("dve", lambda e: e.reciprocal(out=INVC.ap, in_=INVC.ap), reads=[INVC.buf], writes=[INVC.buf])

    P.op("dve", lambda e: e.tensor_scalar(out=CV.ap[:, G1:G1 + 8], in0=CV.ap[:, G1:G1 + 8], scalar1=32.0, scalar2=None,
                                          op0=ALU.mult), reads=[CV.buf], writes=[CV.buf])
    P.dma(lambda e: e.dma_start(out=G3.ap, in_=d_final_norm.partition_broadcast(128)), lane=G3.buf, writes=[G3.buf])
    P.op("dve", lambda e: e.tensor_scalar(out=G3.ap, in0=G3.ap, scalar1=32.0, scalar2=None, op0=ALU.mult),
         reads=[G3.buf], writes=[G3.buf])

    stg = [XS[1], XS[2], XR[0], XR[1], XR[2], XS[0]]
    sc = {"i": 0}

    def nextstg():
        s = stg[sc["i"] % len(stg)]
        sc["i"] += 1
        return s

    s1 = nextstg()
    P.dma(lambda e: e.dma_start(out=s1.ap[0:12, :], in_=d_sconv), lane=s1.buf, writes=[s1.buf])
    P.dma(lambda e: e.dma_start(out=s1.ap[12:16, :], in_=d_slru), lane=s1.buf, writes=[s1.buf])
    bk1 = newbank()
    for c in range(8):
        P.op("pe", lambda e, c=c: e.transpose(out=bk1.ap[:, c * 16:(c + 1) * 16], in_=s1.ap[0:16, c * 128:(c + 1) * 128],
                                              identity=IDF.ap[0:16, 0:16]), reads=[s1.buf, IDF.buf], writes=[bk1.buf])
    P.op("dve", lambda e: e.tensor_copy(out=SC16.ap, in_=bk1.ap[:, 0:128]), reads=[bk1.buf], writes=[SC16.buf])
    relbank(bk1)
    s2 = nextstg()
    P.dma(lambda e: e.dma_start(out=s2.ap[0:60, 0:512], in_=d_spool), lane=s2.buf, writes=[s2.buf])
    bk2 = newbank()
    for g in range(4):
        P.op("pe", lambda e, g=g: e.transpose(out=bk2.ap[:, g * 60:(g + 1) * 60], in_=s2.ap[0:60, g * 128:(g + 1) * 128],
                                              identity=IDF.ap[0:60, 0:60]), reads=[s2.buf, IDF.buf], writes=[bk2.buf])
    P.op("dve", lambda e: e.tensor_copy(out=SP60.ap, in_=bk2.ap[:, 0:240]), reads=[bk2.buf], writes=[SP60.buf])
    relbank(bk2)

    for dsrc, dst in ((d_w_rg_a, WA), (d_w_rg_i, WI)):
        s = nextstg()
        sv = s.ap.rearrange("p (c n) -> p c n", c=8)
        q = dsrc[0].rearrange("(c two) i j -> two i c j", two=2)
        P.op("pool", lambda e, s=s: e.memset(s.ap, 0.0), writes=[s.buf])
        P.dma(lambda e, sv=sv, q=q: e.dma_start(out=sv[0:64, :, 0:64], in_=q[0]), lane=s.buf, writes=[s.buf])
        P.dma(lambda e, sv=sv, q=q: e.dma_start(out=sv[64:128, :, 64:128], in_=q[1]), lane=s.buf, writes=[s.buf])
        P.op("dve", lambda e, s=s, dst=dst: e.tensor_copy(out=dst.ap, in_=s.ap), reads=[s.buf], writes=[dst.buf])
    sa_ = nextstg()
    sb_ = nextstg()
    P.dma(lambda e: e.dma_start(out=sa_.ap.rearrange("p (g j) -> p g j", g=4),
                                in_=d_w_pool[0].rearrange("g p j -> p g j")), lane=sa_.buf, writes=[sa_.buf])
    P.dma(lambda e: e.dma_start(out=sb_.ap, in_=d_pool_scale[0].partition_broadcast(128)), lane=sb_.buf, writes=[sb_.buf])
    P.op("dve", lambda e: e.tensor_tensor(out=WPOOL.ap, in0=sa_.ap, in1=sb_.ap, op=ALU.mult),
         reads=[sa_.buf, sb_.buf], writes=[WPOOL.buf])
    deferred_w = []
    xr_i = {"i": 0}

    def w_unit(dsrc, kc, dst_v, dstbuf, scal, post=1.0):
        def emit():
            s_ = XR[xr_i["i"] % 3]
            xr_i["i"] += 1
            P.dma(lambda e: e.dma_start(out=s_.ap, in_=dsrc[0][kc * 128:(kc + 1) * 128, :]), lane=s_.buf,
                  writes=[s_.buf])
            rd = [s_.buf] + ([CV.buf] if not isinstance(scal, float) else [])
            P.op(anyeng(), lambda e: e.tensor_scalar(out=dst_v[:, kc, :], in0=s_.ap, scalar1=scal, scalar2=post,
                                                     op0=ALU.mult, op1=ALU.mult), reads=rd, writes=[dstbuf])
        return emit

    for kc in range(8):
        deferred_w.append(w_unit(d_w_out, kc, wout_v, WOUT.buf, 0.25))
    for kc in range(8):
        deferred_w.append(w_unit(d_w_pg, kc, wpg_v, WPG.buf, CV.ap[:, G2 + kc:G2 + kc + 1], 16.0))
    for kc in range(2):
        deferred_w.append(w_unit(d_w_ple, kc, wpe_v, WPE.buf, 0.5))

    g1b = CV.ap[:, G1:G1 + 8].unsqueeze(2).to_broadcast([128, 8, 128])
    STGC = [(T(XR[i].ap.rearrange("p (k n) -> p k n", k=8), XR[i].buf), []) for i in range(3)]
    STGC.append((T(TGC[0].ap.rearrange("p (k n) -> p k n", k=8), TGC[0].buf), []))
    mg1f = MG[1].ap.bitcast(F32)
    for h in range(2):
        STGC.append((T(mg1f[:, h * 1024:(h + 1) * 1024].rearrange("p (k n) -> p k n", k=8), P.buf("mg1st%d" % h)),
                     [MG[1].buf]))
    cu = {"loaded": 0, "done": 0}
    chunks_done = set()

    def chunk_load_next():
        i = cu["loaded"]
        m = W_ORDER[i]
        hb, own = STGC[i % len(STGC)]
        P.dma(lambda e: e.dma_start(out=hb.ap, in_=d_w_in[0][:, m * 128:(m + 1) * 128].rearrange(
            "(k p) n -> p k n", p=128)), lane=hb.buf, writes=[hb.buf])
        cu["loaded"] += 1

    def chunk_compute_next():
        i = cu["done"]
        m = W_ORDER[i]
        while cu["loaded"] <= i:
            chunk_load_next()
        hb, own = STGC[i % len(STGC)]
        slot = RING[i % RING_N]
        P.op("dve", lambda e: e.tensor_tensor(out=slot.ap.rearrange("p (k n) -> p k n", k=8), in0=hb.ap, in1=g1b,
                                              op=ALU.mult), reads=[hb.buf, CV.buf] + own, writes=[slot.buf])
        P.dma(lambda e: e.dma_start(out=d_wsc[m], in_=slot.ap), lane=slot.buf, reads=[slot.buf], writes=[wsc_buf[m]])
        cu["done"] += 1
        chunks_done.add(m)
        if cu["loaded"] < NM:
            chunk_load_next()

    def ensure_chunk(m):
        while m not in chunks_done:
            chunk_compute_next()

    tiles = []
    for seq in range(2):
        for st in range(4):
            tiles.append(dict(kind="p", seq=seq, st=st, nseg=1, L=512, N=512, nt=4, tok0=seq * 2048 + st * 512,
                              first=(st == 0), last=(st == 3), idx=len(tiles)))
    tiles.append(dict(kind="s", seq=0, st=0, nseg=4, L=32, N=128, nt=1, tok0=0, first=False, last=True,
                      idx=len(tiles)))
    NST = len(tiles)

    stream = {"q_loaded": 0, "q_used": 0}
    total_q = NM * NST

    def ring_load(q):
        m = W_ORDER[q % NM]
        if q < NM:
            assert cu["done"] == q
            chunk_compute_next()
            return
        slot = RING[q % RING_N]
        P.dma(lambda e, slot=slot, m=m: e.dma_start(out=slot.ap, in_=d_wsc[m]), lane=slot.buf,
              reads=[wsc_buf[m]], writes=[slot.buf])

    def win_get(m):
        q = stream["q_used"]
        assert W_ORDER[q % NM] == m, (q, m)
        while stream["q_loaded"] < min(total_q, q + RING_N):
            ring_load(stream["q_loaded"])
            stream["q_loaded"] += 1
        stream["q_used"] += 1
        return RING[q % RING_N]

    a_tiles = []
    for S in tiles:
        for j in range(S["nt"]):
            a_tiles.append((S, j))
    NAT = len(a_tiles)

    def x_src(S, j):
        if S["kind"] == "p":
            r0 = S["tok0"] + j * 128
            return d_xp[r0:r0 + 128, :]
        return d_xs[:, :]

    def p_src(S, j):
        if S["kind"] == "p":
            r0 = S["tok0"] + j * 128
            return d_pp[r0:r0 + 128, :]
        return d_ps[:, :]

    def y_dst(S, j):
        if S["kind"] == "p":
            r0 = S["tok0"] + j * 128
            return o_yp[r0:r0 + 128, :]
        return o_ys[:, :]

    aload = {"n": 0}

    def a_ensure(i):
        while aload["n"] <= min(i, NAT - 1):
            k = aload["n"]
            S, j = a_tiles[k]
            slot = XS[k % 3]
            P.dma(lambda e, slot=slot, src=x_src(S, j): e.dma_start(out=slot.ap, in_=src), lane=slot.buf,
                  writes=[slot.buf])
            aload["n"] += 1

    cload = {"n": 0}

    def c_ensure(i):
        while cload["n"] <= min(i, NAT - 1):
            k = cload["n"]
            S, j = a_tiles[k]
            slot = XR[k % 3]
            P.dma(lambda e, slot=slot, src=x_src(S, j): e.dma_start(out=slot.ap, in_=src), lane=slot.buf,
                  writes=[slot.buf])
            ps_ = P32[k % 2]
            P.dma(lambda e, ps_=ps_, src=p_src(S, j): e.dma_start(out=ps_.ap, in_=src), lane=ps_.buf,
                  writes=[ps_.buf])
            cload["n"] += 1

    def rstd_ops(st):
        P.op("dve", lambda e: e.tensor_scalar(out=st.ap[:, 2:3], in0=st.ap[:, 0:1], scalar1=float(D) * EPS,
                                              scalar2=None, op0=ALU.add), reads=[st.buf], writes=[st.buf])
        P.op("pool", lambda e: e.tensor_tensor(out=st.ap[:, 1:2], in0=st.ap[:, 2:3], in1=NEGH.ap, op=ALU.pow),
             reads=[st.buf, NEGH.buf], writes=[st.buf])

    a_stat = {}

    def phase_a1(k):
        S, j = a_tiles[k]
        a_ensure(k + 2)
        x = XS[k % 3]
        st = newstat()
        u = UB[k % 2]
        P.op("act", lambda e: e.activation(out=JUNK.ap, in_=x.ap, func=AF.Square, accum_out=st.ap[:, 0:1]),
             reads=[x.buf], writes=[st.buf, JUNK.buf])
        a_stat[k] = st

    def phase_a1r(k):
        rstd_ops(a_stat[k])

    def phase_a1b(k):
        x = XS[k % 3]
        u = UB[k % 2]
        st = a_stat.pop(k)
        P.op("act", lambda e: e.activation(out=u.ap, in_=x.ap, func=AF.Identity, scale=st.ap[:, 1:2]),
             reads=[x.buf, st.buf], writes=[u.buf])

    def phase_a2(k):
        S, j = a_tiles[k]
        u = UB[k % 2]
        utb = UT[S["idx"] % 2]
        N = S["N"]
        ut_v = utb.ap[:, 0:8 * N].rearrange("p (k n) -> p k n", k=8)
        tb = newtbank()
        tbv = tb.ap.bitcast(BF16)
        for c in range(8):
            P.op("pe", lambda e, c=c: e.transpose(out=tbv[:, c * 128:(c + 1) * 128], in_=u.ap[:, c * 128:(c + 1) * 128],
                                                  identity=IDB.ap), reads=[u.buf, IDB.buf], writes=[tb.buf])
        P.op("act", lambda e: e.activation(out=ut_v[:, :, j * 128:(j + 1) * 128],
                                           in_=tbv.rearrange("p (k n) -> p k n", k=8), func=AF.Copy),
             reads=[tb.buf], writes=[utb.buf])

    stepc = {"n": 0}

    def seg3(ap2, nseg):
        return ap2.rearrange("p (s l) -> p s l", s=nseg)

    def mm_group(bank, N, wslot, utb, ut_v):
        wv = wslot.ap.rearrange("p (k n) -> p k n", k=8)
        for kc in range(8):
            P.op("pe", lambda e, kc=kc: e.matmul(bank.ap[:, 0:N], lhsT=wv[:, kc, :], rhs=ut_v[:, kc, :],
                                                 start=(kc == 0), stop=(kc == 7)),
                 reads=[wslot.buf, utb.buf], writes=[bank.buf])

    carry = {}
    deferred_ops = []
    deferred_ops2 = []

    def flush_deferred():
        while deferred_ops:
            deferred_ops.pop(0)()
        while deferred_ops2:
            deferred_ops.append(deferred_ops2.pop(0))

    def views(S):
        N = S["N"]
        utb = UT[S["idx"] % 2]
        ut_v = utb.ap[:, 0:8 * N].rearrange("p (k n) -> p k n", k=8)
        mgb = MG[S["idx"] % 2]
        mg_v = mgb.ap[:, 0:8 * N].rearrange("p (k n) -> p k n", k=8)
        return utb, ut_v, mgb, mg_v

    def lru_p1(S, c):
        nseg, L, N = S["nseg"], S["L"], S["N"]
        par = c % 2
        utb, ut_v, mgb, mg_v = views(S)
        sample = S["kind"] == "s"
        b_xl, b_gl, b_ml = newbank(), newbank(), newbank()
        for bank, m in ((b_xl, c), (b_gl, 8 + c), (b_ml, 28 + c)):
            mm_group(bank, N, win_get(m), utb, ut_v)
        xlw = XLW[par]
        xlw_v = xlw.ap[:, 0:nseg * (3 + L)].rearrange("p (s w) -> p s w", s=nseg)
        if S["first"]:
            P.op("pool", lambda e: e.memset(xlw_v[:, :, 0:3], 0.0), writes=[XLWH[par]])
        elif sample:
            P.op("pool", lambda e: e.tensor_copy(out=xlw_v[:, :, 0:3],
                                                 in_=sc16_v[:, c, 0:12].rearrange("p (s r) -> p s r", r=3)),
                 reads=[SC16.buf], writes=[XLWH[par]])
        else:
            P.op("pool", lambda e: e.tensor_copy(out=xlw_v[:, :, 0:3], in_=CONVH[c].ap.unsqueeze(1)),
                 reads=[CONVH[c].buf], writes=[XLWH[par]])
        P.op("act", lambda e: e.activation(out=xlw_v[:, :, 3:3 + L], in_=seg3(b_xl.ap[:, 0:N], nseg), func=AF.Copy),
             reads=[b_xl.buf], writes=[xlw.buf])
        acc = newwork()
        acc2 = acc.ap[:, 0:N]
        acc_v = seg3(acc2, nseg)
        P.op("act", lambda e: e.activation(out=acc2, in_=b_xl.ap[:, 0:N], func=AF.Identity,
                                           scale=CV.ap[:, CW + 24 + c:CW + 24 + c + 1], bias=CV.ap[:, CB + c:CB + c + 1]),
             reads=[b_xl.buf, CV.buf], writes=[acc.buf])
        relbank(b_xl)
        tg, tm = newwork(), newwork()
        P.op("act", lambda e: e.activation(out=tg.ap[:, 0:N], in_=b_gl.ap[:, 0:N], func=AF.Tanh, scale=0.5),
             reads=[b_gl.buf], writes=[tg.buf])
        P.op("act", lambda e: e.activation(out=tm.ap[:, 0:N], in_=b_ml.ap[:, 0:N], func=AF.Tanh, scale=0.5),
             reads=[b_ml.buf], writes=[tm.buf])
        relbank(b_ml)
        if not S["last"]:
            P.op("pool", lambda e: e.tensor_copy(out=CONVH[c].ap.unsqueeze(1), in_=xlw_v[:, :, L:L + 3]),
                 reads=[xlw.buf], writes=[CONVH[c].buf])
        else:
            P.op("pool", lambda e: e.tensor_copy(
                out=st_v[:, 0:nseg, 0:24].rearrange("p s (r c) -> p s r c", c=8)[:, :, :, c],
                in_=xlw_v[:, :, L:L + 3]), reads=[xlw.buf], writes=[ST.buf])
        P.op("dve", lambda e: e.scalar_tensor_tensor(out=tg.ap[:, 0:N], in0=tg.ap[:, 0:N], scalar=1.0,
                                                     in1=b_gl.ap[:, 0:N], op0=ALU.add, op1=ALU.mult),
             reads=[tg.buf, b_gl.buf], writes=[tg.buf])
        relbank(b_gl)
        xcb = XCB[par]

        def later():
            for j in (1, 2, 3):
                k = 3 - j
                P.op("dve", lambda e, j=j, k=k: e.scalar_tensor_tensor(
                    out=acc_v, in0=xlw_v[:, :, 3 - j:3 - j + L], scalar=CV.ap[:, CW + k * 8 + c:CW + k * 8 + c + 1],
                    in1=acc_v, op0=ALU.mult, op1=ALU.add), reads=[xlw.buf, XLWH[par], acc.buf, CV.buf],
                    writes=[acc.buf])
            P.op("dve", lambda e: e.tensor_copy(out=xcb.ap[:, 0:N], in_=acc2), reads=[acc.buf], writes=[xcb.buf])
            P.op("dve", lambda e: e.scalar_tensor_tensor(out=tm.ap[:, 0:N], in0=tm.ap[:, 0:N], scalar=1.0,
                                                         in1=tg.ap[:, 0:N], op0=ALU.add, op1=ALU.mult),
                 reads=[tm.buf, tg.buf], writes=[tm.buf])
            relwork(tg)
        deferred_ops.append(later)
        carry[("L", c)] = (acc, tm, xcb)

    def lru_p2(S, c):
        nseg, L, N = S["nseg"], S["L"], S["N"]
        utb, ut_v, mgb, mg_v = views(S)
        sample = S["kind"] == "s"
        acc, tm, xcb = carry.pop(("L", c))
        acc2 = acc.ap[:, 0:N]
        b_r, b_i = newbank(), newbank()
        P.op("pe", lambda e: e.matmul(b_r.ap[:, 0:N], lhsT=wa_v[:, c, :], rhs=xcb.ap[:, 0:N], start=True, stop=True),
             reads=[xcb.buf, WA.buf], writes=[b_r.buf])
        P.op("pe", lambda e: e.matmul(b_i.ap[:, 0:N], lhsT=wi_v[:, c, :], rhs=xcb.ap[:, 0:N], start=True, stop=True),
             reads=[xcb.buf, WI.buf], writes=[b_i.buf])
        tr, ti, aa, a2 = (newwork() for _ in range(4))
        P.op("act", lambda e: e.activation(out=tr.ap[:, 0:N], in_=b_r.ap[:, 0:N], func=AF.Tanh, scale=0.5,
                                           bias=CV2.ap[:, HBA + c:HBA + c + 1]), reads=[b_r.buf, CV2.buf], writes=[tr.buf])
        P.op("act", lambda e: e.activation(out=ti.ap[:, 0:N], in_=b_i.ap[:, 0:N], func=AF.Tanh, scale=0.5,
                                           bias=CV2.ap[:, HBI + c:HBI + c + 1]), reads=[b_i.buf, CV2.buf], writes=[ti.buf])
        relbank(b_r)
        relbank(b_i)
        P.op("act", lambda e: e.activation(out=aa.ap[:, 0:N], in_=tr.ap[:, 0:N], func=AF.Exp,
                                           scale=CV2.ap[:, HCL + c:HCL + c + 1], bias=CV2.ap[:, HCL + c:HCL + c + 1]),
             reads=[tr.buf, CV2.buf], writes=[aa.buf])
        P.op("act", lambda e: e.activation(out=a2.ap[:, 0:N], in_=tr.ap[:, 0:N], func=AF.Exp,
                                           scale=CV2.ap[:, CL + c:CL + c + 1], bias=CV2.ap[:, CL + c:CL + c + 1]),
             reads=[tr.buf, CV2.buf], writes=[a2.buf])
        relwork(tr)

        def later():
            P.op("dve", lambda e: e.tensor_scalar(out=a2.ap[:, 0:N], in0=a2.ap[:, 0:N], scalar1=1.0 - 2.0 ** -24,
                                                  scalar2=None, op0=ALU.min), reads=[a2.buf], writes=[a2.buf])
            P.op("dve", lambda e: e.scalar_tensor_tensor(out=ti.ap[:, 0:N], in0=ti.ap[:, 0:N], scalar=1.0, in1=acc2,
                                                         op0=ALU.add, op1=ALU.mult), reads=[ti.buf, acc.buf],
                 writes=[ti.buf])
            relwork(acc)
        deferred_ops.append(later)
        carry[("M", c)] = (ti, aa, a2, tm)

    def lru_p2b(S, c):
        nseg, L, N = S["nseg"], S["L"], S["N"]
        utb, ut_v, mgb, mg_v = views(S)
        sample = S["kind"] == "s"
        ti, aa, a2, tm = carry.pop(("M", c))
        hh = newwork()
        P.op("act", lambda e: e.activation(out=a2.ap[:, 0:N], in_=a2.ap[:, 0:N], func=AF.Sqrt, scale=-0.25, bias=0.25),
             reads=[a2.buf], writes=[a2.buf])
        deferred_ops.append(lambda: lru_p2c(S, c, ti, aa, a2, tm, hh))

    def lru_p2c(S, c, ti, aa, a2, tm, hh):
        nseg, L, N = S["nseg"], S["L"], S["N"]
        utb, ut_v, mgb, mg_v = views(S)
        sample = S["kind"] == "s"
        if S["first"]:
            P.op("dve", lambda e: e.memset(a2.ap[:, 0:1], 0.5), reads=[a2.buf], writes=[a2.buf])
        P.op("dve", lambda e: e.tensor_tensor(out=ti.ap[:, 0:N], in0=ti.ap[:, 0:N], in1=a2.ap[:, 0:N], op=ALU.mult),
             reads=[ti.buf, a2.buf], writes=[ti.buf])
        relwork(a2)
        for sg in range(nseg):
            if S["first"]:
                init, rb = 0.0, []
            elif sample:
                init, rb = sc16_v[:, c, 12 + sg:13 + sg], [SC16.buf]
            else:
                init, rb = HST[c].ap, [HST[c].buf]
            P.op("dve", lambda e, sg=sg, init=init: e.tensor_tensor_scan(
                out=hh.ap[:, sg * L:(sg + 1) * L], data0=aa.ap[:, sg * L:(sg + 1) * L],
                data1=ti.ap[:, sg * L:(sg + 1) * L], initial=init, op0=ALU.mult, op1=ALU.add),
                reads=[aa.buf, ti.buf] + rb, writes=[hh.buf])
        relwork(ti, aa)
        hh_v = seg3(hh.ap[:, 0:N], nseg)
        if not S["last"]:
            P.op("pool", lambda e: e.tensor_copy(out=HST[c].ap, in_=hh.ap[:, N - 1:N]), reads=[hh.buf],
                 writes=[HST[c].buf])
        else:
            P.op("pool", lambda e: e.tensor_copy(out=st_v[:, 0:nseg, 84 + c:85 + c], in_=hh_v[:, :, L - 1:L]),
                 reads=[hh.buf], writes=[ST.buf])
        P.op("pool", lambda e: e.tensor_tensor(out=mg_v[:, c, :], in0=hh.ap[:, 0:N], in1=tm.ap[:, 0:N], op=ALU.mult),
             reads=[hh.buf, tm.buf], writes=[mgb.buf])
        relwork(hh, tm)

    def pool_p1(S, g):
        nseg, L, N = S["nseg"], S["L"], S["N"]
        par = g % 2
        W = 15 + L
        utb, ut_v, mgb, mg_v = views(S)
        sample = S["kind"] == "s"
        wnd = 2 ** (g + 1)
        b_xp = newbank()
        mm_group(b_xp, N, win_get(16 + g), utb, ut_v)
        xpw = XPW[par]
        xpw_v = xpw.ap[:, 0:nseg * W].rearrange("p (s w) -> p s w", s=nseg)
        if S["first"]:
            P.op("pool", lambda e: e.memset(xpw_v[:, :, 0:15], 0.0), writes=[XPWH[par]])
        elif sample:
            P.op("pool", lambda e: e.tensor_copy(out=xpw_v[:, :, 0:15],
                                                 in_=sp60_v[:, g, :].rearrange("p (s r) -> p s r", r=15)),
                 reads=[SP60.buf], writes=[XPWH[par]])
        else:
            P.op("pool", lambda e: e.tensor_copy(out=xpw_v[:, :, 0:15], in_=POOLH[g].ap.unsqueeze(1)),
                 reads=[POOLH[g].buf], writes=[XPWH[par]])
        P.op("act", lambda e: e.activation(out=xpw_v[:, :, 15:W], in_=seg3(b_xp.ap[:, 0:N], nseg), func=AF.Copy),
             reads=[b_xp.buf], writes=[xpw.buf])
        relbank(b_xp)
        src, src_v = xpw, xpw_v
        for i in range(g + 1):
            sh = 2 ** i
            lo = 2 ** (i + 1) - 1
            dst = (SA, SBB)[i % 2]
            dst_v = dst.ap[:, 0:nseg * W].rearrange("p (s w) -> p s w", s=nseg)
            P.op("pool", lambda e, dst_v=dst_v, src_v=src_v, lo=lo, sh=sh: e.tensor_tensor(
                out=dst_v[:, :, lo:W], in0=src_v[:, :, lo:W], in1=src_v[:, :, lo - sh:W - sh], op=ALU.add),
                reads=[src.buf, XPWH[par]], writes=[dst.buf])
            src, src_v = dst, dst_v
        if not S["last"]:
            P.op("pool", lambda e: e.tensor_copy(out=POOLH[g].ap.unsqueeze(1), in_=xpw_v[:, :, L:L + 15]),
                 reads=[xpw.buf], writes=[POOLH[g].buf])
        else:
            P.op("pool", lambda e: e.tensor_copy(
                out=st_v[:, 0:nseg, 24:84].rearrange("p s (r g) -> p s r g", g=4)[:, :, :, g],
                in_=xpw_v[:, :, L:L + 15]), reads=[xpw.buf], writes=[ST.buf])
        pl = PL[par]
        pl_v = seg3(pl.ap[:, 0:N], nseg)
        wsum_v = src_v[:, :, 15:W]

        def later():
            P.op("dve", lambda e: e.scalar_tensor_tensor(out=pl_v, in0=wsum_v, scalar=1.0 / wnd, in1=xpw_v[:, :, 15:W],
                                                         op0=ALU.mult, op1=ALU.subtract),
                 reads=[src.buf, xpw.buf], writes=[pl.buf])
            if S["first"]:
                P.op("dve", lambda e: e.tensor_tensor(out=T16.ap, in0=src_v[:, 0, 15:31], in1=invc_v[:, g, :],
                                                      op=ALU.mult), reads=[src.buf, INVC.buf], writes=[T16.buf])
                P.op("dve", lambda e: e.tensor_tensor(out=pl.ap[:, 0:16], in0=T16.ap, in1=xpw_v[:, 0, 15:31],
                                                      op=ALU.subtract), reads=[T16.buf, xpw.buf, pl.buf], writes=[pl.buf])
        (deferred_ops2 if g < 3 else deferred_ops).append(later)

    def pool_p2(S, g):
        nseg, L, N = S["nseg"], S["L"], S["N"]
        utb, ut_v, mgb, mg_v = views(S)
        pl = PL[g % 2]
        for hf in range(2):
            cc = 2 * g + hf
            b_gp, b_mp = newbank(), newbank()
            mm_group(b_gp, N, win_get(20 + cc), utb, ut_v)
            mm_group(b_mp, N, win_get(36 + cc), utb, ut_v)
            b_pg = newbank()
            P.op("pe", lambda e, hf=hf, b_pg=b_pg: e.matmul(b_pg.ap[:, 0:N], lhsT=wpool_v[:, g, hf * 128:(hf + 1) * 128],
                                                            rhs=pl.ap[:, 0:N], start=True, stop=True),
                 reads=[pl.buf, WPOOL.buf], writes=[b_pg.buf])
            tg, tm = newwork(), newwork()
            P.op("act", lambda e, tg=tg, b_gp=b_gp: e.activation(out=tg.ap[:, 0:N], in_=b_gp.ap[:, 0:N], func=AF.Tanh,
                                                                 scale=0.5), reads=[b_gp.buf], writes=[tg.buf])
            P.op("act", lambda e, tm=tm, b_mp=b_mp: e.activation(out=tm.ap[:, 0:N], in_=b_mp.ap[:, 0:N], func=AF.Tanh,
                                                                 scale=0.5), reads=[b_mp.buf], writes=[tm.buf])
            relbank(b_mp)
            P.op("dve", lambda e, tg=tg, b_gp=b_gp: e.scalar_tensor_tensor(
                out=tg.ap[:, 0:N], in0=tg.ap[:, 0:N], scalar=1.0, in1=b_gp.ap[:, 0:N], op0=ALU.add, op1=ALU.mult),
                reads=[tg.buf, b_gp.buf], writes=[tg.buf])
            relbank(b_gp)
            P.op("dve", lambda e, tg=tg, tm=tm: e.scalar_tensor_tensor(
                out=tm.ap[:, 0:N], in0=tm.ap[:, 0:N], scalar=1.0, in1=tg.ap[:, 0:N], op0=ALU.add, op1=ALU.mult),
                reads=[tm.buf, tg.buf], writes=[tm.buf])
            P.op("dve", lambda e, tg=tg, tm=tm, b_pg=b_pg, cc=cc: e.scalar_tensor_tensor(
                out=tg.ap[:, 0:N], in0=b_pg.ap[:, 0:N], scalar=CV2.ap[:, BPS + cc:BPS + cc + 1],
                in1=tm.ap[:, 0:N], op0=ALU.add, op1=ALU.mult), reads=[b_pg.buf, tm.buf, CV2.buf], writes=[tg.buf])
            relbank(b_pg)
            P.op("pool", lambda e, tg=tg, cc=cc: e.tensor_tensor(out=mg_v[:, cc, :], in0=mg_v[:, cc, :],
                                                                 in1=tg.ap[:, 0:N], op=ALU.add),
                 reads=[mgb.buf, tg.buf], writes=[mgb.buf])
            relwork(tg, tm)

    def state_out(S):
        nseg = S["nseg"]
        bk_ = newbank()
        for sg in range(nseg):
            P.op("pe", lambda e, sg=sg: e.transpose(out=bk_.ap[0:92, sg * 128:(sg + 1) * 128], in_=st_v[:, sg, :],
                                                    identity=IDF.ap), reads=[ST.buf, IDF.buf], writes=[bk_.buf])
        P.op("act", lambda e: e.activation(out=STO.ap[0:92, 0:nseg * 128], in_=bk_.ap[0:92, 0:nseg * 128], func=AF.Copy),
             reads=[bk_.buf], writes=[STO.buf])
        relbank(bk_)
        for sg in range(nseg):
            if S["kind"] == "p":
                oc, ol, op_ = o_ncp[S["seq"]], o_nlp[S["seq"]], o_npp[S["seq"]]
            else:
                oc, ol, op_ = o_ncs[sg], o_nls[sg], o_nps[sg]
            P.dma(lambda e, sg=sg, oc=oc: e.dma_start(out=oc.rearrange("r (c p) -> (r c) p", p=128),
                                                      in_=STO.ap[0:24, sg * 128:(sg + 1) * 128]),
                  lane=STO.buf, reads=[STO.buf], queue="act")
            P.dma(lambda e, sg=sg, op_=op_: e.dma_start(out=op_.rearrange("r (g p) -> (r g) p", p=128),
                                                        in_=STO.ap[24:84, sg * 128:(sg + 1) * 128]),
                  lane=STO.buf, reads=[STO.buf], queue="act")
            P.dma(lambda e, sg=sg, ol=ol: e.dma_start(out=ol.rearrange("(c p) -> c p", p=128),
                                                      in_=STO.ap[84:92, sg * 128:(sg + 1) * 128]),
                  lane=STO.buf, reads=[STO.buf], queue="act")

    def b_task(S, t):
        kind, a = B_TASKS[t]
        flush_deferred()
        {"L1": lru_p1, "L2a": lru_p2, "L2b": lru_p2b, "Q1": pool_p1, "Q2": pool_p2}[kind](S, a)
        if t == len(B_TASKS) - 1:
            flush_deferred()
            flush_deferred()
            if S["last"]:
                state_out(S)

    pending_st = []

    def flush_stores():
        while pending_st:
            pending_st.pop(0)()

    NCS = 8
    cst_state = {}

    def c_stage(k, stage):
        S, j = a_tiles[k]
        N = S["N"]
        mgb = MG[S["idx"] % 2]
        mg_v = mgb.ap[:, 0:8 * N].rearrange("p (k n) -> p k n", k=8)
        xr = XR[k % 3]
        u2 = U2[k % 2]
        u2t = U2T[k % 2]
        p32, pb, pt = P32[k % 2], PB[k % 2], PT[k % 2]
        tg = TGC[0]
        if stage == 0:
            c_ensure(k + 1)
            P.op("pool", lambda e: e.tensor_copy(out=pb.ap, in_=p32.ap), reads=[p32.buf], writes=[pb.buf])
            d = [newbank(), newbank()]
            for hf in range(2):
                for kc in range(8):
                    P.op("pe", lambda e, hf=hf, kc=kc: e.matmul(
                        d[hf].ap, lhsT=mg_v[:, kc, j * 128:(j + 1) * 128], rhs=wout_v[:, kc, hf * 512:(hf + 1) * 512],
                        start=(kc == 0), stop=(kc == 7)), reads=[mgb.buf, WOUT.buf], writes=[d[hf].buf])
            for hf in range(2):
                P.op("dve", lambda e, hf=hf: e.tensor_tensor(out=xr.ap[:, hf * 512:(hf + 1) * 512],
                                                             in0=xr.ap[:, hf * 512:(hf + 1) * 512], in1=d[hf].ap,
                                                             op=ALU.add), reads=[xr.buf, d[hf].buf], writes=[xr.buf])
            relbank(d[0])
            relbank(d[1])
            P.op("dve", lambda e: e.tensor_copy(out=u2.ap, in_=xr.ap), reads=[xr.buf], writes=[u2.buf])
        elif stage == 1:
            st = newstat()
            P.op("act", lambda e: e.activation(out=JUNK.ap, in_=xr.ap, func=AF.Square, accum_out=st.ap[:, 0:1]),
                 reads=[xr.buf], writes=[st.buf, JUNK.buf])
            cst_state[k] = st
        elif stage == 2:
            rstd_ops(cst_state[k])
            tb = newtbank()
            tbv = tb.ap.bitcast(BF16)
            for c in range(8):
                P.op("pe", lambda e, c=c: e.transpose(out=tbv[:, c * 128:(c + 1) * 128],
                                                      in_=u2.ap[:, c * 128:(c + 1) * 128], identity=IDB.ap),
                     reads=[u2.buf, IDB.buf], writes=[tb.buf])
            P.op("act", lambda e: e.activation(out=u2t.ap, in_=tbv, func=AF.Copy), reads=[tb.buf], writes=[u2t.buf])
            tb2 = newtbank()
            tb2v = tb2.ap.bitcast(BF16)
            for c in range(2):
                P.op("pe", lambda e, c=c: e.transpose(out=tb2v[:, c * 128:(c + 1) * 128],
                                                      in_=pb.ap[:, c * 128:(c + 1) * 128], identity=IDB.ap),
                     reads=[pb.buf, IDB.buf], writes=[tb2.buf])
            P.op("act", lambda e: e.activation(out=pt.ap, in_=tb2v[:, 0:256], func=AF.Copy), reads=[tb2.buf],
                 writes=[pt.buf])
        elif stage == 3:
            u2t_v = u2t.ap.rearrange("p (k n) -> p k n", k=8)
            eb = [newbank(), newbank()]
            for hf in range(2):
                for kc in range(8):
                    P.op("pe", lambda e, hf=hf, kc=kc: e.matmul(
                        eb[hf].ap, lhsT=u2t_v[:, kc, :], rhs=wpg_v[:, kc, hf * 512:(hf + 1) * 512],
                        start=(kc == 0), stop=(kc == 7)), reads=[u2t.buf, WPG.buf], writes=[eb[hf].buf])
            st = cst_state.pop(k)
            for hf in range(2):
                P.op("act", lambda e, hf=hf: e.activation(out=tg.ap[:, hf * 512:(hf + 1) * 512], in_=eb[hf].ap,
                                                          func=AF.Tanh, scale=st.ap[:, 1:2]),
                     reads=[eb[hf].buf, st.buf], writes=[tg.buf])
            relbank(eb[0])
            relbank(eb[1])
        elif stage == 4:
            pt_v = pt.ap.rearrange("p (k n) -> p k n", k=2)
            fbk = [newbank(), newbank()]
            for hf in range(2):
                for kc in range(2):
                    P.op("pe", lambda e, hf=hf, kc=kc: e.matmul(
                        fbk[hf].ap, lhsT=pt_v[:, kc, :], rhs=wpe_v[:, kc, hf * 512:(hf + 1) * 512],
                        start=(kc == 0), stop=(kc == 1)), reads=[pt.buf, WPE.buf], writes=[fbk[hf].buf])
            for hf in range(2):
                P.op("dve", lambda e, hf=hf: e.scalar_tensor_tensor(
                    out=tg.ap[:, hf * 512:(hf + 1) * 512], in0=tg.ap[:, hf * 512:(hf + 1) * 512], scalar=1.0,
                    in1=fbk[hf].ap, op0=ALU.add, op1=ALU.mult), reads=[tg.buf, fbk[hf].buf], writes=[tg.buf])
            relbank(fbk[0])
            relbank(fbk[1])
            P.op("pool", lambda e: e.tensor_tensor(out=xr.ap, in0=xr.ap, in1=tg.ap, op=ALU.add),
                 reads=[xr.buf, tg.buf], writes=[xr.buf])
        elif stage == 5:
            st = newstat()
            P.op("act", lambda e: e.activation(out=JUNK.ap, in_=xr.ap, func=AF.Square, accum_out=st.ap[:, 0:1]),
                 reads=[xr.buf], writes=[st.buf, JUNK.buf])
            cst_state[("y", k)] = st
        elif stage == 6:
            rstd_ops(cst_state[("y", k)])
        elif stage == 7:
            st = cst_state.pop(("y", k))
            P.op("dve", lambda e: e.scalar_tensor_tensor(out=xr.ap, in0=xr.ap, scalar=st.ap[:, 1:2], in1=G3.ap,
                                                         op0=ALU.mult, op1=ALU.mult),
                 reads=[xr.buf, st.buf, G3.buf], writes=[xr.buf])
            pending_st.append(lambda dst=y_dst(S, j), xr=xr: P.dma(
                lambda e: e.dma_start(out=dst, in_=xr.ap), lane=xr.buf, reads=[xr.buf], queue="act"))

    first_tile = {}
    k0 = 0
    for S in tiles:
        first_tile[S["idx"]] = k0
        k0 += S["nt"]
    NT = len(B_TASKS)
    a_ensure(1)
    for j in range(tiles[0]["nt"]):
        phase_a1(first_tile[0] + j)
        phase_a1r(first_tile[0] + j)
        phase_a1b(first_tile[0] + j)
        if j >= 1:
            phase_a2(first_tile[0] + j - 1)
    phase_a2(first_tile[0] + tiles[0]["nt"] - 1)
    for _ in range(len(STGC)):
        chunk_load_next()
    for r in range(NST + 1):
        Sb = tiles[r] if r < NST else None
        Sc = tiles[r - 1] if r >= 1 else None
        Sa = tiles[r + 1] if r + 1 < NST else None
        cslot = {}
        if Sc is not None:
            ntc = Sc["nt"]
            for j in range(ntc):
                base = (j * NT) // ntc
                offs = (0, 1, 3, 5, 6, 8, 9, 11) if ntc > 1 else (0, 4, 8, 12, 16, 20, 24, 28)
                for stg_ in range(NCS):
                    cslot.setdefault(min(NT - 1, base + offs[stg_]), []).append((first_tile[Sc["idx"]] + j, stg_))
        ast = []
        if Sa is not None:
            ast = [first_tile[Sa["idx"]] + j for j in range(Sa["nt"])]
        for t in range(NT):
            if r == 0:
                if cu["done"] >= NM and deferred_w:
                    deferred_w.pop(0)()
                    if deferred_w:
                        deferred_w.pop(0)()
            if Sb is not None:
                b_task(Sb, t)
            flush_stores()
            for (kk, stg_) in cslot.get(t, []):
                c_stage(kk, stg_)
            nA = len(ast)
            for i_, kk in enumerate(ast):
                t2 = ((i_ + 1) * NT) // nA - 1
                if t == max(0, t2 - 7):
                    phase_a1(kk)
                if t == max(1, t2 - 5):
                    phase_a1r(kk)
                if t == max(2, t2 - 3):
                    phase_a1b(kk)
                if t == t2:
                    phase_a2(kk)
        if r == 0:
            assert cu["done"] == NM
            while deferred_w:
                deferred_w.pop(0)()
            c_ensure(0)

    flush_stores()
    P.finalize()
    es.close()
    return nc, P


_CACHE = {}


def _get_program():
    if "nc" not in _CACHE:
        nc, P = build_program()
        _CACHE["nc"] = nc
        _CACHE["stats"] = P.stats
    return _CACHE["nc"]


def kernel(x_prompt, x_sample, p_prompt, p_sample, state_conv, state_lru, state_pool,
           norm_mix, w_in, conv_w, conv_b, w_rg_a, b_rg_a, w_rg_i, b_rg_i, lru_lambda,
           w_pool, b_pool, pool_scale, w_out, norm_ple, w_ple_gate, w_ple, final_norm):
    f = lambda a: np.ascontiguousarray(np.asarray(a, dtype=np.float32))
    x_prompt, x_sample, p_prompt, p_sample = f(x_prompt), f(x_sample), f(p_prompt), f(p_sample)
    state_conv, state_lru, state_pool = f(state_conv), f(state_lru), f(state_pool)
    shared = {
        "norm_mix": f(norm_mix), "w_in": f(w_in), "conv_w": f(conv_w), "conv_b": f(conv_b),
        "w_rg_a": f(w_rg_a), "b_rg_a": f(b_rg_a), "w_rg_i": f(w_rg_i), "b_rg_i": f(b_rg_i),
        "lru_lambda": f(lru_lambda), "w_pool": f(w_pool), "b_pool": f(b_pool), "pool_scale": f(pool_scale),
        "w_out": f(w_out), "norm_ple": f(norm_ple), "w_ple_gate": f(w_ple_gate), "w_ple": f(w_ple),
        "final_norm": f(final_norm),
    }
    in_maps = []
    for i in range(NCORES):
        m = dict(shared)
        m["xp"] = np.ascontiguousarray(x_prompt[2 * i:2 * i + 2].reshape(4096, 1024))
        m["pp"] = np.ascontiguousarray(p_prompt[0, 2 * i:2 * i + 2].reshape(4096, 256))
        m["xs"] = np.ascontiguousarray(x_sample[4 * i:4 * i + 4].reshape(128, 1024))
        m["ps"] = np.ascontiguousarray(p_sample[0, 4 * i:4 * i + 4].reshape(128, 256))
        m["sconv"] = np.ascontiguousarray(state_conv[0, 4 * i:4 * i + 4].reshape(12, 1024))
        m["slru"] = np.ascontiguousarray(state_lru[0, 4 * i:4 * i + 4].reshape(4, 1024))
        m["spool"] = np.ascontiguousarray(state_pool[0, 4 * i:4 * i + 4].reshape(60, 512))
        in_maps.append(m)
    nc = _get_program()
    res = run_bass_kernel_spmd(nc, in_maps, core_ids=list(range(NCORES)))
    R = res.results
    y_prompt = np.concatenate([R[i]["yp"].reshape(2, 2048, 1024) for i in range(NCORES)], axis=0)
    y_sample = np.concatenate([R[i]["ys"].reshape(4, 32, 1024) for i in range(NCORES)], axis=0)
    ncp = np.concatenate([R[i]["ncp"] for i in range(NCORES)], axis=0)[None]
    nlp = np.concatenate([R[i]["nlp"] for i in range(NCORES)], axis=0)[None]
    npp = np.concatenate([R[i]["npp"] for i in range(NCORES)], axis=0)[None]
    ncs = np.concatenate([R[i]["ncs"] for i in range(NCORES)], axis=0)[None]
    nls = np.concatenate([R[i]["nls"] for i in range(NCORES)], axis=0)[None]
    nps = np.concatenate([R[i]["nps"] for i in range(NCORES)], axis=0)[None]
    return tuple(np.ascontiguousarray(a.astype(np.float32)) for a in (y_prompt, y_sample, ncp, nlp, npp, ncs, nls, nps))
```

```python
import numpy as np
from contextlib import ExitStack

import concourse.bass as bass
import concourse.mybir as mybir
from concourse.bass_utils import run_bass_kernel_spmd

F32 = mybir.dt.float32
BF16 = mybir.dt.bfloat16
AF = mybir.ActivationFunctionType
ALU = mybir.AluOpType

NCORES = 8
EPS = 1e-6
D = 1024
IN_COLS = 5632
NM = IN_COLS // 128


class Buf:
    __slots__ = ("name", "last_w", "readers", "sem", "cnt")

    def __init__(self, name):
        self.name = name
        self.last_w = None
        self.readers = []
        self.sem = None
        self.cnt = 0


class Op:
    __slots__ = ("eng", "fn", "deps", "needed", "count", "sem", "waits", "knows", "is_dma", "lane", "dbg", "wdbg")

    def __init__(self, eng, fn, is_dma=False, lane=None):
        self.eng = eng
        self.fn = fn
        self.deps = []
        self.needed = is_dma
        self.count = 0
        self.sem = None
        self.waits = []
        self.knows = None
        self.is_dma = is_dma
        self.lane = lane


class Prog:
    ENGS = ("pe", "act", "dve", "pool", "sp")

    def __init__(self, nc, es):
        self.nc = nc
        self.es = es
        self.ops = []
        self.lanes = []

    def buf(self, name):
        return Buf(name)

    def _add(self, o, reads, writes):
        o.dbg = ([b.name for b in reads], [b.name for b in writes])
        o.wdbg = []
        deps = {}
        for b in reads:
            if b.last_w is not None:
                deps[id(b.last_w)] = (b.last_w, "raw")
        for b in writes:
            if b.last_w is not None and id(b.last_w) not in deps:
                deps[id(b.last_w)] = (b.last_w, "waw")
            for r in b.readers:
                if id(r) not in deps:
                    deps[id(r)] = (r, "war")
        for p, kind in deps.values():
            if p is o:
                continue
            if (not p.is_dma) and (not o.is_dma) and p.eng == o.eng:
                if o.eng == "pe":
                    continue
            if p.is_dma and o.is_dma and p.lane is o.lane and kind == "waw":
                continue
            o.deps.append(p)
            p.needed = True
        for b in reads:
            if b in writes:
                continue
            if o.is_dma:
                b.readers = [r for r in b.readers if not (r.is_dma and r.lane is o.lane)]
            else:
                b.readers = [r for r in b.readers if r.is_dma or r.eng != o.eng]
            b.readers.append(o)
        for b in writes:
            b.last_w = o
            b.readers = []
        self.ops.append(o)
        return o

    def op(self, eng, fn, reads=(), writes=()):
        return self._add(Op(eng, fn), list(reads), list(writes))

    def dma(self, fn, lane, reads=(), writes=(), queue="sp"):
        if not any(l is lane for l in self.lanes):
            self.lanes.append(lane)
        o = Op(queue, fn, is_dma=True, lane=lane)
        return self._add(o, list(reads), list(writes))

    def finalize(self):
        nc, es = self.nc, self.es
        engsem = {e: es.enter_context(nc.semaphore("sem_" + e)) for e in ("pe", "act", "dve", "pool")}
        for i, ln in enumerate(self.lanes):
            ln.sem = es.enter_context(nc.semaphore("lane%d" % i))
        engcnt = {e: 0 for e in engsem}
        known = {e: {} for e in self.ENGS}
        per = {e: [] for e in self.ENGS}
        nwaits = 0
        for o in self.ops:
            K = known[o.eng]
            waits = {}
            for p in o.deps:
                sid = id(p.sem)
                if K.get(sid, 0) >= p.count:
                    continue
                if sid not in waits or waits[sid][1] < p.count:
                    waits[sid] = (p.sem, p.count, p)
            for sid, (sem, val, p) in waits.items():
                if p.knows:
                    for k2, v2 in p.knows.items():
                        if K.get(k2, 0) < v2:
                            K[k2] = v2
                if K.get(sid, 0) < val:
                    K[sid] = val
            o.waits = [(sem, val) for sid, (sem, val, p) in waits.items()]
            o.wdbg = [(p.eng, p.dbg) for sid, (sem, val, p) in waits.items()]
            nwaits += len(o.waits)
            if o.is_dma:
                o.lane.cnt += 16
                o.count = o.lane.cnt
                o.sem = o.lane.sem
                o.knows = dict(K)
            elif o.needed:
                engcnt[o.eng] += 1
                o.count = engcnt[o.eng]
                o.sem = engsem[o.eng]
                o.knows = dict(K)
            per[o.eng].append(o)
        self.per = per
        self.stats = {e: len(per[e]) for e in per}
        self.stats["waits"] = nwaits

        def emit(eng_handle, ops, final=False):
            for o in ops:
                for sem, val in o.waits:
                    eng_handle.wait_ge(sem, val)
                ins = o.fn(eng_handle)
                if o.is_dma:
                    ins.then_inc(o.sem, 16)
                elif o.needed:
                    ins.then_inc(o.sem, 1)
            if final:
                for ln in self.lanes:
                    if ln.cnt:
                        eng_handle.wait_ge(ln.sem, ln.cnt)
                for e, s in engsem.items():
                    if engcnt[e]:
                        eng_handle.wait_ge(s, engcnt[e])

        block = es.enter_context(nc.Block())

        @block.sync
        def _(e):
            emit(e, per["sp"], final=True)

        @block.tensor
        def _(e):
            emit(e, per["pe"])

        @block.scalar
        def _(e):
            emit(e, per["act"])

        @block.vector
        def _(e):
            emit(e, per["dve"])

        @block.gpsimd
        def _(e):
            emit(e, per["pool"])


class T:
    def __init__(self, ap, buf):
        self.ap = ap
        self.buf = buf


G1, G2, CW, CB, BA, BI, LAM, BP, PSC = 0, 8, 16, 48, 56, 64, 72, 80, 88
HBA, HBI, CL, HCL, BPS = 0, 8, 16, 24, 32

B_TASKS = [("L1", 0)]
for _c in range(1, 8):
    B_TASKS.append(("L1", _c))
    B_TASKS.append(("L2a", _c - 1))
    if _c % 2 == 0:
        B_TASKS += [("L2b", _c - 2), ("L2b", _c - 1)]
B_TASKS += [("Q1", 0), ("L2a", 7), ("L2b", 6), ("L2b", 7), ("Q1", 1), ("Q2", 0), ("Q1", 2), ("Q2", 1), ("Q1", 3),
            ("Q2", 2), ("Q2", 3)]
W_ORDER = []
for _t, _a in B_TASKS:
    if _t == "L1":
        W_ORDER += [_a, 8 + _a, 28 + _a]
    elif _t == "Q1":
        W_ORDER += [16 + _a]
    elif _t == "Q2":
        W_ORDER += [20 + 2 * _a, 36 + 2 * _a, 21 + 2 * _a, 37 + 2 * _a]
assert sorted(W_ORDER) == list(range(NM))
PAIR_ORDER = []
for _m in W_ORDER:
    if _m // 2 not in PAIR_ORDER:
        PAIR_ORDER.append(_m // 2)


def build_program(stop=None):
    nc = bass.Bass("TRN2", target_bir_lowering=False)
    es = ExitStack()
    P = Prog(nc, es)

    def din(name, shape, dt=F32):
        return nc.dram_tensor(name, shape, dt, kind="ExternalInput").ap()

    def dout(name, shape):
        return nc.dram_tensor(name, shape, F32, kind="ExternalOutput").ap()

    d_xp = din("xp", [4096, 1024])
    d_pp = din("pp", [4096, 256])
    d_xs = din("xs", [128, 1024])
    d_ps = din("ps", [128, 256])
    d_sconv = din("sconv", [12, 1024])
    d_slru = din("slru", [4, 1024])
    d_spool = din("spool", [60, 512])
    d_norm_mix = din("norm_mix", [1, 1024])
    d_w_in = din("w_in", [1, 1024, IN_COLS])
    d_conv_w = din("conv_w", [1, 4, 1024])
    d_conv_b = din("conv_b", [1, 1024])
    d_w_rg_a = din("w_rg_a", [1, 16, 64, 64])
    d_b_rg_a = din("b_rg_a", [1, 1024])
    d_w_rg_i = din("w_rg_i", [1, 16, 64, 64])
    d_b_rg_i = din("b_rg_i", [1, 1024])
    d_lam = din("lru_lambda", [1, 1024])
    d_w_pool = din("w_pool", [1, 4, 128, 256])
    d_b_pool = din("b_pool", [1, 4, 256])
    d_pool_scale = din("pool_scale", [1, 1024])
    d_w_out = din("w_out", [1, 1024, 1024])
    d_norm_ple = din("norm_ple", [1, 1024])
    d_w_pg = din("w_ple_gate", [1, 1024, 1024])
    d_w_ple = din("w_ple", [1, 256, 1024])
    d_final_norm = din("final_norm", [1024])

    o_yp = dout("yp", [4096, 1024])
    o_ys = dout("ys", [128, 1024])
    o_ncp = dout("ncp", [2, 3, 1024])
    o_nlp = dout("nlp", [2, 1024])
    o_npp = dout("npp", [2, 15, 512])
    o_ncs = dout("ncs", [4, 3, 1024])
    o_nls = dout("nls", [4, 1024])
    o_nps = dout("nps", [4, 15, 512])

    d_wsc = nc.dram_tensor("win_bf16", [NM, 128, 1024], BF16, kind="Internal").ap()
    wsc_buf = [P.buf("wsc%d" % m) for m in range(NM)]

    def sb(name, shape, dt=F32):
        t = es.enter_context(nc.sbuf_tensor(name, shape, dt))
        return T(t[:], P.buf(name))

    WOUT = sb("wout", [128, 8 * 1024], BF16)
    WPG = sb("wpg", [128, 8 * 1024], BF16)
    WPE = sb("wpe", [128, 2 * 1024], BF16)
    WA = sb("wa", [128, 8 * 128], BF16)
    WI = sb("wi", [128, 8 * 128], BF16)
    WPOOL = sb("wpool", [128, 4 * 256], BF16)
    G3 = sb("g3bc", [128, 1024], F32)
    IDF = sb("identf", [128, 128], F32)
    IDB = sb("identb", [128, 128], BF16)
    CV = sb("cvec", [128, 96], F32)
    CV2 = sb("cvec2", [128, 40], F32)
    INVC = sb("invc", [128, 4 * 16], F32)
    NEGH = sb("negh", [128, 1], F32)
    SC16 = sb("sc16", [128, 8 * 16], F32)
    SP60 = sb("sp60", [128, 4 * 60], F32)
    CONVH = [sb("convh%d" % c, [128, 3], F32) for c in range(8)]
    HST = [sb("hst%d" % c, [128, 1], F32) for c in range(8)]
    POOLH = [sb("poolh%d" % g, [128, 15], F32) for g in range(4)]
    ST = sb("stg", [128, 4 * 92], F32)
    STO = sb("sto", [128, 4 * 128], F32)

    wout_v = WOUT.ap.rearrange("p (k n) -> p k n", k=8)
    wpg_v = WPG.ap.rearrange("p (k n) -> p k n", k=8)
    wpe_v = WPE.ap.rearrange("p (k n) -> p k n", k=2)
    wa_v = WA.ap.rearrange("p (k n) -> p k n", k=8)
    wi_v = WI.ap.rearrange("p (k n) -> p k n", k=8)
    wpool_v = WPOOL.ap.rearrange("p (g n) -> p g n", g=4)
    invc_v = INVC.ap.rearrange("p (g n) -> p g n", g=4)
    sc16_v = SC16.ap.rearrange("p (c n) -> p c n", c=8)
    sp60_v = SP60.ap.rearrange("p (g n) -> p g n", g=4)
    st_v = ST.ap.rearrange("p (s n) -> p s n", s=4)

    XS = [sb("xs%d" % i, [128, 1024], F32) for i in range(3)]
    XRT = sb("xrt", [128, 4 * 1024], F32)
    XR = [T(XRT.ap[:, i * 1024:(i + 1) * 1024], P.buf("xr%d" % i)) for i in range(3)]
    UB = [sb("ub%d" % i, [128, 1024], BF16) for i in range(2)]
    U2 = [sb("u2_%d" % i, [128, 1024], BF16) for i in range(2)]
    JUNK = sb("junk", [128, 1024], BF16)
    UT = [sb("ut%d" % i, [128, 8 * 512], BF16) for i in range(2)]
    MG = [sb("mg%d" % i, [128, 8 * 512], BF16) for i in range(2)]
    RING_N = 12
    RING = [sb("wr%d" % i, [128, 1024], BF16) for i in range(8)]
    EXTRA = sb("extra", [128, 4 * 1024], BF16)
    RING += [T(EXTRA.ap[:, i * 1024:(i + 1) * 1024], P.buf("wrx%d" % i)) for i in range(4)]
    NWORK = 18
    WORK = [sb("wk%d" % i, [128, 512], F32) for i in range(NWORK)]
    XLW = [sb("xlw%d" % i, [128, 515], F32) for i in range(2)]
    XPW = [sb("xpw%d" % i, [128, 527], F32) for i in range(2)]
    XLWH = [P.buf("xlwh%d" % i) for i in range(2)]
    XPWH = [P.buf("xpwh%d" % i) for i in range(2)]
    SA = sb("sa", [128, 527], F32)
    SBB = sb("sbb", [128, 527], F32)
    XCB = [sb("xcb%d" % i, [128, 512], BF16) for i in range(2)]
    PL = [sb("pl%d" % i, [128, 512], BF16) for i in range(2)]
    T16 = sb("t16", [128, 16], F32)
    STAT = [sb("stat%d" % i, [128, 4], F32) for i in range(8)]
    U2T = [sb("u2t%d" % i, [128, 8 * 128], BF16) for i in range(2)]
    P32 = [sb("p32_%d" % i, [128, 256], F32) for i in range(2)]
    PB = [sb("pb%d" % i, [128, 256], BF16) for i in range(2)]
    PT = [sb("pt%d" % i, [128, 256], BF16) for i in range(2)]
    TGC = [T(XRT.ap[:, 3072:4096], P.buf("tgc0"))]

    ps_t = es.enter_context(nc.psum_tensor("psum", [128, 8 * 512], F32))
    NFB = 6
    NTB = 8 - NFB
    FB = [T(ps_t[:, i * 512:(i + 1) * 512], P.buf("fb%d" % i)) for i in range(NFB)]
    TB = [T(ps_t[:, i * 512:(i + 1) * 512], P.buf("tb%d" % i)) for i in range(NFB, 8)]
    ctr = {"fb": 0, "tb": 0, "wk": 0, "stat": 0}

    fb_free = list(range(NFB))

    def newbank():
        assert fb_free, "out of PSUM banks"
        return FB[fb_free.pop(0)]

    def relbank(b):
        i = [j for j in range(NFB) if FB[j] is b][0]
        assert i not in fb_free
        fb_free.append(i)

    def newtbank():
        b = TB[ctr["tb"] % len(TB)]
        ctr["tb"] += 1
        return b

    wk_free = list(range(NWORK))

    def newwork():
        assert wk_free, "out of work tiles"
        return WORK[wk_free.pop(0)]

    def relwork(*ts):
        for t_ in ts:
            i = [j for j in range(NWORK) if WORK[j] is t_][0]
            assert i not in wk_free
            wk_free.append(i)

    def newstat():
        b = STAT[ctr["stat"] % len(STAT)]
        ctr["stat"] += 1
        return b

    rr = {"i": 0}

    def anyeng():
        e = "dve"
        rr["i"] += 1
        return e

    P.op("pool", lambda e: e.memset(NEGH.ap, -0.5), writes=[NEGH.buf])
    P.op("pool", lambda e: e.memset(IDF.ap, 1.0), writes=[IDF.buf])
    P.op("pool", lambda e: e.affine_select(out=IDF.ap, in_=IDF.ap, pattern=[[-1, 128]],
                                           compare_op=ALU.is_equal, fill=0.0, base=0,
                                           channel_multiplier=1), reads=[IDF.buf], writes=[IDF.buf])
    P.op("dve", lambda e: e.tensor_copy(out=IDB.ap, in_=IDF.ap), reads=[IDF.buf], writes=[IDB.buf])

    vs = XS[0]
    vec_rows = [
        (d_norm_mix[0], G1, 8), (d_norm_ple[0], G2, 8), (d_conv_b[0], CB, 8),
        (d_b_rg_a[0], BA, 8), (d_b_rg_i[0], BI, 8), (d_lam[0], LAM, 8), (d_pool_scale[0], PSC, 8),
    ]
    for src, r0, n in vec_rows:
        P.dma(lambda e, src=src, r0=r0, n=n: e.dma_start(out=vs.ap[r0:r0 + n, 0:128],
                                                         in_=src.rearrange("(a b) -> a b", b=128)),
              lane=vs.buf, writes=[vs.buf])
    P.dma(lambda e: e.dma_start(out=vs.ap[CW:CW + 32, 0:128],
                                in_=d_conv_w[0].rearrange("k (a b) -> (k a) b", b=128)),
          lane=vs.buf, writes=[vs.buf])
    P.dma(lambda e: e.dma_start(out=vs.ap[BP:BP + 8, 0:128],
                                in_=d_b_pool[0].rearrange("g (a b) -> (g a) b", b=128)),
          lane=vs.buf, writes=[vs.buf])
    bk = newbank()
    P.op("pe", lambda e: e.transpose(out=bk.ap[:, 0:96], in_=vs.ap[0:96, 0:128], identity=IDF.ap[0:96, 0:96]),
         reads=[vs.buf, IDF.buf], writes=[bk.buf])
    P.op("dve", lambda e: e.tensor_copy(out=CV.ap, in_=bk.ap[:, 0:96]), reads=[bk.buf], writes=[CV.buf])
    relbank(bk)
    P.op("dve", lambda e: e.tensor_scalar(out=CV2.ap[:, HBA:HBA + 16], in0=CV.ap[:, BA:BA + 16], scalar1=0.5,
                                          scalar2=None, op0=ALU.mult), reads=[CV.buf], writes=[CV2.buf])
    P.op("dve", lambda e: e.tensor_tensor(out=CV2.ap[:, BPS:BPS + 8], in0=CV.ap[:, BP:BP + 8],
                                          in1=CV.ap[:, PSC:PSC + 8], op=ALU.mult), reads=[CV.buf], writes=[CV2.buf])
    P.op("act", lambda e: e.activation(out=CV2.ap[:, CL:CL + 8], in_=CV.ap[:, LAM:LAM + 8], func=AF.Exp, scale=-1.0),
         reads=[CV.buf], writes=[CV2.buf])
    P.op("act", lambda e: e.activation(out=CV2.ap[:, CL:CL + 8], in_=CV2.ap[:, CL:CL + 8], func=AF.Ln, bias=1.0),
         reads=[CV2.buf], writes=[CV2.buf])
    P.op("dve", lambda e: e.tensor_scalar(out=CV2.ap[:, HCL:HCL + 8], in0=CV2.ap[:, CL:CL + 8], scalar1=-4.0,
                                          scalar2=None, op0=ALU.mult), reads=[CV2.buf], writes=[CV2.buf])
    P.op("dve", lambda e: e.tensor_scalar(out=CV2.ap[:, CL:CL + 8], in0=CV2.ap[:, CL:CL + 8], scalar1=-8.0,
                                          scalar2=None, op0=ALU.mult), reads=[CV2.buf], writes=[CV2.buf])
    P.op("pool", lambda e: e.iota(INVC.ap, pattern=[[0, 4], [1, 16]], base=1, channel_multiplier=0,
                                  allow_small_or_imprecise_dtypes=True), writes=[INVC.buf])
    for g in range(4):
        P.op("dve", lambda e, g=g: e.tensor_scalar(out=invc_v[:, g, :], in0=invc_v[:, g, :], scalar1=float(2 ** (g + 1)),
                                                   scalar2=None, op0=ALU.min), reads=[INVC.buf], writes=[INVC.buf])
    P.op("dve", lambda e: e.reciprocal(out=INVC.ap, in_=INVC.ap), reads=[INVC.buf], writes=[INVC.buf])

    P.op("dve", lambda e: e.tensor_scalar(out=CV.ap[:, G1:G1 + 8], in0=CV.ap[:, G1:G1 + 8], scalar1=32.0, scalar2=None,
                                          op0=ALU.mult), reads=[CV.buf], writes=[CV.buf])
    P.dma(lambda e: e.dma_start(out=G3.ap, in_=d_final_norm.partition_broadcast(128)), lane=G3.buf, writes=[G3.buf])
    P.op("dve", lambda e: e.tensor_scalar(out=G3.ap, in0=G3.ap, scalar1=32.0, scalar2=None, op0=ALU.mult),
         reads=[G3.buf], writes=[G3.buf])

    stg = [XS[1], XS[2], XR[0], XR[1], XR[2], XS[0]]
    sc = {"i": 0}

    def nextstg():
        s = stg[sc["i"] % len(stg)]
        sc["i"] += 1
        return s

    s1 = nextstg()
    P.dma(lambda e: e.dma_start(out=s1.ap[0:12, :], in_=d_sconv), lane=s1.buf, writes=[s1.buf])
    P.dma(lambda e: e.dma_start(out=s1.ap[12:16, :], in_=d_slru), lane=s1.buf, writes=[s1.buf])
    bk1 = newbank()
    for c in range(8):
        P.op("pe", lambda e, c=c: e.transpose(out=bk1.ap[:, c * 16:(c + 1) * 16], in_=s1.ap[0:16, c * 128:(c + 1) * 128],
                                              identity=IDF.ap[0:16, 0:16]), reads=[s1.buf, IDF.buf], writes=[bk1.buf])
    P.op("dve", lambda e: e.tensor_copy(out=SC16.ap, in_=bk1.ap[:, 0:128]), reads=[bk1.buf], writes=[SC16.buf])
    relbank(bk1)
    s2 = nextstg()
    P.dma(lambda e: e.dma_start(out=s2.ap[0:60, 0:512], in_=d_spool), lane=s2.buf, writes=[s2.buf])
    bk2 = newbank()
    for g in range(4):
        P.op("pe", lambda e, g=g: e.transpose(out=bk2.ap[:, g * 60:(g + 1) * 60], in_=s2.ap[0:60, g * 128:(g + 1) * 128],
                                              identity=IDF.ap[0:60, 0:60]), reads=[s2.buf, IDF.buf], writes=[bk2.buf])
    P.op("dve", lambda e: e.tensor_copy(out=SP60.ap, in_=bk2.ap[:, 0:240]), reads=[bk2.buf], writes=[SP60.buf])
    relbank(bk2)

    for dsrc, dst in ((d_w_rg_a, WA), (d_w_rg_i, WI)):
        s = nextstg()
        sv = s.ap.rearrange("p (c n) -> p c n", c=8)
        q = dsrc[0].rearrange("(c two) i j -> two i c j", two=2)
        P.op("pool", lambda e, s=s: e.memset(s.ap, 0.0), writes=[s.buf])
        P.dma(lambda e, sv=sv, q=q: e.dma_start(out=sv[0:64, :, 0:64], in_=q[0]), lane=s.buf, writes=[s.buf])
        P.dma(lambda e, sv=sv, q=q: e.dma_start(out=sv[64:128, :, 64:128], in_=q[1]), lane=s.buf, writes=[s.buf])
        P.op("dve", lambda e, s=s, dst=dst: e.tensor_copy(out=dst.ap, in_=s.ap), reads=[s.buf], writes=[dst.buf])
    sa_ = nextstg()
    sb_ = nextstg()
    P.dma(lambda e: e.dma_start(out=sa_.ap.rearrange("p (g j) -> p g j", g=4),
                                in_=d_w_pool[0].rearrange("g p j -> p g j")), lane=sa_.buf, writes=[sa_.buf])
    P.dma(lambda e: e.dma_start(out=sb_.ap, in_=d_pool_scale[0].partition_broadcast(128)), lane=sb_.buf, writes=[sb_.buf])
    P.op("dve", lambda e: e.tensor_tensor(out=WPOOL.ap, in0=sa_.ap, in1=sb_.ap, op=ALU.mult),
         reads=[sa_.buf, sb_.buf], writes=[WPOOL.buf])
    deferred_w = []
    xr_i = {"i": 0}

    def w_unit(dsrc, kc, dst_v, dstbuf, scal, post=1.0):
        def emit():
            s_ = XR[xr_i["i"] % 3]
            xr_i["i"] += 1
            P.dma(lambda e: e.dma_start(out=s_.ap, in_=dsrc[0][kc * 128:(kc + 1) * 128, :]), lane=s_.buf,
                  writes=[s_.buf])
            rd = [s_.buf] + ([CV.buf] if not isinstance(scal, float) else [])
            P.op(anyeng(), lambda e: e.tensor_scalar(out=dst_v[:, kc, :], in0=s_.ap, scalar1=scal, scalar2=post,
                                                     op0=ALU.mult, op1=ALU.mult), reads=rd, writes=[dstbuf])
        return emit

    for kc in range(8):
        deferred_w.append(w_unit(d_w_out, kc, wout_v, WOUT.buf, 0.25))
    for kc in range(8):
        deferred_w.append(w_unit(d_w_pg, kc, wpg_v, WPG.buf, CV.ap[:, G2 + kc:G2 + kc + 1], 16.0))
    for kc in range(2):
        deferred_w.append(w_unit(d_w_ple, kc, wpe_v, WPE.buf, 0.5))

    g1b = CV.ap[:, G1:G1 + 8].unsqueeze(2).to_broadcast([128, 8, 128])
    STGC = [(T(XR[i].ap.rearrange("p (k n) -> p k n", k=8), XR[i].buf), []) for i in range(3)]
    STGC.append((T(TGC[0].ap.rearrange("p (k n) -> p k n", k=8), TGC[0].buf), []))
    mg1f = MG[1].ap.bitcast(F32)
    for h in range(2):
        STGC.append((T(mg1f[:, h * 1024:(h + 1) * 1024].rearrange("p (k n) -> p k n", k=8), P.buf("mg1st%d" % h)),
                     [MG[1].buf]))
    cu = {"loaded": 0, "done": 0}
    chunks_done = set()

    def chunk_load_next():
        i = cu["loaded"]
        m = W_ORDER[i]
        hb, own = STGC[i % len(STGC)]
        P.dma(lambda e: e.dma_start(out=hb.ap, in_=d_w_in[0][:, m * 128:(m + 1) * 128].rearrange(
            "(k p) n -> p k n", p=128)), lane=hb.buf, writes=[hb.buf])
        cu["loaded"] += 1

    def chunk_compute_next():
        i = cu["done"]
        m = W_ORDER[i]
        while cu["loaded"] <= i:
            chunk_load_next()
        hb, own = STGC[i % len(STGC)]
        slot = RING[i % RING_N]
        P.op("dve", lambda e: e.tensor_tensor(out=slot.ap.rearrange("p (k n) -> p k n", k=8), in0=hb.ap, in1=g1b,
                                              op=ALU.mult), reads=[hb.buf, CV.buf] + own, writes=[slot.buf])
        P.dma(lambda e: e.dma_start(out=d_wsc[m], in_=slot.ap), lane=slot.buf, reads=[slot.buf], writes=[wsc_buf[m]])
        cu["done"] += 1
        chunks_done.add(m)
        if cu["loaded"] < NM:
            chunk_load_next()

    def ensure_chunk(m):
        while m not in chunks_done:
            chunk_compute_next()

    tiles = []
    for seq in range(2):
        for st in range(4):
            tiles.append(dict(kind="p", seq=seq, st=st, nseg=1, L=512, N=512, nt=4, tok0=seq * 2048 + st * 512,
                              first=(st == 0), last=(st == 3), idx=len(tiles)))
    tiles.append(dict(kind="s", seq=0, st=0, nseg=4, L=32, N=128, nt=1, tok0=0, first=False, last=True,
                      idx=len(tiles)))
    NST = len(tiles)

    stream = {"q_loaded": 0, "q_used": 0}
    total_q = NM * NST

    def ring_load(q):
        m = W_ORDER[q % NM]
        if q < NM:
            assert cu["done"] == q
            chunk_compute_next()
            return
        slot = RING[q % RING_N]
        P.dma(lambda e, slot=slot, m=m: e.dma_start(out=slot.ap, in_=d_wsc[m]), lane=slot.buf,
              reads=[wsc_buf[m]], writes=[slot.buf])

    def win_get(m):
        q = stream["q_used"]
        assert W_ORDER[q % NM] == m, (q, m)
        while stream["q_loaded"] < min(total_q, q + RING_N):
            ring_load(stream["q_loaded"])
            stream["q_loaded"] += 1
        stream["q_used"] += 1
        return RING[q % RING_N]

    a_tiles = []
    for S in tiles:
        for j in range(S["nt"]):
            a_tiles.append((S, j))
    NAT = len(a_tiles)

    def x_src(S, j):
        if S["kind"] == "p":
            r0 = S["tok0"] + j * 128
            return d_xp[r0:r0 + 128, :]
        return d_xs[:, :]

    def p_src(S, j):
        if S["kind"] == "p":
            r0 = S["tok0"] + j * 128
            return d_pp[r0:r0 + 128, :]
        return d_ps[:, :]

    def y_dst(S, j):
        if S["kind"] == "p":
            r0 = S["tok0"] + j * 128
            return o_yp[r0:r0 + 128, :]
        return o_ys[:, :]

    aload = {"n": 0}

    def a_ensure(i):
        while aload["n"] <= min(i, NAT - 1):
            k = aload["n"]
            S, j = a_tiles[k]
            slot = XS[k % 3]
            P.dma(lambda e, slot=slot, src=x_src(S, j): e.dma_start(out=slot.ap, in_=src), lane=slot.buf,
                  writes=[slot.buf])
            aload["n"] += 1

    cload = {"n": 0}

    def c_ensure(i):
        while cload["n"] <= min(i, NAT - 1):
            k = cload["n"]
            S, j = a_tiles[k]
            slot = XR[k % 3]
            P.dma(lambda e, slot=slot, src=x_src(S, j): e.dma_start(out=slot.ap, in_=src), lane=slot.buf,
                  writes=[slot.buf])
            ps_ = P32[k % 2]
            P.dma(lambda e, ps_=ps_, src=p_src(S, j): e.dma_start(out=ps_.ap, in_=src), lane=ps_.buf,
                  writes=[ps_.buf])
            cload["n"] += 1

    def rstd_ops(st):
        P.op("dve", lambda e: e.tensor_scalar(out=st.ap[:, 2:3], in0=st.ap[:, 0:1], scalar1=float(D) * EPS,
                                              scalar2=None, op0=ALU.add), reads=[st.buf], writes=[st.buf])
        P.op("pool", lambda e: e.tensor_tensor(out=st.ap[:, 1:2], in0=st.ap[:, 2:3], in1=NEGH.ap, op=ALU.pow),
             reads=[st.buf, NEGH.buf], writes=[st.buf])

    a_stat = {}

    def phase_a1(k):
        S, j = a_tiles[k]
        a_ensure(k + 2)
        x = XS[k % 3]
        st = newstat()
        u = UB[k % 2]
        P.op("act", lambda e: e.activation(out=JUNK.ap, in_=x.ap, func=AF.Square, accum_out=st.ap[:, 0:1]),
             reads=[x.buf], writes=[st.buf, JUNK.buf])
        a_stat[k] = st

    def phase_a1r(k):
        rstd_ops(a_stat[k])

    def phase_a1b(k):
        x = XS[k % 3]
        u = UB[k % 2]
        st = a_stat.pop(k)
        P.op("act", lambda e: e.activation(out=u.ap, in_=x.ap, func=AF.Identity, scale=st.ap[:, 1:2]),
             reads=[x.buf, st.buf], writes=[u.buf])

    def phase_a2(k):
        S, j = a_tiles[k]
        u = UB[k % 2]
        utb = UT[S["idx"] % 2]
        N = S["N"]
        ut_v = utb.ap[:, 0:8 * N].rearrange("p (k n) -> p k n", k=8)
        tb = newtbank()
        tbv = tb.ap.bitcast(BF16)
        for c in range(8):
            P.op("pe", lambda e, c=c: e.transpose(out=tbv[:, c * 128:(c + 1) * 128], in_=u.ap[:, c * 128:(c + 1) * 128],
                                                  identity=IDB.ap), reads=[u.buf, IDB.buf], writes=[tb.buf])
        P.op("act", lambda e: e.activation(out=ut_v[:, :, j * 128:(j + 1) * 128],
                                           in_=tbv.rearrange("p (k n) -> p k n", k=8), func=AF.Copy),
             reads=[tb.buf], writes=[utb.buf])

    stepc = {"n": 0}

    def seg3(ap2, nseg):
        return ap2.rearrange("p (s l) -> p s l", s=nseg)

    def mm_group(bank, N, wslot, utb, ut_v):
        wv = wslot.ap.rearrange("p (k n) -> p k n", k=8)
        for kc in range(8):
            P.op("pe", lambda e, kc=kc: e.matmul(bank.ap[:, 0:N], lhsT=wv[:, kc, :], rhs=ut_v[:, kc, :],
                                                 start=(kc == 0), stop=(kc == 7)),
                 reads=[wslot.buf, utb.buf], writes=[bank.buf])

    carry = {}
    deferred_ops = []
    deferred_ops2 = []

    def flush_deferred():
        while deferred_ops:
            deferred_ops.pop(0)()
        while deferred_ops2:
            deferred_ops.append(deferred_ops2.pop(0))

    def views(S):
        N = S["N"]
        utb = UT[S["idx"] % 2]
        ut_v = utb.ap[:, 0:8 * N].rearrange("p (k n) -> p k n", k=8)
        mgb = MG[S["idx"] % 2]
        mg_v = mgb.ap[:, 0:8 * N].rearrange("p (k n) -> p k n", k=8)
        return utb, ut_v, mgb, mg_v

    def lru_p1(S, c):
        nseg, L, N = S["nseg"], S["L"], S["N"]
        par = c % 2
        utb, ut_v, mgb, mg_v = views(S)
        sample = S["kind"] == "s"
        b_xl, b_gl, b_ml = newbank(), newbank(), newbank()
        for bank, m in ((b_xl, c), (b_gl, 8 + c), (b_ml, 28 + c)):
            mm_group(bank, N, win_get(m), utb, ut_v)
        xlw = XLW[par]
        xlw_v = xlw.ap[:, 0:nseg * (3 + L)].rearrange("p (s w) -> p s w", s=nseg)
        if S["first"]:
            P.op("pool", lambda e: e.memset(xlw_v[:, :, 0:3], 0.0), writes=[XLWH[par]])
        elif sample:
            P.op("pool", lambda e: e.tensor_copy(out=xlw_v[:, :, 0:3],
                                                 in_=sc16_v[:, c, 0:12].rearrange("p (s r) -> p s r", r=3)),
                 reads=[SC16.buf], writes=[XLWH[par]])
        else:
            P.op("pool", lambda e: e.tensor_copy(out=xlw_v[:, :, 0:3], in_=CONVH[c].ap.unsqueeze(1)),
                 reads=[CONVH[c].buf], writes=[XLWH[par]])
        P.op("act", lambda e: e.activation(out=xlw_v[:, :, 3:3 + L], in_=seg3(b_xl.ap[:, 0:N], nseg), func=AF.Copy),
             reads=[b_xl.buf], writes=[xlw.buf])
        acc = newwork()
        acc2 = acc.ap[:, 0:N]
        acc_v = seg3(acc2, nseg)
        P.op("act", lambda e: e.activation(out=acc2, in_=b_xl.ap[:, 0:N], func=AF.Identity,
                                           scale=CV.ap[:, CW + 24 + c:CW + 24 + c + 1], bias=CV.ap[:, CB + c:CB + c + 1]),
             reads=[b_xl.buf, CV.buf], writes=[acc.buf])
        relbank(b_xl)
        tg, tm = newwork(), newwork()
        P.op("act", lambda e: e.activation(out=tg.ap[:, 0:N], in_=b_gl.ap[:, 0:N], func=AF.Tanh, scale=0.5),
             reads=[b_gl.buf], writes=[tg.buf])
        P.op("act", lambda e: e.activation(out=tm.ap[:, 0:N], in_=b_ml.ap[:, 0:N], func=AF.Tanh, scale=0.5),
             reads=[b_ml.buf], writes=[tm.buf])
        relbank(b_ml)
        if not S["last"]:
            P.op("pool", lambda e: e.tensor_copy(out=CONVH[c].ap.unsqueeze(1), in_=xlw_v[:, :, L:L + 3]),
                 reads=[xlw.buf], writes=[CONVH[c].buf])
        else:
            P.op("pool", lambda e: e.tensor_copy(
                out=st_v[:, 0:nseg, 0:24].rearrange("p s (r c) -> p s r c", c=8)[:, :, :, c],
                in_=xlw_v[:, :, L:L + 3]), reads=[xlw.buf], writes=[ST.buf])
        xcb = XCB[par]

        def later():
            P.op("dve", lambda e: e.scalar_tensor_tensor(out=tg.ap[:, 0:N], in0=tg.ap[:, 0:N], scalar=1.0,
                                                         in1=b_gl.ap[:, 0:N], op0=ALU.add, op1=ALU.mult),
                 reads=[tg.buf, b_gl.buf], writes=[tg.buf])
            relbank(b_gl)
            for j in (1, 2, 3):
                k = 3 - j
                P.op("dve", lambda e, j=j, k=k: e.scalar_tensor_tensor(
                    out=acc_v, in0=xlw_v[:, :, 3 - j:3 - j + L], scalar=CV.ap[:, CW + k * 8 + c:CW + k * 8 + c + 1],
                    in1=acc_v, op0=ALU.mult, op1=ALU.add), reads=[xlw.buf, XLWH[par], acc.buf, CV.buf],
                    writes=[acc.buf])
            P.op("dve", lambda e: e.tensor_copy(out=xcb.ap[:, 0:N], in_=acc2), reads=[acc.buf], writes=[xcb.buf])
            P.op("dve", lambda e: e.scalar_tensor_tensor(out=tm.ap[:, 0:N], in0=tm.ap[:, 0:N], scalar=1.0,
                                                         in1=tg.ap[:, 0:N], op0=ALU.add, op1=ALU.mult),
                 reads=[tm.buf, tg.buf], writes=[tm.buf])
            relwork(tg)
        deferred_ops.append(later)
        carry[("L", c)] = (acc, tm, xcb)

    def lru_p2(S, c):
        nseg, L, N = S["nseg"], S["L"], S["N"]
        utb, ut_v, mgb, mg_v = views(S)
        sample = S["kind"] == "s"
        acc, tm, xcb = carry.pop(("L", c))
        acc2 = acc.ap[:, 0:N]
        b_r, b_i = newbank(), newbank()
        P.op("pe", lambda e: e.matmul(b_r.ap[:, 0:N], lhsT=wa_v[:, c, :], rhs=xcb.ap[:, 0:N], start=True, stop=True),
             reads=[xcb.buf, WA.buf], writes=[b_r.buf])
        P.op("pe", lambda e: e.matmul(b_i.ap[:, 0:N], lhsT=wi_v[:, c, :], rhs=xcb.ap[:, 0:N], start=True, stop=True),
             reads=[xcb.buf, WI.buf], writes=[b_i.buf])
        tr, ti, aa, a2 = (newwork() for _ in range(4))
        P.op("act", lambda e: e.activation(out=tr.ap[:, 0:N], in_=b_r.ap[:, 0:N], func=AF.Tanh, scale=0.5,
                                           bias=CV2.ap[:, HBA + c:HBA + c + 1]), reads=[b_r.buf, CV2.buf], writes=[tr.buf])
        P.op("act", lambda e: e.activation(out=ti.ap[:, 0:N], in_=b_i.ap[:, 0:N], func=AF.Tanh, scale=0.5,
                                           bias=CV2.ap[:, HBI + c:HBI + c + 1]), reads=[b_i.buf, CV2.buf], writes=[ti.buf])
        relbank(b_r)
        relbank(b_i)
        P.op("act", lambda e: e.activation(out=aa.ap[:, 0:N], in_=tr.ap[:, 0:N], func=AF.Exp,
                                           scale=CV2.ap[:, HCL + c:HCL + c + 1], bias=CV2.ap[:, HCL + c:HCL + c + 1]),
             reads=[tr.buf, CV2.buf], writes=[aa.buf])
        P.op("act", lambda e: e.activation(out=a2.ap[:, 0:N], in_=tr.ap[:, 0:N], func=AF.Exp,
                                           scale=CV2.ap[:, CL + c:CL + c + 1], bias=CV2.ap[:, CL + c:CL + c + 1]),
             reads=[tr.buf, CV2.buf], writes=[a2.buf])
        relwork(tr)

        def later():
            P.op("dve", lambda e: e.tensor_scalar(out=a2.ap[:, 0:N], in0=a2.ap[:, 0:N], scalar1=1.0 - 2.0 ** -24,
                                                  scalar2=None, op0=ALU.min), reads=[a2.buf], writes=[a2.buf])
            P.op("dve", lambda e: e.scalar_tensor_tensor(out=ti.ap[:, 0:N], in0=ti.ap[:, 0:N], scalar=1.0, in1=acc2,
                                                         op0=ALU.add, op1=ALU.mult), reads=[ti.buf, acc.buf],
                 writes=[ti.buf])
            relwork(acc)
        deferred_ops.append(later)
        carry[("M", c)] = (ti, aa, a2, tm)

    def lru_p2b(S, c):
        nseg, L, N = S["nseg"], S["L"], S["N"]
        utb, ut_v, mgb, mg_v = views(S)
        sample = S["kind"] == "s"
        ti, aa, a2, tm = carry.pop(("M", c))
        hh = newwork()
        P.op("act", lambda e: e.activation(out=a2.ap[:, 0:N], in_=a2.ap[:, 0:N], func=AF.Sqrt, scale=-0.25, bias=0.25),
             reads=[a2.buf], writes=[a2.buf])
        deferred_ops.append(lambda: lru_p2c(S, c, ti, aa, a2, tm, hh))

    def lru_p2c(S, c, ti, aa, a2, tm, hh):
        nseg, L, N = S["nseg"], S["L"], S["N"]
        utb, ut_v, mgb, mg_v = views(S)
        sample = S["kind"] == "s"
        if S["first"]:
            P.op("dve", lambda e: e.memset(a2.ap[:, 0:1], 0.5), reads=[a2.buf], writes=[a2.buf])
        P.op("dve", lambda e: e.tensor_tensor(out=ti.ap[:, 0:N], in0=ti.ap[:, 0:N], in1=a2.ap[:, 0:N], op=ALU.mult),
             reads=[ti.buf, a2.buf], writes=[ti.buf])
        relwork(a2)
        for sg in range(nseg):
            if S["first"]:
                init, rb = 0.0, []
            elif sample:
                init, rb = sc16_v[:, c, 12 + sg:13 + sg], [SC16.buf]
            else:
                init, rb = HST[c].ap, [HST[c].buf]
            P.op("dve", lambda e, sg=sg, init=init: e.tensor_tensor_scan(
                out=hh.ap[:, sg * L:(sg + 1) * L], data0=aa.ap[:, sg * L:(sg + 1) * L],
                data1=ti.ap[:, sg * L:(sg + 1) * L], initial=init, op0=ALU.mult, op1=ALU.add),
                reads=[aa.buf, ti.buf] + rb, writes=[hh.buf])
        relwork(ti, aa)
        hh_v = seg3(hh.ap[:, 0:N], nseg)
        if not S["last"]:
            P.op("pool", lambda e: e.tensor_copy(out=HST[c].ap, in_=hh.ap[:, N - 1:N]), reads=[hh.buf],
                 writes=[HST[c].buf])
        else:
            P.op("pool", lambda e: e.tensor_copy(out=st_v[:, 0:nseg, 84 + c:85 + c], in_=hh_v[:, :, L - 1:L]),
                 reads=[hh.buf], writes=[ST.buf])
        P.op("pool", lambda e: e.tensor_tensor(out=mg_v[:, c, :], in0=hh.ap[:, 0:N], in1=tm.ap[:, 0:N], op=ALU.mult),
             reads=[hh.buf, tm.buf], writes=[mgb.buf])
        relwork(hh, tm)

    def pool_p1(S, g):
        nseg, L, N = S["nseg"], S["L"], S["N"]
        par = g % 2
        W = 15 + L
        utb, ut_v, mgb, mg_v = views(S)
        sample = S["kind"] == "s"
        wnd = 2 ** (g + 1)
        b_xp = newbank()
        mm_group(b_xp, N, win_get(16 + g), utb, ut_v)
        xpw = XPW[par]
        xpw_v = xpw.ap[:, 0:nseg * W].rearrange("p (s w) -> p s w", s=nseg)
        if S["first"]:
            P.op("pool", lambda e: e.memset(xpw_v[:, :, 0:15], 0.0), writes=[XPWH[par]])
        elif sample:
            P.op("pool", lambda e: e.tensor_copy(out=xpw_v[:, :, 0:15],
                                                 in_=sp60_v[:, g, :].rearrange("p (s r) -> p s r", r=15)),
                 reads=[SP60.buf], writes=[XPWH[par]])
        else:
            P.op("pool", lambda e: e.tensor_copy(out=xpw_v[:, :, 0:15], in_=POOLH[g].ap.unsqueeze(1)),
                 reads=[POOLH[g].buf], writes=[XPWH[par]])
        P.op("act", lambda e: e.activation(out=xpw_v[:, :, 15:W], in_=seg3(b_xp.ap[:, 0:N], nseg), func=AF.Copy),
             reads=[b_xp.buf], writes=[xpw.buf])
        relbank(b_xp)
        src, src_v = xpw, xpw_v
        for i in range(g + 1):
            sh = 2 ** i
            lo = 2 ** (i + 1) - 1
            dst = (SA, SBB)[i % 2]
            dst_v = dst.ap[:, 0:nseg * W].rearrange("p (s w) -> p s w", s=nseg)
            P.op("pool", lambda e, dst_v=dst_v, src_v=src_v, lo=lo, sh=sh: e.tensor_tensor(
                out=dst_v[:, :, lo:W], in0=src_v[:, :, lo:W], in1=src_v[:, :, lo - sh:W - sh], op=ALU.add),
                reads=[src.buf, XPWH[par]], writes=[dst.buf])
            src, src_v = dst, dst_v
        if not S["last"]:
            P.op("pool", lambda e: e.tensor_copy(out=POOLH[g].ap.unsqueeze(1), in_=xpw_v[:, :, L:L + 15]),
                 reads=[xpw.buf], writes=[POOLH[g].buf])
        else:
            P.op("pool", lambda e: e.tensor_copy(
                out=st_v[:, 0:nseg, 24:84].rearrange("p s (r g) -> p s r g", g=4)[:, :, :, g],
                in_=xpw_v[:, :, L:L + 15]), reads=[xpw.buf], writes=[ST.buf])
        pl = PL[par]
        pl_v = seg3(pl.ap[:, 0:N], nseg)
        wsum_v = src_v[:, :, 15:W]

        def later():
            P.op("dve", lambda e: e.scalar_tensor_tensor(out=pl_v, in0=wsum_v, scalar=1.0 / wnd, in1=xpw_v[:, :, 15:W],
                                                         op0=ALU.mult, op1=ALU.subtract),
                 reads=[src.buf, xpw.buf], writes=[pl.buf])
            if S["first"]:
                P.op("dve", lambda e: e.tensor_tensor(out=T16.ap, in0=src_v[:, 0, 15:31], in1=invc_v[:, g, :],
                                                      op=ALU.mult), reads=[src.buf, INVC.buf], writes=[T16.buf])
                P.op("dve", lambda e: e.tensor_tensor(out=pl.ap[:, 0:16], in0=T16.ap, in1=xpw_v[:, 0, 15:31],
                                                      op=ALU.subtract), reads=[T16.buf, xpw.buf, pl.buf], writes=[pl.buf])
        (deferred_ops2 if g < 3 else deferred_ops).append(later)

    def pool_p2(S, g):
        nseg, L, N = S["nseg"], S["L"], S["N"]
        utb, ut_v, mgb, mg_v = views(S)
        pl = PL[g % 2]
        for hf in range(2):
            cc = 2 * g + hf
            b_gp, b_mp = newbank(), newbank()
            mm_group(b_gp, N, win_get(20 + cc), utb, ut_v)
            mm_group(b_mp, N, win_get(36 + cc), utb, ut_v)
            b_pg = newbank()
            P.op("pe", lambda e, hf=hf, b_pg=b_pg: e.matmul(b_pg.ap[:, 0:N], lhsT=wpool_v[:, g, hf * 128:(hf + 1) * 128],
                                                            rhs=pl.ap[:, 0:N], start=True, stop=True),
                 reads=[pl.buf, WPOOL.buf], writes=[b_pg.buf])
            tg, tm = newwork(), newwork()
            P.op("act", lambda e, tg=tg, b_gp=b_gp: e.activation(out=tg.ap[:, 0:N], in_=b_gp.ap[:, 0:N], func=AF.Tanh,
                                                                 scale=0.5), reads=[b_gp.buf], writes=[tg.buf])
            P.op("act", lambda e, tm=tm, b_mp=b_mp: e.activation(out=tm.ap[:, 0:N], in_=b_mp.ap[:, 0:N], func=AF.Tanh,
                                                                 scale=0.5), reads=[b_mp.buf], writes=[tm.buf])
            relbank(b_mp)
            P.op("dve", lambda e, tg=tg, b_gp=b_gp: e.scalar_tensor_tensor(
                out=tg.ap[:, 0:N], in0=tg.ap[:, 0:N], scalar=1.0, in1=b_gp.ap[:, 0:N], op0=ALU.add, op1=ALU.mult),
                reads=[tg.buf, b_gp.buf], writes=[tg.buf])
            relbank(b_gp)
            P.op("dve", lambda e, tg=tg, tm=tm: e.scalar_tensor_tensor(
                out=tm.ap[:, 0:N], in0=tm.ap[:, 0:N], scalar=1.0, in1=tg.ap[:, 0:N], op0=ALU.add, op1=ALU.mult),
                reads=[tm.buf, tg.buf], writes=[tm.buf])
            P.op("dve", lambda e, tg=tg, tm=tm, b_pg=b_pg, cc=cc: e.scalar_tensor_tensor(
                out=tg.ap[:, 0:N], in0=b_pg.ap[:, 0:N], scalar=CV2.ap[:, BPS + cc:BPS + cc + 1],
                in1=tm.ap[:, 0:N], op0=ALU.add, op1=ALU.mult), reads=[b_pg.buf, tm.buf, CV2.buf], writes=[tg.buf])
            relbank(b_pg)
            P.op("pool", lambda e, tg=tg, cc=cc: e.tensor_tensor(out=mg_v[:, cc, :], in0=mg_v[:, cc, :],
                                                                 in1=tg.ap[:, 0:N], op=ALU.add),
                 reads=[mgb.buf, tg.buf], writes=[mgb.buf])
            relwork(tg, tm)

    def state_out(S):
        nseg = S["nseg"]
        bk_ = newbank()
        for sg in range(nseg):
            P.op("pe", lambda e, sg=sg: e.transpose(out=bk_.ap[0:92, sg * 128:(sg + 1) * 128], in_=st_v[:, sg, :],
                                                    identity=IDF.ap), reads=[ST.buf, IDF.buf], writes=[bk_.buf])
        P.op("act", lambda e: e.activation(out=STO.ap[0:92, 0:nseg * 128], in_=bk_.ap[0:92, 0:nseg * 128], func=AF.Copy),
             reads=[bk_.buf], writes=[STO.buf])
        relbank(bk_)
        for sg in range(nseg):
            if S["kind"] == "p":
                oc, ol, op_ = o_ncp[S["seq"]], o_nlp[S["seq"]], o_npp[S["seq"]]
            else:
                oc, ol, op_ = o_ncs[sg], o_nls[sg], o_nps[sg]
            P.dma(lambda e, sg=sg, oc=oc: e.dma_start(out=oc.rearrange("r (c p) -> (r c) p", p=128),
                                                      in_=STO.ap[0:24, sg * 128:(sg + 1) * 128]),
                  lane=STO.buf, reads=[STO.buf], queue="act")
            P.dma(lambda e, sg=sg, op_=op_: e.dma_start(out=op_.rearrange("r (g p) -> (r g) p", p=128),
                                                        in_=STO.ap[24:84, sg * 128:(sg + 1) * 128]),
                  lane=STO.buf, reads=[STO.buf], queue="act")
            P.dma(lambda e, sg=sg, ol=ol: e.dma_start(out=ol.rearrange("(c p) -> c p", p=128),
                                                      in_=STO.ap[84:92, sg * 128:(sg + 1) * 128]),
                  lane=STO.buf, reads=[STO.buf], queue="act")

    def b_task(S, t):
        kind, a = B_TASKS[t]
        flush_deferred()
        {"L1": lru_p1, "L2a": lru_p2, "L2b": lru_p2b, "Q1": pool_p1, "Q2": pool_p2}[kind](S, a)
        if t == len(B_TASKS) - 1:
            flush_deferred()
            flush_deferred()
            if S["last"]:
                state_out(S)

    pending_st = []

    def flush_stores():
        while pending_st:
            pending_st.pop(0)()

    NCS = 8
    cst_state = {}

    def c_stage(k, stage):
        S, j = a_tiles[k]
        N = S["N"]
        mgb = MG[S["idx"] % 2]
        mg_v = mgb.ap[:, 0:8 * N].rearrange("p (k n) -> p k n", k=8)
        xr = XR[k % 3]
        u2 = U2[k % 2]
        u2t = U2T[k % 2]
        p32, pb, pt = P32[k % 2], PB[k % 2], PT[k % 2]
        tg = TGC[0]
        if stage == 0:
            c_ensure(k + 1)
            P.op("pool", lambda e: e.tensor_copy(out=pb.ap, in_=p32.ap), reads=[p32.buf], writes=[pb.buf])
            d = [newbank(), newbank()]
            for hf in range(2):
                for kc in range(8):
                    P.op("pe", lambda e, hf=hf, kc=kc: e.matmul(
                        d[hf].ap, lhsT=mg_v[:, kc, j * 128:(j + 1) * 128], rhs=wout_v[:, kc, hf * 512:(hf + 1) * 512],
                        start=(kc == 0), stop=(kc == 7)), reads=[mgb.buf, WOUT.buf], writes=[d[hf].buf])
            for hf in range(2):
                P.op("dve", lambda e, hf=hf: e.tensor_tensor(out=xr.ap[:, hf * 512:(hf + 1) * 512],
                                                             in0=xr.ap[:, hf * 512:(hf + 1) * 512], in1=d[hf].ap,
                                                             op=ALU.add), reads=[xr.buf, d[hf].buf], writes=[xr.buf])
            relbank(d[0])
            relbank(d[1])
            P.op("dve", lambda e: e.tensor_copy(out=u2.ap, in_=xr.ap), reads=[xr.buf], writes=[u2.buf])
        elif stage == 1:
            st = newstat()
            P.op("act", lambda e: e.activation(out=JUNK.ap, in_=xr.ap, func=AF.Square, accum_out=st.ap[:, 0:1]),
                 reads=[xr.buf], writes=[st.buf, JUNK.buf])
            cst_state[k] = st
        elif stage == 2:
            rstd_ops(cst_state[k])
            tb = newtbank()
            tbv = tb.ap.bitcast(BF16)
            for c in range(8):
                P.op("pe", lambda e, c=c: e.transpose(out=tbv[:, c * 128:(c + 1) * 128],
                                                      in_=u2.ap[:, c * 128:(c + 1) * 128], identity=IDB.ap),
                     reads=[u2.buf, IDB.buf], writes=[tb.buf])
            P.op("act", lambda e: e.activation(out=u2t.ap, in_=tbv, func=AF.Copy), reads=[tb.buf], writes=[u2t.buf])
            tb2 = newtbank()
            tb2v = tb2.ap.bitcast(BF16)
            for c in range(2):
                P.op("pe", lambda e, c=c: e.transpose(out=tb2v[:, c * 128:(c + 1) * 128],
                                                      in_=pb.ap[:, c * 128:(c + 1) * 128], identity=IDB.ap),
                     reads=[pb.buf, IDB.buf], writes=[tb2.buf])
            P.op("act", lambda e: e.activation(out=pt.ap, in_=tb2v[:, 0:256], func=AF.Copy), reads=[tb2.buf],
                 writes=[pt.buf])
        elif stage == 3:
            u2t_v = u2t.ap.rearrange("p (k n) -> p k n", k=8)
            eb = [newbank(), newbank()]
            for hf in range(2):
                for kc in range(8):
                    P.op("pe", lambda e, hf=hf, kc=kc: e.matmul(
                        eb[hf].ap, lhsT=u2t_v[:, kc, :], rhs=wpg_v[:, kc, hf * 512:(hf + 1) * 512],
                        start=(kc == 0), stop=(kc == 7)), reads=[u2t.buf, WPG.buf], writes=[eb[hf].buf])
            st = cst_state.pop(k)
            for hf in range(2):
                P.op("act", lambda e, hf=hf: e.activation(out=tg.ap[:, hf * 512:(hf + 1) * 512], in_=eb[hf].ap,
                                                          func=AF.Tanh, scale=st.ap[:, 1:2]),
                     reads=[eb[hf].buf, st.buf], writes=[tg.buf])
            relbank(eb[0])
            relbank(eb[1])
        elif stage == 4:
            pt_v = pt.ap.rearrange("p (k n) -> p k n", k=2)
            fbk = [newbank(), newbank()]
            for hf in range(2):
                for kc in range(2):
                    P.op("pe", lambda e, hf=hf, kc=kc: e.matmul(
                        fbk[hf].ap, lhsT=pt_v[:, kc, :], rhs=wpe_v[:, kc, hf * 512:(hf + 1) * 512],
                        start=(kc == 0), stop=(kc == 1)), reads=[pt.buf, WPE.buf], writes=[fbk[hf].buf])
            for hf in range(2):
                P.op("dve", lambda e, hf=hf: e.scalar_tensor_tensor(
                    out=tg.ap[:, hf * 512:(hf + 1) * 512], in0=tg.ap[:, hf * 512:(hf + 1) * 512], scalar=1.0,
                    in1=fbk[hf].ap, op0=ALU.add, op1=ALU.mult), reads=[tg.buf, fbk[hf].buf], writes=[tg.buf])
            relbank(fbk[0])
            relbank(fbk[1])
            P.op("pool", lambda e: e.tensor_tensor(out=xr.ap, in0=xr.ap, in1=tg.ap, op=ALU.add),
                 reads=[xr.buf, tg.buf], writes=[xr.buf])
        elif stage == 5:
            st = newstat()
            P.op("act", lambda e: e.activation(out=JUNK.ap, in_=xr.ap, func=AF.Square, accum_out=st.ap[:, 0:1]),
                 reads=[xr.buf], writes=[st.buf, JUNK.buf])
            cst_state[("y", k)] = st
        elif stage == 6:
            rstd_ops(cst_state[("y", k)])
        elif stage == 7:
            st = cst_state.pop(("y", k))
            P.op("dve", lambda e: e.scalar_tensor_tensor(out=xr.ap, in0=xr.ap, scalar=st.ap[:, 1:2], in1=G3.ap,
                                                         op0=ALU.mult, op1=ALU.mult),
                 reads=[xr.buf, st.buf, G3.buf], writes=[xr.buf])
            pending_st.append(lambda dst=y_dst(S, j), xr=xr: P.dma(
                lambda e: e.dma_start(out=dst, in_=xr.ap), lane=xr.buf, reads=[xr.buf], queue="act"))

    first_tile = {}
    k0 = 0
    for S in tiles:
        first_tile[S["idx"]] = k0
        k0 += S["nt"]
    NT = len(B_TASKS)
    a_ensure(1)
    for j in range(tiles[0]["nt"]):
        phase_a1(first_tile[0] + j)
        phase_a1r(first_tile[0] + j)
        phase_a1b(first_tile[0] + j)
        if j >= 1:
            phase_a2(first_tile[0] + j - 1)
    phase_a2(first_tile[0] + tiles[0]["nt"] - 1)
    for _ in range(len(STGC)):
        chunk_load_next()
    for r in range(NST + 1):
        Sb = tiles[r] if r < NST else None
        Sc = tiles[r - 1] if r >= 1 else None
        Sa = tiles[r + 1] if r + 1 < NST else None
        cslot = {}
        if Sc is not None:
            ntc = Sc["nt"]
            for j in range(ntc):
                base = (j * NT) // ntc
                offs = (0, 1, 3, 5, 6, 8, 9, 11) if ntc > 1 else (0, 4, 8, 12, 16, 20, 24, 28)
                for stg_ in range(NCS):
                    cslot.setdefault(min(NT - 1, base + offs[stg_]), []).append((first_tile[Sc["idx"]] + j, stg_))
        ast = []
        if Sa is not None:
            ast = [first_tile[Sa["idx"]] + j for j in range(Sa["nt"])]
        for t in range(NT):
            if r == 0:
                if cu["done"] >= NM and deferred_w:
                    deferred_w.pop(0)()
                    if deferred_w:
                        deferred_w.pop(0)()
            if Sb is not None:
                b_task(Sb, t)
            flush_stores()
            for (kk, stg_) in cslot.get(t, []):
                c_stage(kk, stg_)
            nA = len(ast)
            for i_, kk in enumerate(ast):
                t2 = ((i_ + 1) * NT) // nA - 1
                if t == max(0, t2 - 7):
                    phase_a1(kk)
                if t == max(1, t2 - 5):
                    phase_a1r(kk)
                if t == max(2, t2 - 3):
                    phase_a1b(kk)
                if t == t2:
                    phase_a2(kk)
        if r == 0:
            assert cu["done"] == NM
            while deferred_w:
                deferred_w.pop(0)()
            c_ensure(0)

    flush_stores()
    P.finalize()
    es.close()
    return nc, P


_CACHE = {}


def _get_program():
    if "nc" not in _CACHE:
        nc, P = build_program()
        _CACHE["nc"] = nc
        _CACHE["stats"] = P.stats
    return _CACHE["nc"]


def kernel(x_prompt, x_sample, p_prompt, p_sample, state_conv, state_lru, state_pool,
           norm_mix, w_in, conv_w, conv_b, w_rg_a, b_rg_a, w_rg_i, b_rg_i, lru_lambda,
           w_pool, b_pool, pool_scale, w_out, norm_ple, w_ple_gate, w_ple, final_norm):
    f = lambda a: np.ascontiguousarray(np.asarray(a, dtype=np.float32))
    x_prompt, x_sample, p_prompt, p_sample = f(x_prompt), f(x_sample), f(p_prompt), f(p_sample)
    state_conv, state_lru, state_pool = f(state_conv), f(state_lru), f(state_pool)
    shared = {
        "norm_mix": f(norm_mix), "w_in": f(w_in), "conv_w": f(conv_w), "conv_b": f(conv_b),
        "w_rg_a": f(w_rg_a), "b_rg_a": f(b_rg_a), "w_rg_i": f(w_rg_i), "b_rg_i": f(b_rg_i),
        "lru_lambda": f(lru_lambda), "w_pool": f(w_pool), "b_pool": f(b_pool), "pool_scale": f(pool_scale),
        "w_out": f(w_out), "norm_ple": f(norm_ple), "w_ple_gate": f(w_ple_gate), "w_ple": f(w_ple),
        "final_norm": f(final_norm),
    }
    in_maps = []
    for i in range(NCORES):
        m = dict(shared)
        m["xp"] = np.ascontiguousarray(x_prompt[2 * i:2 * i + 2].reshape(4096, 1024))
        m["pp"] = np.ascontiguousarray(p_prompt[0, 2 * i:2 * i + 2].reshape(4096, 256))
        m["xs"] = np.ascontiguousarray(x_sample[4 * i:4 * i + 4].reshape(128, 1024))
        m["ps"] = np.ascontiguousarray(p_sample[0, 4 * i:4 * i + 4].reshape(128, 256))
        m["sconv"] = np.ascontiguousarray(state_conv[0, 4 * i:4 * i + 4].reshape(12, 1024))
        m["slru"] = np.ascontiguousarray(state_lru[0, 4 * i:4 * i + 4].reshape(4, 1024))
        m["spool"] = np.ascontiguousarray(state_pool[0, 4 * i:4 * i + 4].reshape(60, 512))
        in_maps.append(m)
    nc = _get_program()
    res = run_bass_kernel_spmd(nc, in_maps, core_ids=list(range(NCORES)))
    R = res.results
    y_prompt = np.concatenate([R[i]["yp"].reshape(2, 2048, 1024) for i in range(NCORES)], axis=0)
    y_sample = np.concatenate([R[i]["ys"].reshape(4, 32, 1024) for i in range(NCORES)], axis=0)
    ncp = np.concatenate([R[i]["ncp"] for i in range(NCORES)], axis=0)[None]
    nlp = np.concatenate([R[i]["nlp"] for i in range(NCORES)], axis=0)[None]
    npp = np.concatenate([R[i]["npp"] for i in range(NCORES)], axis=0)[None]
    ncs = np.concatenate([R[i]["ncs"] for i in range(NCORES)], axis=0)[None]
    nls = np.concatenate([R[i]["nls"] for i in range(NCORES)], axis=0)[None]
    nps = np.concatenate([R[i]["nps"] for i in range(NCORES)], axis=0)[None]
    return tuple(np.ascontiguousarray(a.astype(np.float32)) for a in (y_prompt, y_sample, ncp, nlp, npp, ncs, nls, nps))
```

```python
import numpy as np
from contextlib import ExitStack

import concourse.bass as bass
import concourse.mybir as mybir
from concourse.bass_utils import run_bass_kernel_spmd

F32 = mybir.dt.float32
BF16 = mybir.dt.bfloat16
AF = mybir.ActivationFunctionType
ALU = mybir.AluOpType

NCORES = 8
EPS = 1e-6
D = 1024
IN_COLS = 5632
NM = IN_COLS // 128


class Buf:
    __slots__ = ("name", "last_w", "readers", "sem", "cnt")

    def __init__(self, name):
        self.name = name
        self.last_w = None
        self.readers = []
        self.sem = None
        self.cnt = 0


class Op:
    __slots__ = ("eng", "fn", "deps", "needed", "count", "sem", "waits", "knows", "is_dma", "lane", "dbg", "wdbg")

    def __init__(self, eng, fn, is_dma=False, lane=None):
        self.eng = eng
        self.fn = fn
        self.deps = []
        self.needed = is_dma
        self.count = 0
        self.sem = None
        self.waits = []
        self.knows = None
        self.is_dma = is_dma
        self.lane = lane


class Prog:
    ENGS = ("pe", "act", "dve", "pool", "sp")

    def __init__(self, nc, es):
        self.nc = nc
        self.es = es
        self.ops = []
        self.lanes = []

    def buf(self, name):
        return Buf(name)

    def _add(self, o, reads, writes):
        o.dbg = ([b.name for b in reads], [b.name for b in writes])
        o.wdbg = []
        deps = {}
        for b in reads:
            if b.last_w is not None:
                deps[id(b.last_w)] = (b.last_w, "raw")
        for b in writes:
            if b.last_w is not None and id(b.last_w) not in deps:
                deps[id(b.last_w)] = (b.last_w, "waw")
            for r in b.readers:
                if id(r) not in deps:
                    deps[id(r)] = (r, "war")
        for p, kind in deps.values():
            if p is o:
                continue
            if (not p.is_dma) and (not o.is_dma) and p.eng == o.eng:
                if o.eng == "pe":
                    continue
            if p.is_dma and o.is_dma and p.lane is o.lane and kind == "waw":
                continue
            o.deps.append(p)
            p.needed = True
        for b in reads:
            if b in writes:
                continue
            if o.is_dma:
                b.readers = [r for r in b.readers if not (r.is_dma and r.lane is o.lane)]
            else:
                b.readers = [r for r in b.readers if r.is_dma or r.eng != o.eng]
            b.readers.append(o)
        for b in writes:
            b.last_w = o
            b.readers = []
        self.ops.append(o)
        return o

    def op(self, eng, fn, reads=(), writes=()):
        return self._add(Op(eng, fn), list(reads), list(writes))

    def dma(self, fn, lane, reads=(), writes=(), queue="sp"):
        if not any(l is lane for l in self.lanes):
            self.lanes.append(lane)
        o = Op(queue, fn, is_dma=True, lane=lane)
        return self._add(o, list(reads), list(writes))

    def finalize(self):
        nc, es = self.nc, self.es
        engsem = {e: es.enter_context(nc.semaphore("sem_" + e)) for e in ("pe", "act", "dve", "pool")}
        for i, ln in enumerate(self.lanes):
            ln.sem = es.enter_context(nc.semaphore("lane%d" % i))
        engcnt = {e: 0 for e in engsem}
        known = {e: {} for e in self.ENGS}
        per = {e: [] for e in self.ENGS}
        nwaits = 0
        for o in self.ops:
            K = known[o.eng]
            waits = {}
            for p in o.deps:
                sid = id(p.sem)
                if K.get(sid, 0) >= p.count:
                    continue
                if sid not in waits or waits[sid][1] < p.count:
                    waits[sid] = (p.sem, p.count, p)
            for sid, (sem, val, p) in waits.items():
                if p.knows:
                    for k2, v2 in p.knows.items():
                        if K.get(k2, 0) < v2:
                            K[k2] = v2
                if K.get(sid, 0) < val:
                    K[sid] = val
            o.waits = [(sem, val) for sid, (sem, val, p) in waits.items()]
            o.wdbg = [(p.eng, p.dbg) for sid, (sem, val, p) in waits.items()]
            nwaits += len(o.waits)
            if o.is_dma:
                o.lane.cnt += 16
                o.count = o.lane.cnt
                o.sem = o.lane.sem
                o.knows = dict(K)
            elif o.needed:
                engcnt[o.eng] += 1
                o.count = engcnt[o.eng]
                o.sem = engsem[o.eng]
                o.knows = dict(K)
            per[o.eng].append(o)
        self.per = per
        self.stats = {e: len(per[e]) for e in per}
        self.stats["waits"] = nwaits

        def emit(eng_handle, ops, final=False):
            for o in ops:
                for sem, val in o.waits:
                    eng_handle.wait_ge(sem, val)
                ins = o.fn(eng_handle)
                if o.is_dma:
                    ins.then_inc(o.sem, 16)
                elif o.needed:
                    ins.then_inc(o.sem, 1)
            if final:
                for ln in self.lanes:
                    if ln.cnt:
                        eng_handle.wait_ge(ln.sem, ln.cnt)
                for e, s in engsem.items():
                    if engcnt[e]:
                        eng_handle.wait_ge(s, engcnt[e])

        block = es.enter_context(nc.Block())

        @block.sync
        def _(e):
            emit(e, per["sp"], final=True)

        @block.tensor
        def _(e):
            emit(e, per["pe"])

        @block.scalar
        def _(e):
            emit(e, per["act"])

        @block.vector
        def _(e):
            emit(e, per["dve"])

        @block.gpsimd
        def _(e):
            emit(e, per["pool"])


class T:
    def __init__(self, ap, buf):
        self.ap = ap
        self.buf = buf


G1, G2, CW, CB, BA, BI, LAM, BP, PSC = 0, 8, 16, 48, 56, 64, 72, 80, 88
HBA, HBI, CL, HCL, BPS = 0, 8, 16, 24, 32

B_TASKS = [
    ("L1", 0), ("L1", 1), ("L2a", 0), ("Q1", 0), ("L1", 2), ("L2a", 1), ("L2b", 0), ("L2b", 1), ("Q1", 1),
    ("L1", 3), ("L2a", 2), ("Q2", 0), ("L1", 4), ("L2a", 3), ("L2b", 2), ("L2b", 3), ("Q1", 2),
    ("L1", 5), ("L2a", 4), ("Q2", 1), ("L1", 6), ("L2a", 5), ("L2b", 4), ("L2b", 5), ("Q1", 3),
    ("L1", 7), ("L2a", 6), ("Q2", 2), ("L2a", 7), ("L2b", 6), ("L2b", 7), ("Q2", 3),
]
W_ORDER = []
for _t, _a in B_TASKS:
    if _t == "L1":
        W_ORDER += [_a, 8 + _a, 28 + _a]
    elif _t == "Q1":
        W_ORDER += [16 + _a]
    elif _t == "Q2":
        W_ORDER += [20 + 2 * _a, 36 + 2 * _a, 21 + 2 * _a, 37 + 2 * _a]
assert sorted(W_ORDER) == list(range(NM))
PAIR_ORDER = []
for _m in W_ORDER:
    if _m // 2 not in PAIR_ORDER:
        PAIR_ORDER.append(_m // 2)


def build_program(stop=None):
    nc = bass.Bass("TRN2", target_bir_lowering=False)
    es = ExitStack()
    P = Prog(nc, es)

    def din(name, shape, dt=F32):
        return nc.dram_tensor(name, shape, dt, kind="ExternalInput").ap()

    def dout(name, shape):
        return nc.dram_tensor(name, shape, F32, kind="ExternalOutput").ap()

    d_xp = din("xp", [4096, 1024])
    d_pp = din("pp", [4096, 256])
    d_xs = din("xs", [128, 1024])
    d_ps = din("ps", [128, 256])
    d_sconv = din("sconv", [12, 1024])
    d_slru = din("slru", [4, 1024])
    d_spool = din("spool", [60, 512])
    d_norm_mix = din("norm_mix", [1, 1024])
    d_w_in = din("w_in", [1, 1024, IN_COLS])
    d_conv_w = din("conv_w", [1, 4, 1024])
    d_conv_b = din("conv_b", [1, 1024])
    d_w_rg_a = din("w_rg_a", [1, 16, 64, 64])
    d_b_rg_a = din("b_rg_a", [1, 1024])
    d_w_rg_i = din("w_rg_i", [1, 16, 64, 64])
    d_b_rg_i = din("b_rg_i", [1, 1024])
    d_lam = din("lru_lambda", [1, 1024])
    d_w_pool = din("w_pool", [1, 4, 128, 256])
    d_b_pool = din("b_pool", [1, 4, 256])
    d_pool_scale = din("pool_scale", [1, 1024])
    d_w_out = din("w_out", [1, 1024, 1024])
    d_norm_ple = din("norm_ple", [1, 1024])
    d_w_pg = din("w_ple_gate", [1, 1024, 1024])
    d_w_ple = din("w_ple", [1, 256, 1024])
    d_final_norm = din("final_norm", [1024])

    o_yp = dout("yp", [4096, 1024])
    o_ys = dout("ys", [128, 1024])
    o_ncp = dout("ncp", [2, 3, 1024])
    o_nlp = dout("nlp", [2, 1024])
    o_npp = dout("npp", [2, 15, 512])
    o_ncs = dout("ncs", [4, 3, 1024])
    o_nls = dout("nls", [4, 1024])
    o_nps = dout("nps", [4, 15, 512])

    d_wsc = nc.dram_tensor("win_bf16", [NM, 128, 1024], BF16, kind="Internal").ap()
    wsc_buf = [P.buf("wsc%d" % m) for m in range(NM)]

    def sb(name, shape, dt=F32):
        t = es.enter_context(nc.sbuf_tensor(name, shape, dt))
        return T(t[:], P.buf(name))

    WOUT = sb("wout", [128, 8 * 1024], BF16)
    WPG = sb("wpg", [128, 8 * 1024], BF16)
    WPE = sb("wpe", [128, 2 * 1024], BF16)
    WA = sb("wa", [128, 8 * 128], BF16)
    WI = sb("wi", [128, 8 * 128], BF16)
    WPOOL = sb("wpool", [128, 4 * 256], BF16)
    G3 = sb("g3bc", [128, 1024], F32)
    IDF = sb("identf", [128, 128], F32)
    IDB = sb("identb", [128, 128], BF16)
    CV = sb("cvec", [128, 96], F32)
    CV2 = sb("cvec2", [128, 40], F32)
    INVC = sb("invc", [128, 4 * 16], F32)
    NEGH = sb("negh", [128, 1], F32)
    SC16 = sb("sc16", [128, 8 * 16], F32)
    SP60 = sb("sp60", [128, 4 * 60], F32)
    CONVH = [sb("convh%d" % c, [128, 3], F32) for c in range(8)]
    HST = [sb("hst%d" % c, [128, 1], F32) for c in range(8)]
    POOLH = [sb("poolh%d" % g, [128, 15], F32) for g in range(4)]
    ST = sb("stg", [128, 4 * 92], F32)
    STO = sb("sto", [128, 4 * 128], F32)

    wout_v = WOUT.ap.rearrange("p (k n) -> p k n", k=8)
    wpg_v = WPG.ap.rearrange("p (k n) -> p k n", k=8)
    wpe_v = WPE.ap.rearrange("p (k n) -> p k n", k=2)
    wa_v = WA.ap.rearrange("p (k n) -> p k n", k=8)
    wi_v = WI.ap.rearrange("p (k n) -> p k n", k=8)
    wpool_v = WPOOL.ap.rearrange("p (g n) -> p g n", g=4)
    invc_v = INVC.ap.rearrange("p (g n) -> p g n", g=4)
    sc16_v = SC16.ap.rearrange("p (c n) -> p c n", c=8)
    sp60_v = SP60.ap.rearrange("p (g n) -> p g n", g=4)
    st_v = ST.ap.rearrange("p (s n) -> p s n", s=4)

    XS = [sb("xs%d" % i, [128, 1024], F32) for i in range(3)]
    XRT = sb("xrt", [128, 4 * 1024], F32)
    XR = [T(XRT.ap[:, i * 1024:(i + 1) * 1024], P.buf("xr%d" % i)) for i in range(3)]
    UB = [sb("ub%d" % i, [128, 1024], BF16) for i in range(2)]
    U2 = [sb("u2_%d" % i, [128, 1024], BF16) for i in range(2)]
    JUNK = sb("junk", [128, 1024], BF16)
    UT = [sb("ut%d" % i, [128, 8 * 512], BF16) for i in range(2)]
    MG = [sb("mg%d" % i, [128, 8 * 512], BF16) for i in range(2)]
    RING_N = 12
    RING = [sb("wr%d" % i, [128, 1024], BF16) for i in range(8)]
    EXTRA = sb("extra", [128, 4 * 1024], BF16)
    RING += [T(EXTRA.ap[:, i * 1024:(i + 1) * 1024], P.buf("wrx%d" % i)) for i in range(4)]
    NWORK = 18
    WORK = [sb("wk%d" % i, [128, 512], F32) for i in range(NWORK)]
    XLW = [sb("xlw%d" % i, [128, 515], F32) for i in range(2)]
    XPW = [sb("xpw%d" % i, [128, 527], F32) for i in range(2)]
    XLWH = [P.buf("xlwh%d" % i) for i in range(2)]
    XPWH = [P.buf("xpwh%d" % i) for i in range(2)]
    SA = sb("sa", [128, 527], F32)
    SBB = sb("sbb", [128, 527], F32)
    XCB = [sb("xcb%d" % i, [128, 512], BF16) for i in range(2)]
    PL = [sb("pl%d" % i, [128, 512], BF16) for i in range(2)]
    T16 = sb("t16", [128, 16], F32)
    STAT = [sb("stat%d" % i, [128, 4], F32) for i in range(8)]
    U2T = [sb("u2t%d" % i, [128, 8 * 128], BF16) for i in range(2)]
    P32 = [sb("p32_%d" % i, [128, 256], F32) for i in range(2)]
    PB = [sb("pb%d" % i, [128, 256], BF16) for i in range(2)]
    PT = [sb("pt%d" % i, [128, 256], BF16) for i in range(2)]
    TGC = [T(XRT.ap[:, 3072:4096], P.buf("tgc0"))]

    ps_t = es.enter_context(nc.psum_tensor("psum", [128, 8 * 512], F32))
    NFB = 6
    NTB = 8 - NFB
    FB = [T(ps_t[:, i * 512:(i + 1) * 512], P.buf("fb%d" % i)) for i in range(NFB)]
    TB = [T(ps_t[:, i * 512:(i + 1) * 512], P.buf("tb%d" % i)) for i in range(NFB, 8)]
    ctr = {"fb": 0, "tb": 0, "wk": 0, "stat": 0}

    fb_free = list(range(NFB))

    def newbank():
        assert fb_free, "out of PSUM banks"
        return FB[fb_free.pop(0)]

    def relbank(b):
        i = [j for j in range(NFB) if FB[j] is b][0]
        assert i not in fb_free
        fb_free.append(i)

    def newtbank():
        b = TB[ctr["tb"] % len(TB)]
        ctr["tb"] += 1
        return b

    wk_free = list(range(NWORK))

    def newwork():
        assert wk_free, "out of work tiles"
        return WORK[wk_free.pop(0)]

    def relwork(*ts):
        for t_ in ts:
            i = [j for j in range(NWORK) if WORK[j] is t_][0]
            assert i not in wk_free
            wk_free.append(i)

    def newstat():
        b = STAT[ctr["stat"] % len(STAT)]
        ctr["stat"] += 1
        return b

    rr = {"i": 0}

    def anyeng():
        e = "dve"
        rr["i"] += 1
        return e

    P.op("pool", lambda e: e.memset(NEGH.ap, -0.5), writes=[NEGH.buf])
    P.op("pool", lambda e: e.memset(IDF.ap, 1.0), writes=[IDF.buf])
    P.op("pool", lambda e: e.affine_select(out=IDF.ap, in_=IDF.ap, pattern=[[-1, 128]],
                                           compare_op=ALU.is_equal, fill=0.0, base=0,
                                           channel_multiplier=1), reads=[IDF.buf], writes=[IDF.buf])
    P.op("dve", lambda e: e.tensor_copy(out=IDB.ap, in_=IDF.ap), reads=[IDF.buf], writes=[IDB.buf])

    vs = XS[0]
    vec_rows = [
        (d_norm_mix[0], G1, 8), (d_norm_ple[0], G2, 8), (d_conv_b[0], CB, 8),
        (d_b_rg_a[0], BA, 8), (d_b_rg_i[0], BI, 8), (d_lam[0], LAM, 8), (d_pool_scale[0], PSC, 8),
    ]
    for src, r0, n in vec_rows:
        P.dma(lambda e, src=src, r0=r0, n=n: e.dma_start(out=vs.ap[r0:r0 + n, 0:128],
                                                         in_=src.rearrange("(a b) -> a b", b=128)),
              lane=vs.buf, writes=[vs.buf])
    P.dma(lambda e: e.dma_start(out=vs.ap[CW:CW + 32, 0:128],
                                in_=d_conv_w[0].rearrange("k (a b) -> (k a) b", b=128)),
          lane=vs.buf, writes=[vs.buf])
    P.dma(lambda e: e.dma_start(out=vs.ap[BP:BP + 8, 0:128],
                                in_=d_b_pool[0].rearrange("g (a b) -> (g a) b", b=128)),
          lane=vs.buf, writes=[vs.buf])
    bk = newbank()
    P.op("pe", lambda e: e.transpose(out=bk.ap[:, 0:96], in_=vs.ap[0:96, 0:128], identity=IDF.ap[0:96, 0:96]),
         reads=[vs.buf, IDF.buf], writes=[bk.buf])
    P.op("dve", lambda e: e.tensor_copy(out=CV.ap, in_=bk.ap[:, 0:96]), reads=[bk.buf], writes=[CV.buf])
    relbank(bk)
    P.op("dve", lambda e: e.tensor_scalar(out=CV2.ap[:, HBA:HBA + 16], in0=CV.ap[:, BA:BA + 16], scalar1=0.5,
                                          scalar2=None, op0=ALU.mult), reads=[CV.buf], writes=[CV2.buf])
    P.op("dve", lambda e: e.tensor_tensor(out=CV2.ap[:, BPS:BPS + 8], in0=CV.ap[:, BP:BP + 8],
                                          in1=CV.ap[:, PSC:PSC + 8], op=ALU.mult), reads=[CV.buf], writes=[CV2.buf])
    P.op("act", lambda e: e.activation(out=CV2.ap[:, CL:CL + 8], in_=CV.ap[:, LAM:LAM + 8], func=AF.Exp, scale=-1.0),
         reads=[CV.buf], writes=[CV2.buf])
    P.op("act", lambda e: e.activation(out=CV2.ap[:, CL:CL + 8], in_=CV2.ap[:, CL:CL + 8], func=AF.Ln, bias=1.0),
         reads=[CV2.buf], writes=[CV2.buf])
    P.op("dve", lambda e: e.tensor_scalar(out=CV2.ap[:, HCL:HCL + 8], in0=CV2.ap[:, CL:CL + 8], scalar1=-4.0,
                                          scalar2=None, op0=ALU.mult), reads=[CV2.buf], writes=[CV2.buf])
    P.op("dve", lambda e: e.tensor_scalar(out=CV2.ap[:, CL:CL + 8], in0=CV2.ap[:, CL:CL + 8], scalar1=-8.0,
                                          scalar2=None, op0=ALU.mult), reads=[CV2.buf], writes=[CV2.buf])
    P.op("pool", lambda e: e.iota(INVC.ap, pattern=[[0, 4], [1, 16]], base=1, channel_multiplier=0,
                                  allow_small_or_imprecise_dtypes=True), writes=[INVC.buf])
    for g in range(4):
        P.op("dve", lambda e, g=g: e.tensor_scalar(out=invc_v[:, g, :], in0=invc_v[:, g, :], scalar1=float(2 ** (g + 1)),
                                                   scalar2=None, op0=ALU.min), reads=[INVC.buf], writes=[INVC.buf])
    P.op("dve", lambda e: e.reciprocal(out=INVC.ap, in_=INVC.ap), reads=[INVC.buf], writes=[INVC.buf])

    P.op("dve", lambda e: e.tensor_scalar(out=CV.ap[:, G1:G1 + 8], in0=CV.ap[:, G1:G1 + 8], scalar1=32.0, scalar2=None,
                                          op0=ALU.mult), reads=[CV.buf], writes=[CV.buf])
    P.dma(lambda e: e.dma_start(out=G3.ap, in_=d_final_norm.partition_broadcast(128)), lane=G3.buf, writes=[G3.buf])
    P.op("dve", lambda e: e.tensor_scalar(out=G3.ap, in0=G3.ap, scalar1=32.0, scalar2=None, op0=ALU.mult),
         reads=[G3.buf], writes=[G3.buf])

    stg = [XS[1], XS[2], XR[0], XR[1], XR[2], XS[0]]
    sc = {"i": 0}

    def nextstg():
        s = stg[sc["i"] % len(stg)]
        sc["i"] += 1
        return s

    s1 = nextstg()
    P.dma(lambda e: e.dma_start(out=s1.ap[0:12, :], in_=d_sconv), lane=s1.buf, writes=[s1.buf])
    P.dma(lambda e: e.dma_start(out=s1.ap[12:16, :], in_=d_slru), lane=s1.buf, writes=[s1.buf])
    bk1 = newbank()
    for c in range(8):
        P.op("pe", lambda e, c=c: e.transpose(out=bk1.ap[:, c * 16:(c + 1) * 16], in_=s1.ap[0:16, c * 128:(c + 1) * 128],
                                              identity=IDF.ap[0:16, 0:16]), reads=[s1.buf, IDF.buf], writes=[bk1.buf])
    P.op("dve", lambda e: e.tensor_copy(out=SC16.ap, in_=bk1.ap[:, 0:128]), reads=[bk1.buf], writes=[SC16.buf])
    relbank(bk1)
    s2 = nextstg()
    P.dma(lambda e: e.dma_start(out=s2.ap[0:60, 0:512], in_=d_spool), lane=s2.buf, writes=[s2.buf])
    bk2 = newbank()
    for g in range(4):
        P.op("pe", lambda e, g=g: e.transpose(out=bk2.ap[:, g * 60:(g + 1) * 60], in_=s2.ap[0:60, g * 128:(g + 1) * 128],
                                              identity=IDF.ap[0:60, 0:60]), reads=[s2.buf, IDF.buf], writes=[bk2.buf])
    P.op("dve", lambda e: e.tensor_copy(out=SP60.ap, in_=bk2.ap[:, 0:240]), reads=[bk2.buf], writes=[SP60.buf])
    relbank(bk2)

    for dsrc, dst in ((d_w_rg_a, WA), (d_w_rg_i, WI)):
        s = nextstg()
        sv = s.ap.rearrange("p (c n) -> p c n", c=8)
        q = dsrc[0].rearrange("(c two) i j -> two i c j", two=2)
        P.op("pool", lambda e, s=s: e.memset(s.ap, 0.0), writes=[s.buf])
        P.dma(lambda e, sv=sv, q=q: e.dma_start(out=sv[0:64, :, 0:64], in_=q[0]), lane=s.buf, writes=[s.buf])
        P.dma(lambda e, sv=sv, q=q: e.dma_start(out=sv[64:128, :, 64:128], in_=q[1]), lane=s.buf, writes=[s.buf])
        P.op("dve", lambda e, s=s, dst=dst: e.tensor_copy(out=dst.ap, in_=s.ap), reads=[s.buf], writes=[dst.buf])
    sa_ = nextstg()
    sb_ = nextstg()
    P.dma(lambda e: e.dma_start(out=sa_.ap.rearrange("p (g j) -> p g j", g=4),
                                in_=d_w_pool[0].rearrange("g p j -> p g j")), lane=sa_.buf, writes=[sa_.buf])
    P.dma(lambda e: e.dma_start(out=sb_.ap, in_=d_pool_scale[0].partition_broadcast(128)), lane=sb_.buf, writes=[sb_.buf])
    P.op("dve", lambda e: e.tensor_tensor(out=WPOOL.ap, in0=sa_.ap, in1=sb_.ap, op=ALU.mult),
         reads=[sa_.buf, sb_.buf], writes=[WPOOL.buf])
    deferred_w = []
    xr_i = {"i": 0}

    def w_unit(dsrc, kc, dst_v, dstbuf, scal, post=1.0):
        def emit():
            s_ = XR[xr_i["i"] % 3]
            xr_i["i"] += 1
            P.dma(lambda e: e.dma_start(out=s_.ap, in_=dsrc[0][kc * 128:(kc + 1) * 128, :]), lane=s_.buf,
                  writes=[s_.buf])
            rd = [s_.buf] + ([CV.buf] if not isinstance(scal, float) else [])
            P.op(anyeng(), lambda e: e.tensor_scalar(out=dst_v[:, kc, :], in0=s_.ap, scalar1=scal, scalar2=post,
                                                     op0=ALU.mult, op1=ALU.mult), reads=rd, writes=[dstbuf])
        return emit

    for kc in range(8):
        deferred_w.append(w_unit(d_w_out, kc, wout_v, WOUT.buf, 0.25))
    for kc in range(8):
        deferred_w.append(w_unit(d_w_pg, kc, wpg_v, WPG.buf, CV.ap[:, G2 + kc:G2 + kc + 1], 16.0))
    for kc in range(2):
        deferred_w.append(w_unit(d_w_ple, kc, wpe_v, WPE.buf, 0.5))

    g1b = CV.ap[:, G1:G1 + 8].unsqueeze(2).to_broadcast([128, 8, 128])
    STGC = [(T(XR[i].ap.rearrange("p (k n) -> p k n", k=8), XR[i].buf), []) for i in range(3)]
    STGC.append((T(TGC[0].ap.rearrange("p (k n) -> p k n", k=8), TGC[0].buf), []))
    mg1f = MG[1].ap.bitcast(F32)
    for h in range(2):
        STGC.append((T(mg1f[:, h * 1024:(h + 1) * 1024].rearrange("p (k n) -> p k n", k=8), P.buf("mg1st%d" % h)),
                     [MG[1].buf]))
    cu = {"loaded": 0, "done": 0}
    chunks_done = set()

    def chunk_load_next():
        i = cu["loaded"]
        m = W_ORDER[i]
        hb, own = STGC[i % len(STGC)]
        P.dma(lambda e: e.dma_start(out=hb.ap, in_=d_w_in[0][:, m * 128:(m + 1) * 128].rearrange(
            "(k p) n -> p k n", p=128)), lane=hb.buf, writes=[hb.buf])
        cu["loaded"] += 1

    def chunk_compute_next():
        i = cu["done"]
        m = W_ORDER[i]
        while cu["loaded"] <= i:
            chunk_load_next()
        hb, own = STGC[i % len(STGC)]
        slot = RING[i % RING_N]
        P.op("dve", lambda e: e.tensor_tensor(out=slot.ap.rearrange("p (k n) -> p k n", k=8), in0=hb.ap, in1=g1b,
                                              op=ALU.mult), reads=[hb.buf, CV.buf] + own, writes=[slot.buf])
        P.dma(lambda e: e.dma_start(out=d_wsc[m], in_=slot.ap), lane=slot.buf, reads=[slot.buf], writes=[wsc_buf[m]])
        cu["done"] += 1
        chunks_done.add(m)
        if cu["loaded"] < NM:
            chunk_load_next()

    def ensure_chunk(m):
        while m not in chunks_done:
            chunk_compute_next()

    tiles = []
    for seq in range(2):
        for st in range(4):
            tiles.append(dict(kind="p", seq=seq, st=st, nseg=1, L=512, N=512, nt=4, tok0=seq * 2048 + st * 512,
                              first=(st == 0), last=(st == 3), idx=len(tiles)))
    tiles.append(dict(kind="s", seq=0, st=0, nseg=4, L=32, N=128, nt=1, tok0=0, first=False, last=True,
                      idx=len(tiles)))
    NST = len(tiles)

    stream = {"q_loaded": 0, "q_used": 0}
    total_q = NM * NST

    def ring_load(q):
        m = W_ORDER[q % NM]
        if q < NM:
            assert cu["done"] == q
            chunk_compute_next()
            return
        slot = RING[q % RING_N]
        P.dma(lambda e, slot=slot, m=m: e.dma_start(out=slot.ap, in_=d_wsc[m]), lane=slot.buf,
              reads=[wsc_buf[m]], writes=[slot.buf])

    def win_get(m):
        q = stream["q_used"]
        assert W_ORDER[q % NM] == m, (q, m)
        while stream["q_loaded"] < min(total_q, q + RING_N):
            ring_load(stream["q_loaded"])
            stream["q_loaded"] += 1
        stream["q_used"] += 1
        return RING[q % RING_N]

    a_tiles = []
    for S in tiles:
        for j in range(S["nt"]):
            a_tiles.append((S, j))
    NAT = len(a_tiles)

    def x_src(S, j):
        if S["kind"] == "p":
            r0 = S["tok0"] + j * 128
            return d_xp[r0:r0 + 128, :]
        return d_xs[:, :]

    def p_src(S, j):
        if S["kind"] == "p":
            r0 = S["tok0"] + j * 128
            return d_pp[r0:r0 + 128, :]
        return d_ps[:, :]

    def y_dst(S, j):
        if S["kind"] == "p":
            r0 = S["tok0"] + j * 128
            return o_yp[r0:r0 + 128, :]
        return o_ys[:, :]

    aload = {"n": 0}

    def a_ensure(i):
        while aload["n"] <= min(i, NAT - 1):
            k = aload["n"]
            S, j = a_tiles[k]
            slot = XS[k % 3]
            P.dma(lambda e, slot=slot, src=x_src(S, j): e.dma_start(out=slot.ap, in_=src), lane=slot.buf,
                  writes=[slot.buf])
            aload["n"] += 1

    cload = {"n": 0}

    def c_ensure(i):
        while cload["n"] <= min(i, NAT - 1):
            k = cload["n"]
            S, j = a_tiles[k]
            slot = XR[k % 3]
            P.dma(lambda e, slot=slot, src=x_src(S, j): e.dma_start(out=slot.ap, in_=src), lane=slot.buf,
                  writes=[slot.buf])
            ps_ = P32[k % 2]
            P.dma(lambda e, ps_=ps_, src=p_src(S, j): e.dma_start(out=ps_.ap, in_=src), lane=ps_.buf,
                  writes=[ps_.buf])
            cload["n"] += 1

    def rstd_ops(st):
        P.op("dve", lambda e: e.tensor_scalar(out=st.ap[:, 2:3], in0=st.ap[:, 0:1], scalar1=float(D) * EPS,
                                              scalar2=None, op0=ALU.add), reads=[st.buf], writes=[st.buf])
        P.op("pool", lambda e: e.tensor_tensor(out=st.ap[:, 1:2], in0=st.ap[:, 2:3], in1=NEGH.ap, op=ALU.pow),
             reads=[st.buf, NEGH.buf], writes=[st.buf])

    a_stat = {}

    def phase_a1(k):
        S, j = a_tiles[k]
        a_ensure(k + 2)
        x = XS[k % 3]
        st = newstat()
        u = UB[k % 2]
        P.op("act", lambda e: e.activation(out=JUNK.ap, in_=x.ap, func=AF.Square, accum_out=st.ap[:, 0:1]),
             reads=[x.buf], writes=[st.buf, JUNK.buf])
        a_stat[k] = st

    def phase_a1r(k):
        rstd_ops(a_stat[k])

    def phase_a1b(k):
        x = XS[k % 3]
        u = UB[k % 2]
        st = a_stat.pop(k)
        P.op("act", lambda e: e.activation(out=u.ap, in_=x.ap, func=AF.Identity, scale=st.ap[:, 1:2]),
             reads=[x.buf, st.buf], writes=[u.buf])

    def phase_a2(k):
        S, j = a_tiles[k]
        u = UB[k % 2]
        utb = UT[S["idx"] % 2]
        N = S["N"]
        ut_v = utb.ap[:, 0:8 * N].rearrange("p (k n) -> p k n", k=8)
        tb = newtbank()
        tbv = tb.ap.bitcast(BF16)
        for c in range(8):
            P.op("pe", lambda e, c=c: e.transpose(out=tbv[:, c * 128:(c + 1) * 128], in_=u.ap[:, c * 128:(c + 1) * 128],
                                                  identity=IDB.ap), reads=[u.buf, IDB.buf], writes=[tb.buf])
        P.op("act", lambda e: e.activation(out=ut_v[:, :, j * 128:(j + 1) * 128],
                                           in_=tbv.rearrange("p (k n) -> p k n", k=8), func=AF.Copy),
             reads=[tb.buf], writes=[utb.buf])

    stepc = {"n": 0}

    def seg3(ap2, nseg):
        return ap2.rearrange("p (s l) -> p s l", s=nseg)

    def mm_group(bank, N, wslot, utb, ut_v):
        wv = wslot.ap.rearrange("p (k n) -> p k n", k=8)
        for kc in range(8):
            P.op("pe", lambda e, kc=kc: e.matmul(bank.ap[:, 0:N], lhsT=wv[:, kc, :], rhs=ut_v[:, kc, :],
                                                 start=(kc == 0), stop=(kc == 7)),
                 reads=[wslot.buf, utb.buf], writes=[bank.buf])

    carry = {}
    deferred_ops = []
    deferred_ops2 = []

    def flush_deferred():
        while deferred_ops:
            deferred_ops.pop(0)()
        while deferred_ops2:
            deferred_ops.append(deferred_ops2.pop(0))

    def views(S):
        N = S["N"]
        utb = UT[S["idx"] % 2]
        ut_v = utb.ap[:, 0:8 * N].rearrange("p (k n) -> p k n", k=8)
        mgb = MG[S["idx"] % 2]
        mg_v = mgb.ap[:, 0:8 * N].rearrange("p (k n) -> p k n", k=8)
        return utb, ut_v, mgb, mg_v

    def lru_p1(S, c):
        nseg, L, N = S["nseg"], S["L"], S["N"]
        par = c % 2
        utb, ut_v, mgb, mg_v = views(S)
        sample = S["kind"] == "s"
        b_xl, b_gl, b_ml = newbank(), newbank(), newbank()
        for bank, m in ((b_xl, c), (b_gl, 8 + c), (b_ml, 28 + c)):
            mm_group(bank, N, win_get(m), utb, ut_v)
        xlw = XLW[par]
        xlw_v = xlw.ap[:, 0:nseg * (3 + L)].rearrange("p (s w) -> p s w", s=nseg)
        if S["first"]:
            P.op("pool", lambda e: e.memset(xlw_v[:, :, 0:3], 0.0), writes=[XLWH[par]])
        elif sample:
            P.op("pool", lambda e: e.tensor_copy(out=xlw_v[:, :, 0:3],
                                                 in_=sc16_v[:, c, 0:12].rearrange("p (s r) -> p s r", r=3)),
                 reads=[SC16.buf], writes=[XLWH[par]])
        else:
            P.op("pool", lambda e: e.tensor_copy(out=xlw_v[:, :, 0:3], in_=CONVH[c].ap.unsqueeze(1)),
                 reads=[CONVH[c].buf], writes=[XLWH[par]])
        P.op("act", lambda e: e.activation(out=xlw_v[:, :, 3:3 + L], in_=seg3(b_xl.ap[:, 0:N], nseg), func=AF.Copy),
             reads=[b_xl.buf], writes=[xlw.buf])
        acc = newwork()
        acc2 = acc.ap[:, 0:N]
        acc_v = seg3(acc2, nseg)
        P.op("act", lambda e: e.activation(out=acc2, in_=b_xl.ap[:, 0:N], func=AF.Identity,
                                           scale=CV.ap[:, CW + 24 + c:CW + 24 + c + 1], bias=CV.ap[:, CB + c:CB + c + 1]),
             reads=[b_xl.buf, CV.buf], writes=[acc.buf])
        relbank(b_xl)
        tg, tm = newwork(), newwork()
        P.op("act", lambda e: e.activation(out=tg.ap[:, 0:N], in_=b_gl.ap[:, 0:N], func=AF.Tanh, scale=0.5),
             reads=[b_gl.buf], writes=[tg.buf])
        P.op("act", lambda e: e.activation(out=tm.ap[:, 0:N], in_=b_ml.ap[:, 0:N], func=AF.Tanh, scale=0.5),
             reads=[b_ml.buf], writes=[tm.buf])
        relbank(b_ml)
        if not S["last"]:
            P.op("pool", lambda e: e.tensor_copy(out=CONVH[c].ap.unsqueeze(1), in_=xlw_v[:, :, L:L + 3]),
                 reads=[xlw.buf], writes=[CONVH[c].buf])
        else:
            P.op("pool", lambda e: e.tensor_copy(
                out=st_v[:, 0:nseg, 0:24].rearrange("p s (r c) -> p s r c", c=8)[:, :, :, c],
                in_=xlw_v[:, :, L:L + 3]), reads=[xlw.buf], writes=[ST.buf])
        P.op("dve", lambda e: e.scalar_tensor_tensor(out=tg.ap[:, 0:N], in0=tg.ap[:, 0:N], scalar=1.0,
                                                     in1=b_gl.ap[:, 0:N], op0=ALU.add, op1=ALU.mult),
             reads=[tg.buf, b_gl.buf], writes=[tg.buf])
        relbank(b_gl)
        xcb = XCB[par]

        def later():
            for j in (1, 2, 3):
                k = 3 - j
                P.op("dve", lambda e, j=j, k=k: e.scalar_tensor_tensor(
                    out=acc_v, in0=xlw_v[:, :, 3 - j:3 - j + L], scalar=CV.ap[:, CW + k * 8 + c:CW + k * 8 + c + 1],
                    in1=acc_v, op0=ALU.mult, op1=ALU.add), reads=[xlw.buf, XLWH[par], acc.buf, CV.buf],
                    writes=[acc.buf])
            P.op("dve", lambda e: e.tensor_copy(out=xcb.ap[:, 0:N], in_=acc2), reads=[acc.buf], writes=[xcb.buf])
            P.op("dve", lambda e: e.scalar_tensor_tensor(out=tm.ap[:, 0:N], in0=tm.ap[:, 0:N], scalar=1.0,
                                                         in1=tg.ap[:, 0:N], op0=ALU.add, op1=ALU.mult),
                 reads=[tm.buf, tg.buf], writes=[tm.buf])
            relwork(tg)
        deferred_ops.append(later)
        carry[("L", c)] = (acc, tm, xcb)

    def lru_p2(S, c):
        nseg, L, N = S["nseg"], S["L"], S["N"]
        utb, ut_v, mgb, mg_v = views(S)
        sample = S["kind"] == "s"
        acc, tm, xcb = carry.pop(("L", c))
        acc2 = acc.ap[:, 0:N]
        b_r, b_i = newbank(), newbank()
        P.op("pe", lambda e: e.matmul(b_r.ap[:, 0:N], lhsT=wa_v[:, c, :], rhs=xcb.ap[:, 0:N], start=True, stop=True),
             reads=[xcb.buf, WA.buf], writes=[b_r.buf])
        P.op("pe", lambda e: e.matmul(b_i.ap[:, 0:N], lhsT=wi_v[:, c, :], rhs=xcb.ap[:, 0:N], start=True, stop=True),
             reads=[xcb.buf, WI.buf], writes=[b_i.buf])
        tr, ti, aa, a2 = (newwork() for _ in range(4))
        P.op("act", lambda e: e.activation(out=tr.ap[:, 0:N], in_=b_r.ap[:, 0:N], func=AF.Tanh, scale=0.5,
                                           bias=CV2.ap[:, HBA + c:HBA + c + 1]), reads=[b_r.buf, CV2.buf], writes=[tr.buf])
        P.op("act", lambda e: e.activation(out=ti.ap[:, 0:N], in_=b_i.ap[:, 0:N], func=AF.Tanh, scale=0.5,
                                           bias=CV2.ap[:, HBI + c:HBI + c + 1]), reads=[b_i.buf, CV2.buf], writes=[ti.buf])
        relbank(b_r)
        relbank(b_i)
        P.op("act", lambda e: e.activation(out=aa.ap[:, 0:N], in_=tr.ap[:, 0:N], func=AF.Exp,
                                           scale=CV2.ap[:, HCL + c:HCL + c + 1], bias=CV2.ap[:, HCL + c:HCL + c + 1]),
             reads=[tr.buf, CV2.buf], writes=[aa.buf])
        P.op("act", lambda e: e.activation(out=a2.ap[:, 0:N], in_=tr.ap[:, 0:N], func=AF.Exp,
                                           scale=CV2.ap[:, CL + c:CL + c + 1], bias=CV2.ap[:, CL + c:CL + c + 1]),
             reads=[tr.buf, CV2.buf], writes=[a2.buf])
        relwork(tr)

        def later():
            P.op("dve", lambda e: e.tensor_scalar(out=a2.ap[:, 0:N], in0=a2.ap[:, 0:N], scalar1=1.0 - 2.0 ** -24,
                                                  scalar2=None, op0=ALU.min), reads=[a2.buf], writes=[a2.buf])
            P.op("dve", lambda e: e.scalar_tensor_tensor(out=ti.ap[:, 0:N], in0=ti.ap[:, 0:N], scalar=1.0, in1=acc2,
                                                         op0=ALU.add, op1=ALU.mult), reads=[ti.buf, acc.buf],
                 writes=[ti.buf])
            relwork(acc)
        deferred_ops.append(later)
        carry[("M", c)] = (ti, aa, a2, tm)

    def lru_p2b(S, c):
        nseg, L, N = S["nseg"], S["L"], S["N"]
        utb, ut_v, mgb, mg_v = views(S)
        sample = S["kind"] == "s"
        ti, aa, a2, tm = carry.pop(("M", c))
        hh = newwork()
        P.op("act", lambda e: e.activation(out=a2.ap[:, 0:N], in_=a2.ap[:, 0:N], func=AF.Sqrt, scale=-0.25, bias=0.25),
             reads=[a2.buf], writes=[a2.buf])
        deferred_ops.append(lambda: lru_p2c(S, c, ti, aa, a2, tm, hh))

    def lru_p2c(S, c, ti, aa, a2, tm, hh):
        nseg, L, N = S["nseg"], S["L"], S["N"]
        utb, ut_v, mgb, mg_v = views(S)
        sample = S["kind"] == "s"
        if S["first"]:
            P.op("dve", lambda e: e.memset(a2.ap[:, 0:1], 0.5), reads=[a2.buf], writes=[a2.buf])
        P.op("dve", lambda e: e.tensor_tensor(out=ti.ap[:, 0:N], in0=ti.ap[:, 0:N], in1=a2.ap[:, 0:N], op=ALU.mult),
             reads=[ti.buf, a2.buf], writes=[ti.buf])
        relwork(a2)
        for sg in range(nseg):
            if S["first"]:
                init, rb = 0.0, []
            elif sample:
                init, rb = sc16_v[:, c, 12 + sg:13 + sg], [SC16.buf]
            else:
                init, rb = HST[c].ap, [HST[c].buf]
            P.op("dve", lambda e, sg=sg, init=init: e.tensor_tensor_scan(
                out=hh.ap[:, sg * L:(sg + 1) * L], data0=aa.ap[:, sg * L:(sg + 1) * L],
                data1=ti.ap[:, sg * L:(sg + 1) * L], initial=init, op0=ALU.mult, op1=ALU.add),
                reads=[aa.buf, ti.buf] + rb, writes=[hh.buf])
        relwork(ti, aa)
        hh_v = seg3(hh.ap[:, 0:N], nseg)
        if not S["last"]:
            P.op("pool", lambda e: e.tensor_copy(out=HST[c].ap, in_=hh.ap[:, N - 1:N]), reads=[hh.buf],
                 writes=[HST[c].buf])
        else:
            P.op("pool", lambda e: e.tensor_copy(out=st_v[:, 0:nseg, 84 + c:85 + c], in_=hh_v[:, :, L - 1:L]),
                 reads=[hh.buf], writes=[ST.buf])
        P.op("pool", lambda e: e.tensor_tensor(out=mg_v[:, c, :], in0=hh.ap[:, 0:N], in1=tm.ap[:, 0:N], op=ALU.mult),
             reads=[hh.buf, tm.buf], writes=[mgb.buf])
        relwork(hh, tm)

    def pool_p1(S, g):
        nseg, L, N = S["nseg"], S["L"], S["N"]
        par = g % 2
        W = 15 + L
        utb, ut_v, mgb, mg_v = views(S)
        sample = S["kind"] == "s"
        wnd = 2 ** (g + 1)
        b_xp = newbank()
        mm_group(b_xp, N, win_get(16 + g), utb, ut_v)
        xpw = XPW[par]
        xpw_v = xpw.ap[:, 0:nseg * W].rearrange("p (s w) -> p s w", s=nseg)
        if S["first"]:
            P.op("pool", lambda e: e.memset(xpw_v[:, :, 0:15], 0.0), writes=[XPWH[par]])
        elif sample:
            P.op("pool", lambda e: e.tensor_copy(out=xpw_v[:, :, 0:15],
                                                 in_=sp60_v[:, g, :].rearrange("p (s r) -> p s r", r=15)),
                 reads=[SP60.buf], writes=[XPWH[par]])
        else:
            P.op("pool", lambda e: e.tensor_copy(out=xpw_v[:, :, 0:15], in_=POOLH[g].ap.unsqueeze(1)),
                 reads=[POOLH[g].buf], writes=[XPWH[par]])
        P.op("act", lambda e: e.activation(out=xpw_v[:, :, 15:W], in_=seg3(b_xp.ap[:, 0:N], nseg), func=AF.Copy),
             reads=[b_xp.buf], writes=[xpw.buf])
        relbank(b_xp)
        src, src_v = xpw, xpw_v
        for i in range(g + 1):
            sh = 2 ** i
            lo = 2 ** (i + 1) - 1
            dst = (SA, SBB)[i % 2]
            dst_v = dst.ap[:, 0:nseg * W].rearrange("p (s w) -> p s w", s=nseg)
            P.op("pool", lambda e, dst_v=dst_v, src_v=src_v, lo=lo, sh=sh: e.tensor_tensor(
                out=dst_v[:, :, lo:W], in0=src_v[:, :, lo:W], in1=src_v[:, :, lo - sh:W - sh], op=ALU.add),
                reads=[src.buf, XPWH[par]], writes=[dst.buf])
            src, src_v = dst, dst_v
        if not S["last"]:
            P.op("pool", lambda e: e.tensor_copy(out=POOLH[g].ap.unsqueeze(1), in_=xpw_v[:, :, L:L + 15]),
                 reads=[xpw.buf], writes=[POOLH[g].buf])
        else:
            P.op("pool", lambda e: e.tensor_copy(
                out=st_v[:, 0:nseg, 24:84].rearrange("p s (r g) -> p s r g", g=4)[:, :, :, g],
                in_=xpw_v[:, :, L:L + 15]), reads=[xpw.buf], writes=[ST.buf])
        pl = PL[par]
        pl_v = seg3(pl.ap[:, 0:N], nseg)
        wsum_v = src_v[:, :, 15:W]

        def later():
            P.op("dve", lambda e: e.scalar_tensor_tensor(out=pl_v, in0=wsum_v, scalar=1.0 / wnd, in1=xpw_v[:, :, 15:W],
                                                         op0=ALU.mult, op1=ALU.subtract),
                 reads=[src.buf, xpw.buf], writes=[pl.buf])
            if S["first"]:
                P.op("dve", lambda e: e.tensor_tensor(out=T16.ap, in0=src_v[:, 0, 15:31], in1=invc_v[:, g, :],
                                                      op=ALU.mult), reads=[src.buf, INVC.buf], writes=[T16.buf])
                P.op("dve", lambda e: e.tensor_tensor(out=pl.ap[:, 0:16], in0=T16.ap, in1=xpw_v[:, 0, 15:31],
                                                      op=ALU.subtract), reads=[T16.buf, xpw.buf, pl.buf], writes=[pl.buf])
        (deferred_ops2 if g < 3 else deferred_ops).append(later)

    def pool_p2(S, g):
        nseg, L, N = S["nseg"], S["L"], S["N"]
        utb, ut_v, mgb, mg_v = views(S)
        pl = PL[g % 2]
        for hf in range(2):
            cc = 2 * g + hf
            b_gp, b_mp = newbank(), newbank()
            mm_group(b_gp, N, win_get(20 + cc), utb, ut_v)
            mm_group(b_mp, N, win_get(36 + cc), utb, ut_v)
            b_pg = newbank()
            P.op("pe", lambda e, hf=hf, b_pg=b_pg: e.matmul(b_pg.ap[:, 0:N], lhsT=wpool_v[:, g, hf * 128:(hf + 1) * 128],
                                                            rhs=pl.ap[:, 0:N], start=True, stop=True),
                 reads=[pl.buf, WPOOL.buf], writes=[b_pg.buf])
            tg, tm = newwork(), newwork()
            P.op("act", lambda e, tg=tg, b_gp=b_gp: e.activation(out=tg.ap[:, 0:N], in_=b_gp.ap[:, 0:N], func=AF.Tanh,
                                                                 scale=0.5), reads=[b_gp.buf], writes=[tg.buf])
            P.op("act", lambda e, tm=tm, b_mp=b_mp: e.activation(out=tm.ap[:, 0:N], in_=b_mp.ap[:, 0:N], func=AF.Tanh,
                                                                 scale=0.5), reads=[b_mp.buf], writes=[tm.buf])
            relbank(b_mp)
            P.op("dve", lambda e, tg=tg, b_gp=b_gp: e.scalar_tensor_tensor(
                out=tg.ap[:, 0:N], in0=tg.ap[:, 0:N], scalar=1.0, in1=b_gp.ap[:, 0:N], op0=ALU.add, op1=ALU.mult),
                reads=[tg.buf, b_gp.buf], writes=[tg.buf])
            relbank(b_gp)
            P.op("dve", lambda e, tg=tg, tm=tm: e.scalar_tensor_tensor(
                out=tm.ap[:, 0:N], in0=tm.ap[:, 0:N], scalar=1.0, in1=tg.ap[:, 0:N], op0=ALU.add, op1=ALU.mult),
                reads=[tm.buf, tg.buf], writes=[tm.buf])
            P.op("dve", lambda e, tg=tg, tm=tm, b_pg=b_pg, cc=cc: e.scalar_tensor_tensor(
                out=tg.ap[:, 0:N], in0=b_pg.ap[:, 0:N], scalar=CV2.ap[:, BPS + cc:BPS + cc + 1],
                in1=tm.ap[:, 0:N], op0=ALU.add, op1=ALU.mult), reads=[b_pg.buf, tm.buf, CV2.buf], writes=[tg.buf])
            relbank(b_pg)
            P.op("pool", lambda e, tg=tg, cc=cc: e.tensor_tensor(out=mg_v[:, cc, :], in0=mg_v[:, cc, :],
                                                                 in1=tg.ap[:, 0:N], op=ALU.add),
                 reads=[mgb.buf, tg.buf], writes=[mgb.buf])
            relwork(tg, tm)

    def state_out(S):
        nseg = S["nseg"]
        bk_ = newbank()
        for sg in range(nseg):
            P.op("pe", lambda e, sg=sg: e.transpose(out=bk_.ap[0:92, sg * 128:(sg + 1) * 128], in_=st_v[:, sg, :],
                                                    identity=IDF.ap), reads=[ST.buf, IDF.buf], writes=[bk_.buf])
        P.op("act", lambda e: e.activation(out=STO.ap[0:92, 0:nseg * 128], in_=bk_.ap[0:92, 0:nseg * 128], func=AF.Copy),
             reads=[bk_.buf], writes=[STO.buf])
        relbank(bk_)
        for sg in range(nseg):
            if S["kind"] == "p":
                oc, ol, op_ = o_ncp[S["seq"]], o_nlp[S["seq"]], o_npp[S["seq"]]
            else:
                oc, ol, op_ = o_ncs[sg], o_nls[sg], o_nps[sg]
            P.dma(lambda e, sg=sg, oc=oc: e.dma_start(out=oc.rearrange("r (c p) -> (r c) p", p=128),
                                                      in_=STO.ap[0:24, sg * 128:(sg + 1) * 128]),
                  lane=STO.buf, reads=[STO.buf], queue="act")
            P.dma(lambda e, sg=sg, op_=op_: e.dma_start(out=op_.rearrange("r (g p) -> (r g) p", p=128),
                                                        in_=STO.ap[24:84, sg * 128:(sg + 1) * 128]),
                  lane=STO.buf, reads=[STO.buf], queue="act")
            P.dma(lambda e, sg=sg, ol=ol: e.dma_start(out=ol.rearrange("(c p) -> c p", p=128),
                                                      in_=STO.ap[84:92, sg * 128:(sg + 1) * 128]),
                  lane=STO.buf, reads=[STO.buf], queue="act")

    def b_task(S, t):
        kind, a = B_TASKS[t]
        flush_deferred()
        {"L1": lru_p1, "L2a": lru_p2, "L2b": lru_p2b, "Q1": pool_p1, "Q2": pool_p2}[kind](S, a)
        if t == len(B_TASKS) - 1:
            flush_deferred()
            flush_deferred()
            if S["last"]:
                state_out(S)

    pending_st = []

    def flush_stores():
        while pending_st:
            pending_st.pop(0)()

    NCS = 8
    cst_state = {}

    def c_stage(k, stage):
        S, j = a_tiles[k]
        N = S["N"]
        mgb = MG[S["idx"] % 2]
        mg_v = mgb.ap[:, 0:8 * N].rearrange("p (k n) -> p k n", k=8)
        xr = XR[k % 3]
        u2 = U2[k % 2]
        u2t = U2T[k % 2]
        p32, pb, pt = P32[k % 2], PB[k % 2], PT[k % 2]
        tg = TGC[0]
        if stage == 0:
            c_ensure(k + 1)
            P.op("pool", lambda e: e.tensor_copy(out=pb.ap, in_=p32.ap), reads=[p32.buf], writes=[pb.buf])
            d = [newbank(), newbank()]
            for hf in range(2):
                for kc in range(8):
                    P.op("pe", lambda e, hf=hf, kc=kc: e.matmul(
                        d[hf].ap, lhsT=mg_v[:, kc, j * 128:(j + 1) * 128], rhs=wout_v[:, kc, hf * 512:(hf + 1) * 512],
                        start=(kc == 0), stop=(kc == 7)), reads=[mgb.buf, WOUT.buf], writes=[d[hf].buf])
            for hf in range(2):
                P.op("dve", lambda e, hf=hf: e.tensor_tensor(out=xr.ap[:, hf * 512:(hf + 1) * 512],
                                                             in0=xr.ap[:, hf * 512:(hf + 1) * 512], in1=d[hf].ap,
                                                             op=ALU.add), reads=[xr.buf, d[hf].buf], writes=[xr.buf])
            relbank(d[0])
            relbank(d[1])
            P.op("dve", lambda e: e.tensor_copy(out=u2.ap, in_=xr.ap), reads=[xr.buf], writes=[u2.buf])
        elif stage == 1:
            st = newstat()
            P.op("act", lambda e: e.activation(out=JUNK.ap, in_=xr.ap, func=AF.Square, accum_out=st.ap[:, 0:1]),
                 reads=[xr.buf], writes=[st.buf, JUNK.buf])
            cst_state[k] = st
        elif stage == 2:
            rstd_ops(cst_state[k])
            tb = newtbank()
            tbv = tb.ap.bitcast(BF16)
            for c in range(8):
                P.op("pe", lambda e, c=c: e.transpose(out=tbv[:, c * 128:(c + 1) * 128],
                                                      in_=u2.ap[:, c * 128:(c + 1) * 128], identity=IDB.ap),
                     reads=[u2.buf, IDB.buf], writes=[tb.buf])
            P.op("act", lambda e: e.activation(out=u2t.ap, in_=tbv, func=AF.Copy), reads=[tb.buf], writes=[u2t.buf])
            tb2 = newtbank()
            tb2v = tb2.ap.bitcast(BF16)
            for c in range(2):
                P.op("pe", lambda e, c=c: e.transpose(out=tb2v[:, c * 128:(c + 1) * 128],
                                                      in_=pb.ap[:, c * 128:(c + 1) * 128], identity=IDB.ap),
                     reads=[pb.buf, IDB.buf], writes=[tb2.buf])
            P.op("act", lambda e: e.activation(out=pt.ap, in_=tb2v[:, 0:256], func=AF.Copy), reads=[tb2.buf],
                 writes=[pt.buf])
        elif stage == 3:
            u2t_v = u2t.ap.rearrange("p (k n) -> p k n", k=8)
            eb = [newbank(), newbank()]
            for hf in range(2):
                for kc in range(8):
                    P.op("pe", lambda e, hf=hf, kc=kc: e.matmul(
                        eb[hf].ap, lhsT=u2t_v[:, kc, :], rhs=wpg_v[:, kc, hf * 512:(hf + 1) * 512],
                        start=(kc == 0), stop=(kc == 7)), reads=[u2t.buf, WPG.buf], writes=[eb[hf].buf])
            st = cst_state.pop(k)
            for hf in range(2):
                P.op("act", lambda e, hf=hf: e.activation(out=tg.ap[:, hf * 512:(hf + 1) * 512], in_=eb[hf].ap,
                                                          func=AF.Tanh, scale=st.ap[:, 1:2]),
                     reads=[eb[hf].buf, st.buf], writes=[tg.buf])
            relbank(eb[0])
            relbank(eb[1])
        elif stage == 4:
            pt_v = pt.ap.rearrange("p (k n) -> p k n", k=2)
            fbk = [newbank(), newbank()]
            for hf in range(2):
                for kc in range(2):
                    P.op("pe", lambda e, hf=hf, kc=kc: e.matmul(
                        fbk[hf].ap, lhsT=pt_v[:, kc, :], rhs=wpe_v[:, kc, hf * 512:(hf + 1) * 512],
                        start=(kc == 0), stop=(kc == 1)), reads=[pt.buf, WPE.buf], writes=[fbk[hf].buf])
            for hf in range(2):
                P.op("dve", lambda e, hf=hf: e.scalar_tensor_tensor(
                    out=tg.ap[:, hf * 512:(hf + 1) * 512], in0=tg.ap[:, hf * 512:(hf + 1) * 512], scalar=1.0,
                    in1=fbk[hf].ap, op0=ALU.add, op1=ALU.mult), reads=[tg.buf, fbk[hf].buf], writes=[tg.buf])
            relbank(fbk[0])
            relbank(fbk[1])
            P.op("pool", lambda e: e.tensor_tensor(out=xr.ap, in0=xr.ap, in1=tg.ap, op=ALU.add),
                 reads=[xr.buf, tg.buf], writes=[xr.buf])
        elif stage == 5:
            st = newstat()
            P.op("act", lambda e: e.activation(out=JUNK.ap, in_=xr.ap, func=AF.Square, accum_out=st.ap[:, 0:1]),
                 reads=[xr.buf], writes=[st.buf, JUNK.buf])
            cst_state[("y", k)] = st
        elif stage == 6:
            rstd_ops(cst_state[("y", k)])
        elif stage == 7:
            st = cst_state.pop(("y", k))
            P.op("dve", lambda e: e.scalar_tensor_tensor(out=xr.ap, in0=xr.ap, scalar=st.ap[:, 1:2], in1=G3.ap,
                                                         op0=ALU.mult, op1=ALU.mult),
                 reads=[xr.buf, st.buf, G3.buf], writes=[xr.buf])
            pending_st.append(lambda dst=y_dst(S, j), xr=xr: P.dma(
                lambda e: e.dma_start(out=dst, in_=xr.ap), lane=xr.buf, reads=[xr.buf], queue="act"))

    first_tile = {}
    k0 = 0
    for S in tiles:
        first_tile[S["idx"]] = k0
        k0 += S["nt"]
    NT = len(B_TASKS)
    a_ensure(1)
    for j in range(tiles[0]["nt"]):
        phase_a1(first_tile[0] + j)
        phase_a1r(first_tile[0] + j)
        phase_a1b(first_tile[0] + j)
        if j >= 1:
            phase_a2(first_tile[0] + j - 1)
    phase_a2(first_tile[0] + tiles[0]["nt"] - 1)
    for _ in range(len(STGC)):
        chunk_load_next()
    for r in range(NST + 1):
        Sb = tiles[r] if r < NST else None
        Sc = tiles[r - 1] if r >= 1 else None
        Sa = tiles[r + 1] if r + 1 < NST else None
        cslot = {}
        if Sc is not None:
            ntc = Sc["nt"]
            for j in range(ntc):
                base = (j * NT) // ntc
                offs = (0, 1, 3, 5, 6, 8, 9, 11) if ntc > 1 else (0, 4, 8, 12, 16, 20, 24, 28)
                for stg_ in range(NCS):
                    cslot.setdefault(min(NT - 1, base + offs[stg_]), []).append((first_tile[Sc["idx"]] + j, stg_))
        ast = []
        if Sa is not None:
            ast = [first_tile[Sa["idx"]] + j for j in range(Sa["nt"])]
        for t in range(NT):
            if r == 0:
                if cu["done"] >= NM and deferred_w:
                    deferred_w.pop(0)()
                    if deferred_w:
                        deferred_w.pop(0)()
            if Sb is not None:
                b_task(Sb, t)
            flush_stores()
            for (kk, stg_) in cslot.get(t, []):
                c_stage(kk, stg_)
            nA = len(ast)
            for i_, kk in enumerate(ast):
                t2 = ((i_ + 1) * NT) // nA - 1
                if t == max(0, t2 - 7):
                    phase_a1(kk)
                if t == max(1, t2 - 5):
                    phase_a1r(kk)
                if t == max(2, t2 - 3):
                    phase_a1b(kk)
                if t == t2:
                    phase_a2(kk)
        if r == 0:
            assert cu["done"] == NM
            while deferred_w:
                deferred_w.pop(0)()
            c_ensure(0)

    flush_stores()
    P.finalize()
    es.close()
    return nc, P


_CACHE = {}


def _get_program():
    if "nc" not in _CACHE:
        nc, P = build_program()
        _CACHE["nc"] = nc
        _CACHE["stats"] = P.stats
    return _CACHE["nc"]


def kernel(x_prompt, x_sample, p_prompt, p_sample, state_conv, state_lru, state_pool,
           norm_mix, w_in, conv_w, conv_b, w_rg_a, b_rg_a, w_rg_i, b_rg_i, lru_lambda,
           w_pool, b_pool, pool_scale, w_out, norm_ple, w_ple_gate, w_ple, final_norm):
    f = lambda a: np.ascontiguousarray(np.asarray(a, dtype=np.float32))
    x_prompt, x_sample, p_prompt, p_sample = f(x_prompt), f(x_sample), f(p_prompt), f(p_sample)
    state_conv, state_lru, state_pool = f(state_conv), f(state_lru), f(state_pool)
    shared = {
        "norm_mix": f(norm_mix), "w_in": f(w_in), "conv_w": f(conv_w), "conv_b": f(conv_b),
        "w_rg_a": f(w_rg_a), "b_rg_a": f(b_rg_a), "w_rg_i": f(w_rg_i), "b_rg_i": f(b_rg_i),
        "lru_lambda": f(lru_lambda), "w_pool": f(w_pool), "b_pool": f(b_pool), "pool_scale": f(pool_scale),
        "w_out": f(w_out), "norm_ple": f(norm_ple), "w_ple_gate": f(w_ple_gate), "w_ple": f(w_ple),
        "final_norm": f(final_norm),
    }
    in_maps = []
    for i in range(NCORES):
        m = dict(shared)
        m["xp"] = np.ascontiguousarray(x_prompt[2 * i:2 * i + 2].reshape(4096, 1024))
        m["pp"] = np.ascontiguousarray(p_prompt[0, 2 * i:2 * i + 2].reshape(4096, 256))
        m["xs"] = np.ascontiguousarray(x_sample[4 * i:4 * i + 4].reshape(128, 1024))
        m["ps"] = np.ascontiguousarray(p_sample[0, 4 * i:4 * i + 4].reshape(128, 256))
        m["sconv"] = np.ascontiguousarray(state_conv[0, 4 * i:4 * i + 4].reshape(12, 1024))
        m["slru"] = np.ascontiguousarray(state_lru[0, 4 * i:4 * i + 4].reshape(4, 1024))
        m["spool"] = np.ascontiguousarray(state_pool[0, 4 * i:4 * i + 4].reshape(60, 512))
        in_maps.append(m)
    nc = _get_program()
    res = run_bass_kernel_spmd(nc, in_maps, core_ids=list(range(NCORES)))
    R = res.results
    y_prompt = np.concatenate([R[i]["yp"].reshape(2, 2048, 1024) for i in range(NCORES)], axis=0)
    y_sample = np.concatenate([R[i]["ys"].reshape(4, 32, 1024) for i in range(NCORES)], axis=0)
    ncp = np.concatenate([R[i]["ncp"] for i in range(NCORES)], axis=0)[None]
    nlp = np.concatenate([R[i]["nlp"] for i in range(NCORES)], axis=0)[None]
    npp = np.concatenate([R[i]["npp"] for i in range(NCORES)], axis=0)[None]
    ncs = np.concatenate([R[i]["ncs"] for i in range(NCORES)], axis=0)[None]
    nls = np.concatenate([R[i]["nls"] for i in range(NCORES)], axis=0)[None]
    nps = np.concatenate([R[i]["nps"] for i in range(NCORES)], axis=0)[None]
    return tuple(np.ascontiguousarray(a.astype(np.float32)) for a in (y_prompt, y_sample, ncp, nlp, npp, ncs, nls, nps))
```
